# Optimizing a Trainium2 kernel written in Bass

```python
import jax, jax.numpy as jnp
from jax import lax
import numpy as np

D_MODEL = 1024
BATCH = 16
SEQ = 2048
DEPTH = 2

PLE_DIM = 256
CHUNK = 128
SG_GROUPS = 4
SG_GROUP_DIM = 128
SG_WIDTH = SG_GROUPS * SG_GROUP_DIM
SB_HEADS = 8
SB_HEAD_DIM = 64
SB_WIDTH = SB_HEADS * SB_HEAD_DIM
AB_IN = 2 * SG_WIDTH + 3 * SB_WIDTH
AB_OUT = SG_WIDTH + SB_WIDTH
RET_HEADS = 4
RET_DK = D_MODEL // RET_HEADS
RET_DV = 2 * RET_DK
RET_QK = RET_HEADS * RET_DK
RET_V = RET_HEADS * RET_DV
RET_IN = 2 * RET_QK + 2 * RET_V
ROPE_BASE = 10000.0
D_FF = 2816
CONV_W = 3
RMS_EPS = 1e-6
LN_EPS = 1e-5
N_EVEN = (DEPTH + 1) // 2
N_ODD = DEPTH // 2

kernel_name = 'hybrid_sgmlp_stickbreak_retention_trunk'


def rms_norm(x, g):
    xf = x.astype(jnp.float32)
    y = xf * lax.rsqrt(jnp.mean(xf * xf, axis=-1, keepdims=True) + RMS_EPS)
    return (y * g.astype(jnp.float32)).astype(x.dtype)


def spatial_gating(u, v, ln_g, ln_b, w_s, b_s):
    Bn, S, _ = u.shape
    nc = S // CHUNK
    vf = v.astype(jnp.float32).reshape(Bn, nc, CHUNK, SG_GROUPS, SG_GROUP_DIM)
    mean = jnp.mean(vf, axis=-1, keepdims=True)
    var = jnp.mean(jnp.square(vf - mean), axis=-1, keepdims=True)
    g = ln_g.astype(jnp.float32).reshape(SG_GROUPS, SG_GROUP_DIM)
    b = ln_b.astype(jnp.float32).reshape(SG_GROUPS, SG_GROUP_DIM)
    vn = (vf - mean) * lax.rsqrt(var + LN_EPS) * g + b
    causal = jnp.tril(jnp.ones((CHUNK, CHUNK), dtype=bool))
    w = jnp.where(causal[None], w_s.astype(jnp.float32), 0.0)
    mixed = jnp.einsum('gts,bnsgc->bntgc', w, vn) + jnp.transpose(b_s.astype(jnp.float32))[None, None, :, :, None]
    out = u.reshape(Bn, nc, CHUNK, SG_GROUPS, SG_GROUP_DIM) * mixed.astype(u.dtype)
    return out.reshape(Bn, S, SG_WIDTH)


def stick_breaking_attention(q, k, v):
    Bn, S, H, Dh = q.shape
    qf = q.astype(jnp.float32) * (Dh ** -0.5)
    kf = k.astype(jnp.float32)
    vf = v.astype(jnp.float32)
    outs = []
    for i in range(S // CHUNK):
        q0 = i * CHUNK
        kv_len = q0 + CHUNK
        z = jnp.einsum('bthd,bshd->bhts', qf[:, q0:kv_len], kf[:, :kv_len])
        t_pos = q0 + jnp.arange(CHUNK)[:, None]
        s_pos = jnp.arange(kv_len)[None, :]
        causal = s_pos < t_pos
        neg_log_1m_beta = jnp.where(causal, jax.nn.softplus(z), 0.0)
        later = lax.cumsum(neg_log_1m_beta, axis=3, reverse=True) - neg_log_1m_beta
        log_a = jax.nn.log_sigmoid(z) - later
        a = jnp.where(causal, jnp.exp(log_a), 0.0)
        outs.append(jnp.einsum('bhts,bshd->bthd', a, vf[:, :kv_len]))
    return jnp.concatenate(outs, axis=1).astype(q.dtype)


def hybrid_sg_sb_mixer(h, w_in, ln_g, ln_b, w_s, b_s, w_out):
    Bn, S, _ = h.shape
    proj = h @ w_in
    u, v, q, k, vv = jnp.split(proj, [SG_WIDTH, 2 * SG_WIDTH, 2 * SG_WIDTH + SB_WIDTH, 2 * SG_WIDTH + 2 * SB_WIDTH], axis=-1)
    a_out = spatial_gating(jax.nn.gelu(u, approximate=False), jax.nn.gelu(v, approximate=False), ln_g, ln_b, w_s, b_s)
    q = q.reshape(Bn, S, SB_HEADS, SB_HEAD_DIM)
    k = k.reshape(Bn, S, SB_HEADS, SB_HEAD_DIM)
    vv = vv.reshape(Bn, S, SB_HEADS, SB_HEAD_DIM)
    b_out = stick_breaking_attention(q, k, vv).reshape(Bn, S, SB_WIDTH)
    return jnp.concatenate([a_out, b_out], axis=-1) @ w_out


def rotary(x):
    S = x.shape[1]
    half = x.shape[-1] // 2
    inv = 1.0 / (ROPE_BASE ** (jnp.arange(half, dtype=jnp.float32) / half))
    ang = jnp.arange(S, dtype=jnp.float32)[:, None] * inv[None, :]
    cos = jnp.cos(ang)[None, :, None, :]
    sin = jnp.sin(ang)[None, :, None, :]
    xf = x.astype(jnp.float32)
    x1, x2 = xf[..., :half], xf[..., half:]
    return jnp.concatenate([x1 * cos - x2 * sin, x2 * cos + x1 * sin], axis=-1)


def retention(q, k, v):
    Bn, S, H, Dk = q.shape
    Dv = v.shape[-1]
    nc = S // CHUNK
    log_gamma = jnp.log(1.0 - 2.0 ** (-5.0 - jnp.arange(H, dtype=jnp.float32)))
    idx = jnp.arange(CHUNK, dtype=jnp.float32)
    diff = idx[:, None] - idx[None, :]
    decay_intra = jnp.where(diff[None] >= 0, jnp.exp(diff[None] * log_gamma[:, None, None]), 0.0)
    zeta = jnp.exp((CHUNK - 1 - idx)[None, :] * log_gamma[:, None])
    xi = jnp.exp((idx + 1.0)[None, :] * log_gamma[:, None])
    chunk_decay = jnp.exp(CHUNK * log_gamma)

    def to_chunks(t):
        return t.astype(jnp.float32).reshape(Bn, nc, CHUNK, H, t.shape[-1]).transpose(1, 0, 3, 2, 4)

    qc, kc, vc = to_chunks(q), to_chunks(k), to_chunks(v)

    def step(state, inp):
        qi, ki, vi = inp
        inner = jnp.einsum('bhtd,bhsd->bhts', qi, ki) * decay_intra[None]
        o = jnp.einsum('bhts,bhsv->bhtv', inner, vi) + jnp.einsum('bhtd,bhdv->bhtv', qi, state) * xi[None, :, :, None]
        state = state * chunk_decay[None, :, None, None] + jnp.einsum('bhsd,bhsv->bhdv', ki * zeta[None, :, :, None], vi)
        return state, o

    state0 = jnp.zeros((Bn, H, Dk, Dv), jnp.float32)
    _, o = lax.scan(step, state0, (qc, kc, vc))
    return o.transpose(1, 0, 3, 2, 4).reshape(Bn, S, H, Dv)


def retention_mixer(h, w_in, gn_g, w_out):
    Bn, S, _ = h.shape
    q, k, v, g = jnp.split(h @ w_in, [RET_QK, 2 * RET_QK, 2 * RET_QK + RET_V], axis=-1)
    q = rotary(q.reshape(Bn, S, RET_HEADS, RET_DK))
    k = rotary(k.reshape(Bn, S, RET_HEADS, RET_DK)) * (RET_DK ** -0.5)
    o = retention(q, k, v.reshape(Bn, S, RET_HEADS, RET_DV))
    mean = jnp.mean(o, axis=-1, keepdims=True)
    var = jnp.mean(jnp.square(o - mean), axis=-1, keepdims=True)
    o = ((o - mean) * lax.rsqrt(var + LN_EPS)).reshape(Bn, S, RET_V) * gn_g.astype(jnp.float32)
    return (jax.nn.silu(g) * o.astype(h.dtype)) @ w_out


def conv_ffn(h, w_up, conv_w, conv_b, w_down):
    S = h.shape[1]
    a = h @ w_up
    ap = jnp.pad(a, ((0, 0), (CONV_W - 1, 0), (0, 0)))
    c = conv_b + sum(ap[:, j:j + S] * conv_w[j] for j in range(CONV_W))
    gate, up = jnp.split(c, 2, axis=-1)
    return (jax.nn.gelu(gate, approximate=False) * up) @ w_down


def setup_inputs(seed: int = 0) -> dict:
    key = jax.random.key(seed)
    ks = jax.random.split(key, 24)
    f32 = jnp.float32

    def nrm(k, shape, scale):
        return jax.random.normal(k, shape, f32) * scale

    def gain(k, shape):
        return 1.0 + 0.05 * jax.random.normal(k, shape, f32)

    return {
        'x': nrm(ks[0], (BATCH, SEQ, D_MODEL), 1.0),
        'p': nrm(ks[1], (DEPTH, BATCH, SEQ, PLE_DIM), 1.0),
        'mix_norm_g': gain(ks[2], (DEPTH, D_MODEL)),
        'ffn_norm_g': gain(ks[3], (DEPTH, D_MODEL)),
        'ple_norm_g': gain(ks[4], (DEPTH, D_MODEL)),
        'ab_w_in': nrm(ks[5], (N_EVEN, D_MODEL, AB_IN), D_MODEL ** -0.5),
        'sg_ln_g': gain(ks[6], (N_EVEN, SG_WIDTH)),
        'sg_ln_b': nrm(ks[7], (N_EVEN, SG_WIDTH), 0.02),
        'sg_w': nrm(ks[8], (N_EVEN, SG_GROUPS, CHUNK, CHUNK), CHUNK ** -0.5),
        'sg_b': 1.0 + nrm(ks[9], (N_EVEN, SG_GROUPS, CHUNK), 0.1),
        'ab_w_out': nrm(ks[10], (N_EVEN, AB_OUT, D_MODEL), AB_OUT ** -0.5),
        'ret_w_in': nrm(ks[11], (N_ODD, D_MODEL, RET_IN), D_MODEL ** -0.5),
        'ret_gn_g': gain(ks[12], (N_ODD, RET_V)),
        'ret_w_out': nrm(ks[13], (N_ODD, RET_V, D_MODEL), RET_V ** -0.5),
        'ffn_w_up': nrm(ks[14], (DEPTH, D_MODEL, 2 * D_FF), D_MODEL ** -0.5),
        'ffn_conv_w': nrm(ks[15], (DEPTH, CONV_W, 2 * D_FF), CONV_W ** -0.5),
        'ffn_conv_b': nrm(ks[16], (DEPTH, 2 * D_FF), 0.02),
        'ffn_w_down': nrm(ks[17], (DEPTH, D_FF, D_MODEL), D_FF ** -0.5),
        'ple_w_gate': nrm(ks[18], (DEPTH, D_MODEL, D_MODEL), D_MODEL ** -0.5),
        'ple_w_proj': nrm(ks[19], (DEPTH, PLE_DIM, D_MODEL), PLE_DIM ** -0.5),
        'ple_post_g': gain(ks[20], (DEPTH, D_MODEL)),
        'final_norm_g': gain(ks[21], (D_MODEL,)),
    }


def reference(x, p, mix_norm_g, ffn_norm_g, ple_norm_g, ab_w_in, sg_ln_g, sg_ln_b, sg_w, sg_b, ab_w_out,
              ret_w_in, ret_gn_g, ret_w_out, ffn_w_up, ffn_conv_w, ffn_conv_b, ffn_w_down,
              ple_w_gate, ple_w_proj, ple_post_g, final_norm_g):
    for i in range(DEPTH):
        h = rms_norm(x, mix_norm_g[i])
        if i % 2 == 0:
            e = i // 2
            x = x + hybrid_sg_sb_mixer(h, ab_w_in[e], sg_ln_g[e], sg_ln_b[e], sg_w[e], sg_b[e], ab_w_out[e])
        else:
            o = i // 2
            x = x + retention_mixer(h, ret_w_in[o], ret_gn_g[o], ret_w_out[o])
        x = x + conv_ffn(rms_norm(x, ffn_norm_g[i]), ffn_w_up[i], ffn_conv_w[i], ffn_conv_b[i], ffn_w_down[i])
        gate = jax.nn.sigmoid(rms_norm(x, ple_norm_g[i]) @ ple_w_gate[i])
        x = x + gate * rms_norm(p[i] @ ple_w_proj[i], ple_post_g[i])
    return rms_norm(x, final_norm_g)
```

```python
import contextlib
import numpy as np
import concourse.bass as bass
import concourse.mybir as mybir
from concourse.bass_utils import run_bass_kernel_spmd

F32 = mybir.dt.float32
BF16 = mybir.dt.bfloat16
AF = mybir.ActivationFunctionType
ALU = mybir.AluOpType

D = 1024
SEQ = 2048
NT = 16
DFF = 2816
NFC = 22
RMS_EPS = 1e-6
LN_EPS = 1e-5
N_CORES = 8
SEQ_PER_CORE = 2


class T:
    __slots__ = ("name", "w", "r", "psum")

    def __init__(self, name="", psum=False):
        self.name = name
        self.w = None
        self.r = []
        self.psum = psum


class Op:
    __slots__ = ("eng", "seq", "fn", "waits", "chan", "key", "sig", "clock", "signal")

    def __init__(self, eng, seq, fn, chan):
        self.eng = eng
        self.seq = seq
        self.fn = fn
        self.chan = chan
        self.waits = []
        self.sig = None
        self.clock = None
        self.signal = False


class Sched:
    ENGS = ("pe", "act", "dve", "pool", "sp")

    def __init__(self):
        self.ops = {e: [] for e in self.ENGS}
        self.seen = {e: {} for e in self.ENGS}
        self.chan_count = {}
        self.chan_last = {}
        self.n_waits = 0

    def add(self, eng, fn, reads=(), writes=(), chan=None, extra=()):
        lst = self.ops[eng]
        op = Op(eng, len(lst), fn, chan)
        if chan is not None:
            c = self.chan_count.get(chan, 0) + 1
            self.chan_count[chan] = c
            op.key = ("c", chan)
            op.seq = c
            op.signal = True
            self.chan_last[chan] = op
        else:
            op.key = eng
        deps = {}
        for d in extra:
            deps[id(d)] = d
        for t in reads:
            if t.w is not None:
                deps[id(t.w)] = t.w
            if t.psum:
                for r in t.r:
                    if r.eng != eng:
                        deps[id(r)] = r
        for t in writes:
            if t.w is not None:
                deps[id(t.w)] = t.w
            for r in t.r:
                deps[id(r)] = r
        seen = self.seen[eng]
        for d in sorted(deps.values(), key=lambda o: -o.seq):
            if d is op:
                continue
            if d.chan is None and d.eng == "pe" and eng == "pe" and chan is None:
                continue
            need = d.seq if d.chan is not None else d.seq + 1
            if seen.get(d.key, 0) >= need:
                continue
            op.waits.append(d)
            d.signal = True
            self.n_waits += 1
            for k, v in d.clock.items():
                if seen.get(k, 0) < v:
                    seen[k] = v
        clk = dict(seen)
        clk[op.key] = op.seq if chan is not None else op.seq + 1
        op.clock = clk
        for t in reads:
            t.r.append(op)
        for t in writes:
            t.w = op
            t.r = []
        lst.append(op)
        return op

    def barrier(self):
        lasts = []
        for e in self.ENGS:
            for op in reversed(self.ops[e]):
                if op.chan is None and op.fn is not None:
                    lasts.append(op)
                    break
        lasts += list(self.chan_last.values())
        for e in self.ENGS:
            self.add(e, None, extra=lasts)

    def emit(self, nc):
        handles = {"pe": "tensor", "act": "scalar", "dve": "vector", "pool": "gpsimd", "sp": "sync"}
        with contextlib.ExitStack() as st:
            sems = {}
            for e in self.ENGS:
                sems[e] = st.enter_context(nc.semaphore("s_" + e))
            for c in self.chan_count:
                sems[("c", c)] = st.enter_context(nc.semaphore("c_" + str(c)))
            for e in self.ENGS:
                cnt = 0
                for op in self.ops[e]:
                    if op.chan is not None:
                        op.sig = 16 * op.seq
                    elif op.signal:
                        cnt += 1
                        op.sig = cnt
            block = st.enter_context(nc.Block())

            def make(e):
                def body(eng):
                    for op in self.ops[e]:
                        for d in op.waits:
                            eng.wait_ge(sems[d.key], d.sig)
                        if op.fn is None:
                            continue
                        ins = op.fn(eng)
                        if op.signal:
                            ins.then_inc(sems[op.key], 16 if op.chan is not None else 1)
                return body

            for e in self.ENGS:
                if self.ops[e]:
                    getattr(block, handles[e])(make(e))


def _const_tables():
    idx = np.arange(128)
    c = {}
    c["ident"] = np.eye(128)
    c["negL"] = -(idx[:, None] >= idx[None, :]).astype(np.float64)
    c["ones"] = np.ones((128, 128))
    c["mstrict"] = (idx[:, None] < idx[None, :]).astype(np.float64)
    c["mincl"] = (idx[:, None] <= idx[None, :]).astype(np.float64)
    lg = np.log(1.0 - 2.0 ** (-5.0 - np.arange(4)))
    dec = []
    for h in range(4):
        diff = idx[None, :] - idx[:, None]
        dec.append(np.where(diff >= 0, np.exp(diff * lg[h]), 0.0) / 16.0)
    c["decayT"] = np.concatenate(dec, axis=1)
    xi = np.exp((idx + 1.0)[None, :] * lg[:, None])
    c["xi"] = np.broadcast_to(np.repeat(xi, 2, axis=0).reshape(1, 1024), (128, 1024))
    zeta = np.exp((127 - idx)[:, None] * lg[None, :]) / 16.0
    c["zeta"] = zeta
    half = 128
    inv = 1.0 / (10000.0 ** (np.arange(half, dtype=np.float32) / half))
    ang = (np.arange(SEQ, dtype=np.float32)[None, :] * inv[:, None].astype(np.float32)).astype(np.float32)
    c["cos"] = np.cos(ang)
    c["sin"] = np.sin(ang)
    order = ["ident", "negL", "ones", "mstrict", "mincl", "decayT", "xi", "zeta", "cos", "sin"]
    offs = {}
    o = 0
    for k in order:
        offs[k] = (o, c[k].shape[1])
        o += c[k].shape[1]
    tab = np.concatenate([c[k] for k in order], axis=1).astype(np.float32)
    gam128 = [float(np.exp(128 * lg[h])) for h in range(4)]
    return tab, offs, gam128


CONST_TAB, CONST_OFF, GAM128 = _const_tables()

VFM = {}
_o = 0
for _name, _n in [("mix_g", 16), ("ffn_g", 16), ("ple_g", 16), ("conv_w", 2 * 3 * 44), ("conv_b", 2 * 44), ("sg_b", 4), ("gn_g", 16)]:
    VFM[_name] = _o
    _o += _n
NVFM = _o
VBC = {}
_o = 0
for _name, _n in [("post_g", 2048), ("final_g", 1024), ("ln_g", 512), ("ln_b", 512), ("gn_g", 2048)]:
    VBC[_name] = _o
    _o += _n
NVBC = _o


def _prep_shared(inp):
    f = np.float32
    vfm = np.zeros((128, NVFM), f)

    def fm(v):
        return np.ascontiguousarray(v.reshape(-1, 128).T)

    for l in range(2):
        vfm[:, VFM["mix_g"] + 8 * l: VFM["mix_g"] + 8 * l + 8] = fm(inp["mix_norm_g"][l])
        vfm[:, VFM["ffn_g"] + 8 * l: VFM["ffn_g"] + 8 * l + 8] = fm(inp["ffn_norm_g"][l])
        vfm[:, VFM["ple_g"] + 8 * l: VFM["ple_g"] + 8 * l + 8] = fm(inp["ple_norm_g"][l])
        for j in range(3):
            o = VFM["conv_w"] + (l * 3 + j) * 44
            vfm[:, o:o + 44] = fm(inp["ffn_conv_w"][l, j])
        o = VFM["conv_b"] + l * 44
        vfm[:, o:o + 44] = fm(inp["ffn_conv_b"][l])
    vfm[:, VFM["sg_b"]:VFM["sg_b"] + 4] = inp["sg_b"][0].T
    vfm[:, VFM["gn_g"]:VFM["gn_g"] + 16] = fm(inp["ret_gn_g"][0])
    vbc = np.zeros((128, NVBC), f)

    def bc(v):
        return np.broadcast_to(v[None, :], (128, v.shape[0]))

    for l in range(2):
        vbc[:, VBC["post_g"] + 1024 * l: VBC["post_g"] + 1024 * (l + 1)] = bc(inp["ple_post_g"][l])
    vbc[:, VBC["final_g"]:VBC["final_g"] + 1024] = bc(inp["final_norm_g"])
    vbc[:, VBC["ln_g"]:VBC["ln_g"] + 512] = bc(inp["sg_ln_g"][0])
    vbc[:, VBC["ln_b"]:VBC["ln_b"] + 512] = bc(inp["sg_ln_b"][0])
    vbc[:, VBC["gn_g"]:VBC["gn_g"] + 2048] = bc(inp["ret_gn_g"][0])
    sgwT = np.ascontiguousarray(np.transpose(inp["sg_w"][0], (2, 0, 1))).reshape(128, 512)
    shared = {
        "consts": CONST_TAB, "vfm": vfm, "vbc": vbc, "sgwT": sgwT.astype(f),
        "ab_w_in": np.ascontiguousarray(inp["ab_w_in"][0]), "ab_w_out": np.ascontiguousarray(inp["ab_w_out"][0]),
        "ret_w_in": np.ascontiguousarray(inp["ret_w_in"][0]), "ret_w_out": np.ascontiguousarray(inp["ret_w_out"][0]),
        "ffn_w_up": np.ascontiguousarray(inp["ffn_w_up"]), "ffn_w_down": np.ascontiguousarray(inp["ffn_w_down"]),
        "ple_w_gate": np.ascontiguousarray(inp["ple_w_gate"]), "ple_w_proj": np.ascontiguousarray(inp["ple_w_proj"]),
    }
    return shared


class K:
    pass


class Arena:
    def __init__(self, tensor, nbytes):
        self.t = tensor
        self.n = nbytes
        self.top = 0

    def alloc(self, nbytes):
        nbytes = (nbytes + 63) // 64 * 64
        o = self.top
        self.top += nbytes
        assert self.top <= self.n, ("arena overflow", self.top, self.n)
        return o

    def f32(self, off, n):
        return self.t[:, off // 4: off // 4 + n]

    def bf(self, off, n):
        return self.t[:, off // 4: off // 4 + (n + 1) // 2].bitcast(BF16)

    def new_f32(self, n):
        return self.f32(self.alloc(4 * n), n)

    def new_bf(self, n):
        return self.bf(self.alloc(2 * n), n)


def r3(ap, b):
    return ap.rearrange("p (a b) -> p a b", b=b)


def build_program(n_seq=SEQ_PER_CORE, stages=("mix0", "ffn0", "ple0", "mix1", "ffn1", "ple1")):
    nc = bass.Bass("TRN2", target_bir_lowering=False)
    k = K()
    k.nc = nc
    dt = lambda name, shape, kind="ExternalInput": nc.dram_tensor(name, shape, F32, kind=kind).ap()
    k.x_d = dt("x", [n_seq, SEQ, D])
    k.p_d = dt("p", [2, n_seq, SEQ, 256])
    k.consts_d = dt("consts", list(CONST_TAB.shape))
    k.vfm_d = dt("vfm", [128, NVFM])
    k.vbc_d = dt("vbc", [128, NVBC])
    k.sgwT_d = dt("sgwT", [128, 512])
    k.ab_w_in = dt("ab_w_in", [1024, 2560])
    k.ab_w_out = dt("ab_w_out", [1024, 1024])
    k.ret_w_in = dt("ret_w_in", [1024, 6144])
    k.ret_w_out = dt("ret_w_out", [2048, 1024])
    k.ffn_w_up = dt("ffn_w_up", [2, 1024, 5632])
    k.ffn_w_down = dt("ffn_w_down", [2, 2816, 1024])
    k.ple_w_gate = dt("ple_w_gate", [2, 1024, 1024])
    k.ple_w_proj = dt("ple_w_proj", [2, 256, 1024])
    k.out_d = dt("out", [n_seq, SEQ, D], kind="ExternalOutput")
    k.n_seq = n_seq
    k.stages = stages

    with contextlib.ExitStack() as st:
        ARENA_BYTES = 212736
        at = st.enter_context(nc.sbuf_tensor("arena", [128, ARENA_BYTES // 4], F32))
        k.A = Arena(at, ARENA_BYTES)
        k.pbig = st.enter_context(nc.psum_tensor("pbig", [128, 4096], F32))
        k.pb = [k.pbig[:, i * 512:(i + 1) * 512] for i in range(8)]
        k.S = Sched()
        _emit_all(k)
        k.S.emit(nc)
    k.nc = nc
    return nc, k


def cslice(name):
    o, n = CONST_OFF[name]
    return slice(o, o + n)


def _emit_all(k):
    S, A, nc = k.S, k.A, k.nc
    k.X = r3(A.new_f32(NT * D), D)
    k.Xt = [T("X%d" % i) for i in range(NT)]
    k.ident = A.new_bf(128)
    k.negL = A.new_bf(128)
    k.ones = A.new_bf(128)
    k.mstrict = A.new_bf(128)
    k.vfm = A.new_f32(NVFM)
    k.ss = A.new_f32(16)
    k.lnv = A.new_f32(16)
    k.rstd = A.new_f32(16)
    k.neghalf = A.new_f32(16)[:, 0:1]
    k.Tconst = T("const")
    k.eps_ln = LN_EPS
    k.one_b = 1.0
    k.epsr = RMS_EPS
    k.Tss = [T() for _ in range(16)]
    k.Tstat = T()
    k.persist_top = A.top
    k.pbT = [T("pb%d" % i, psum=True) for i in range(8)]
    def load_x_tile(s, i):
        S.add("sp", (lambda e: e.dma_start(out=k.X[:, i, :], in_=k.x_d[s, i * 128:(i + 1) * 128, :])), writes=[k.Xt[i]], chan="xin%d" % i)
    k.load_x_tile = load_x_tile
    for i in range(NT):
        load_x_tile(0, i)
    for name, dst in [("ident", k.ident), ("negL", k.negL), ("ones", k.ones), ("mstrict", k.mstrict)]:
        S.add("pool", (lambda e, dst=dst, name=name: e.dma_start(out=dst, in_=k.consts_d[:, cslice(name)])),
              writes=[k.Tconst], chan="const")
    S.add("sp", lambda e: e.dma_start(out=k.vfm, in_=k.vfm_d), writes=[k.Tconst], chan="constsp")
    S.add("pool", lambda e: e.memset(k.neghalf, -0.5), writes=[k.Tconst])

    out_ops = []
    def load_x_tile(s, i):
        S.add("sp", (lambda e: e.dma_start(out=k.X[:, i, :], in_=k.x_d[s, i * 128:(i + 1) * 128, :])), writes=[k.Xt[i]], chan="xin%d" % i)
    k.load_x_tile = load_x_tile
    for s in range(k.n_seq):
        k.next_seq = s + 1 if (s + 1 < k.n_seq and "ple1" in k.stages) else None
        if s > 0 and "ple1" not in k.stages:
            for i in range(NT):
                load_x_tile(s, i)
        if "mix0" in k.stages:
            phase_mix0(k, s)
        if "ffn0" in k.stages:
            phase_ffn(k, s, 0)
        if "ple0" in k.stages:
            phase_ple(k, s, 0, final=False)
        if "mix1" in k.stages:
            phase_mix1(k, s)
        if "ffn1" in k.stages:
            phase_ffn(k, s, 1)
        if "ple1" in k.stages:
            out_ops += phase_ple(k, s, 1, final=True)
        else:
            out_ops += phase_dump(k, s)
        S.barrier()
    S.add("sp", None, extra=out_ops)


def phase_dump(k, s):
    ops = []
    for i in range(NT):
        ops.append(k.S.add("sp", (lambda e, i=i: e.dma_start(out=k.out_d[s, i * 128:(i + 1) * 128, :], in_=k.X[:, i, :])),
                           reads=[k.Xt[i]], chan="xout"))
    return ops


def norm_hT(k, i, gcol, hn, hnT, tb, dst, dstT, slot):
    S = k.S
    X = k.X
    ss = k.ss[:, slot:slot + 1]
    lnv = k.lnv[:, slot:slot + 1]
    rstd = k.rstd[:, slot:slot + 1]
    tss = k.Tss[slot]
    S.add("act", lambda e: e.activation(out=hn, in_=X[:, i, :], func=AF.Square, accum_out=ss),
          reads=[k.Xt[i]], writes=[hnT, tss])
    S.add("act", lambda e: e.activation(out=lnv, in_=ss, func=AF.Ln, scale=1.0 / D, bias=k.epsr), reads=[tss, k.Tconst], writes=[tss])
    S.add("act", lambda e: e.activation(out=rstd, in_=lnv, func=AF.Exp, scale=-0.5), reads=[tss], writes=[tss])
    S.add("act", lambda e: e.activation(out=hn, in_=X[:, i, :], func=AF.Copy, scale=rstd),
          reads=[k.Xt[i], tss], writes=[hnT])
    pbv = k.pb[tb][:].bitcast(BF16)

    def tr(e):
        for c in range(8):
            ins = e.transpose(out=pbv[:, c * 128:(c + 1) * 128], in_=hn[:, c * 128:(c + 1) * 128], identity=k.ident)
        return ins
    S.add("pe", tr, reads=[hnT, k.Tconst], writes=[k.pbT[tb]])
    g = k.vfm[:, gcol:gcol + 8].unsqueeze(2).to_broadcast([128, 8, 128])
    S.add("dve", lambda e: e.tensor_tensor(out=dst, in0=r3(pbv, 128), in1=g, op=ALU.mult),
          reads=[k.pbT[tb], k.Tconst], writes=[dstT])


def norm_stats_all(k, junk, Tjunk):
    S = k.S
    if not getattr(k, "stats_ready", False):
        for i in range(NT):
            S.add("act", (lambda e, i=i: e.activation(out=junk, in_=k.X[:, i, :], func=AF.Square, accum_out=k.ss[:, i:i + 1])),
                  reads=[k.Xt[i]] + ([k.Tstat] if i == 0 else []), writes=[k.Tss[i]] + ([Tjunk] if i == 0 else []))
    k.stats_ready = False
    S.add("act", lambda e: e.activation(out=k.lnv[:, 0:16], in_=k.ss[:, 0:16], func=AF.Ln, scale=1.0 / D, bias=k.epsr), reads=list(k.Tss), writes=[k.Tstat])
    S.add("act", lambda e: e.activation(out=k.rstd[:, 0:16], in_=k.lnv[:, 0:16], func=AF.Exp, scale=-0.5), reads=[k.Tstat], writes=[k.Tstat])


def stat_tile(k, i, junk):
    k.S.add("act", lambda e: e.activation(out=junk, in_=k.X[:, i, :], func=AF.Square, accum_out=k.ss[:, i:i + 1]),
            reads=[k.Xt[i]], writes=[k.Tss[i]])


def norm_hT2(k, i, gcol, hn, hnT, tb, dst, dstT):
    S = k.S
    S.add("act", lambda e: e.activation(out=hn, in_=k.X[:, i, :], func=AF.Copy, scale=k.rstd[:, i:i + 1]),
          reads=[k.Xt[i], k.Tstat], writes=[hnT])
    pbv = k.pb[tb][:].bitcast(BF16)

    def tr(e):
        for c in range(8):
            ins = e.transpose(out=pbv[:, c * 128:(c + 1) * 128], in_=hn[:, c * 128:(c + 1) * 128], identity=k.ident)
        return ins
    S.add("pe", tr, reads=[hnT, k.Tconst], writes=[k.pbT[tb]])
    g = k.vfm[:, gcol:gcol + 8].unsqueeze(2).to_broadcast([128, 8, 128])
    S.add("dve", lambda e: e.tensor_tensor(out=dst, in0=r3(pbv, 128), in1=g, op=ALU.mult),
          reads=[k.pbT[tb], k.Tconst], writes=[dstT])


def load_w_slabs(k, dst, src, ncols, slab, chan, Ts, eng="pool"):
    srcv = src.rearrange("(kc p) n -> p kc n", p=128)
    for j, c0 in enumerate(range(0, ncols, slab)):
        c1 = min(ncols, c0 + slab)
        k.S.add(eng, (lambda e, c0=c0, c1=c1: e.dma_start(out=dst[:, :, c0:c1], in_=srcv[:, :, c0:c1])),
                writes=[Ts[j]], chan=chan)


def phase_mix0_old(k, s):
    S, A, nc = k.S, k.A, k.nc
    S.barrier()
    A.top = k.persist_top
    pb, pbT = k.pb, k.pbT
    Win = r3(A.new_bf(8 * 2560), 2560)
    Wout = r3(A.new_bf(8 * 1024), 1024)
    KT = r3(A.new_bf(4 * 2048), 2048)
    V = r3(A.new_bf(16 * 512), 512)
    SGW = A.new_bf(512)
    sgw32 = A.new_f32(512)
    mincl = A.new_f32(128)
    lng = A.new_f32(512)
    lnb = A.new_f32(512)
    hn = A.new_bf(1024)
    hT = [r3(A.new_bf(1024), 128) for _ in range(2)]
    QA = [r3(A.new_bf(512), 128) for _ in range(2)]
    QB = [r3(A.new_bf(512), 128) for _ in range(2)]
    ug = A.new_f32(512)
    vg = A.new_f32(512)
    vn32 = A.new_f32(512)
    vnb = A.new_bf(512)
    stats = A.new_f32(24)
    mv = A.new_f32(8)
    lrs = A.new_f32(4)
    lrl = A.new_f32(4)
    nmr = A.new_f32(4)
    CAT = A.new_bf(1024)
    CATT = r3(A.new_bf(1024), 128)
    E32 = [A.new_f32(512) for _ in range(2)]
    SP = [A.new_bf(512) for _ in range(2)]
    TMP = [A.new_f32(512) for _ in range(2)]
    AW = [A.new_bf(512) for _ in range(2)]
    TOT = [A.new_f32(512) for _ in range(2)]
    TWin = [T() for _ in range(5)]
    TWout = [T() for _ in range(2)]
    Tmisc = T()
    TSGW = T()
    Thn = T()
    ThT = [T(), T()]
    TQ = [T(), T()]
    TKT = [T() for _ in range(NT)]
    TV = [T() for _ in range(NT)]
    Tug, Tvg, Tvn32, Tvnb, Tst = T(), T(), T(), T(), T()
    TCATa, TCATb, TCATT = T(), T(), T()
    TE, TSP, TTMP, TAW = [T(), T()], [T(), T()], [T(), T()], [T(), T()]
    TTOT = [T(), T()]
    k.Tss = [T() for _ in range(16)]
    k.epsr = RMS_EPS

    load_w_slabs(k, Win, k.ab_w_in, 2560, 512, "w0", TWin)
    load_w_slabs(k, Wout, k.ab_w_out, 1024, 512, "w1", TWout)
    S.add("sp", lambda e: e.dma_start(out=sgw32, in_=k.sgwT_d), writes=[Tmisc], chan="misc")
    S.add("sp", lambda e: e.dma_start(out=mincl, in_=k.consts_d[:, cslice("mincl")]), writes=[Tmisc], chan="misc")
    S.add("sp", lambda e: e.dma_start(out=lng, in_=k.vbc_d[:, VBC["ln_g"]:VBC["ln_g"] + 512]), writes=[Tmisc], chan="misc")
    S.add("sp", lambda e: e.dma_start(out=lnb, in_=k.vbc_d[:, VBC["ln_b"]:VBC["ln_b"] + 512]), writes=[Tmisc], chan="misc")
    S.add("dve", lambda e: e.tensor_tensor(out=r3(SGW, 128), in0=r3(sgw32, 128),
                                           in1=mincl.unsqueeze(1).to_broadcast([128, 4, 128]), op=ALU.mult),
          reads=[Tmisc], writes=[TSGW])
    for b in range(2):
        S.add("dve", (lambda e, b=b: e.memset(QA[b][64:128, :, :], 0.0)), writes=[TQ[b]])
        S.add("dve", (lambda e, b=b: e.memset(QB[b][0:64, :, :], 0.0)), writes=[TQ[b]])
    sgb = k.vfm[:, VFM["sg_b"]:VFM["sg_b"] + 4]
    gcol = VFM["mix_g"]
    mstr_b = k.mstrict.unsqueeze(1).to_broadcast([128, 4, 128])

    norm_hT(k, 0, gcol, hn, Thn, 7, hT[0], ThT[0], 0)
    for i in range(NT):
        cur = i % 2
        h_ = hT[cur]
        ts = slice(i * 128, (i + 1) * 128)

        def proj_uv(e, h_=h_):
            for (b, c0) in ((0, 0), (1, 512)):
                for kc in range(8):
                    ins = e.matmul(pb[b][:], lhsT=h_[:, kc, :], rhs=Win[:, kc, c0:c0 + 512], start=(kc == 0), stop=(kc == 7))
            return ins
        S.add("pe", proj_uv, reads=[ThT[cur]] + TWin, writes=[pbT[0], pbT[1]])

        def proj_qk(e, h_=h_):
            for (b, c0) in ((2, 1024), (3, 1536)):
                for c in range(4):
                    for kc in range(8):
                        ins = e.matmul(pb[b][:, c * 128:(c + 1) * 128], lhsT=Win[:, kc, c0 + c * 128:c0 + (c + 1) * 128],
                                       rhs=h_[:, kc, :], start=(kc == 0), stop=(kc == 7))
            return ins
        S.add("pe", proj_qk, reads=[ThT[cur]] + TWin, writes=[pbT[2], pbT[3]])

        def proj_v(e, h_=h_):
            for kc in range(8):
                ins = e.matmul(pb[4][:], lhsT=h_[:, kc, :], rhs=Win[:, kc, 2048:2560], start=(kc == 0), stop=(kc == 7))
            return ins
        S.add("pe", proj_v, reads=[ThT[cur]] + TWin, writes=[pbT[4]])

        S.add("act", lambda e: e.activation(out=ug, in_=pb[0][:], func=AF.Gelu), reads=[pbT[0]], writes=[Tug])
        S.add("act", lambda e: e.activation(out=vg, in_=pb[1][:], func=AF.Gelu), reads=[pbT[1]], writes=[Tvg])
        S.add("act", (lambda e, cur=cur: e.activation(out=QA[cur][0:64, :, :], in_=r3(pb[2][0:64, :], 128), func=AF.Copy, scale=0.125)),
              reads=[pbT[2]], writes=[TQ[cur]])
        S.add("act", (lambda e, cur=cur: e.activation(out=QB[cur][64:128, :, :], in_=r3(pb[2][64:128, :], 128), func=AF.Copy, scale=0.125)),
              reads=[pbT[2]], writes=[TQ[cur]])
        S.add("dve", (lambda e, ts=ts: e.tensor_copy(out=KT[:, :, ts], in_=r3(pb[3][:], 128))), reads=[pbT[3]], writes=[TKT[i]])
        S.add("dve", (lambda e, i=i: e.tensor_copy(out=V[:, i, :], in_=pb[4][:])), reads=[pbT[4]], writes=[TV[i]])

        def bns(e):
            for g in range(4):
                ins = e.bn_stats(out=stats[:, g * 6:(g + 1) * 6], in_=vg[:, g * 128:(g + 1) * 128])
            return ins
        S.add("dve", bns, reads=[Tvg], writes=[Tst])

        def bna(e):
            for g in range(4):
                ins = e.bn_aggr(out=mv[:, 2 * g:2 * g + 2], in_=stats[:, g * 6:(g + 1) * 6])
            return ins
        S.add("dve", bna, reads=[Tst], writes=[Tst])
        mvv = mv.rearrange("p (g two) -> p g two", two=2)
        S.add("act", lambda e: e.activation(out=lrl, in_=mvv[:, :, 1], func=AF.Ln, bias=k.eps_ln), reads=[Tst, k.Tconst], writes=[Tst])
        S.add("act", lambda e: e.activation(out=lrs, in_=lrl, func=AF.Exp, scale=-0.5), reads=[Tst], writes=[Tst])
        S.add("dve", lambda e: e.scalar_tensor_tensor(out=nmr, in0=mvv[:, :, 0], scalar=-1.0, in1=lrs, op0=ALU.mult, op1=ALU.mult),
              reads=[Tst], writes=[Tst])

        def nrm(e):
            for g in range(4):
                ins = e.activation(out=vn32[:, g * 128:(g + 1) * 128], in_=vg[:, g * 128:(g + 1) * 128], func=AF.Identity,
                                   scale=lrs[:, g:g + 1], bias=nmr[:, g:g + 1])
            return ins
        S.add("act", nrm, reads=[Tvg, Tst], writes=[Tvn32])
        S.add("pool", lambda e: e.tensor_tensor(out=vn32, in0=vn32, in1=lng, op=ALU.mult), reads=[Tvn32, Tmisc], writes=[Tvn32])
        S.add("pool", lambda e: e.tensor_tensor(out=vnb, in0=vn32, in1=lnb, op=ALU.add), reads=[Tvn32, Tmisc], writes=[Tvnb])

        def mixmm(e):
            for g in range(4):
                ins = e.matmul(pb[5][:, g * 128:(g + 1) * 128], lhsT=SGW[:, g * 128:(g + 1) * 128], rhs=vnb[:, g * 128:(g + 1) * 128],
                               start=True, stop=True)
            return ins
        S.add("pe", mixmm, reads=[TSGW, Tvnb], writes=[pbT[5]])

        def aout(e):
            for g in range(4):
                ins = e.scalar_tensor_tensor(out=CAT[:, g * 128:(g + 1) * 128], in0=pb[5][:, g * 128:(g + 1) * 128],
                                             scalar=sgb[:, g:g + 1], in1=ug[:, g * 128:(g + 1) * 128], op0=ALU.add, op1=ALU.mult)
            return ins
        S.add("dve", aout, reads=[pbT[5], Tug, k.Tconst], writes=[TCATa])

        if i + 1 < NT:
            norm_hT(k, i + 1, gcol, hn, Thn, 7, hT[1 - cur], ThT[1 - cur], (i + 1) % 16)

        items = [(j, hf) for j in range(i, -1, -1) for hf in range(2)]
        n_it = len(items)

        def stage_a(n, cur=cur, items=items, i=i):
            j, hf = items[n]
            par = n % 2
            zb = par
            ks = slice(j * 128, (j + 1) * 128)

            def zmm(e):
                for hh in range(4):
                    h = 4 * hf + hh
                    c = h // 2
                    q = QA[cur] if h % 2 == 0 else QB[cur]
                    ins = e.matmul(pb[zb][:, hh * 128:(hh + 1) * 128], lhsT=KT[:, c, ks], rhs=q[:, c, :], start=True, stop=True)
                return ins
            S.add("pe", zmm, reads=[TKT[j], TQ[cur]], writes=[pbT[zb]])
            S.add("act", lambda e: e.activation(out=E32[par], in_=pb[zb][:], func=AF.Exp), reads=[pbT[zb]], writes=[TE[par]])
            S.add("act", lambda e: e.activation(out=SP[par], in_=E32[par], func=AF.Ln, bias=k.one_b), reads=[TE[par], k.Tconst], writes=[TSP[par]])
            if j == i:
                S.add("dve", lambda e: e.tensor_tensor(out=r3(SP[par], 128), in0=r3(SP[par], 128), in1=mstr_b, op=ALU.mult),
                      reads=[TSP[par], k.Tconst], writes=[TSP[par]])

        def stage_b(n, cur=cur, items=items, i=i):
            j, hf = items[n]
            par = n % 2
            wb = 2 + par
            tb = 4 + par
            ks = slice(j * 128, (j + 1) * 128)

            def wmm(e):
                for hh in range(4):
                    h = 4 * hf + hh
                    c = h // 2
                    q = QA[cur] if h % 2 == 0 else QB[cur]
                    e.matmul(pb[wb][:, hh * 128:(hh + 1) * 128], lhsT=KT[:, c, ks], rhs=q[:, c, :], start=(hh == 0), stop=False,
                             skip_group_check=True)
                ins = e.matmul(pb[wb][:], lhsT=k.negL, rhs=SP[par], start=False, stop=True, skip_group_check=True)
                return ins
            S.add("pe", wmm, reads=[TKT[j], TQ[cur], TSP[par], k.Tconst], writes=[pbT[wb]])
            if j > 0:
                S.add("pe", lambda e: e.matmul(pb[tb][:], lhsT=k.ones, rhs=SP[par], start=True, stop=True),
                      reads=[TSP[par], k.Tconst], writes=[pbT[tb]])
            if j < i:
                S.add("dve", lambda e: e.tensor_tensor(out=TMP[par], in0=pb[wb][:], in1=TOT[hf], op=ALU.subtract),
                      reads=[pbT[wb], TTOT[hf]], writes=[TTMP[par]])
                S.add("act", lambda e: e.activation(out=AW[par], in_=TMP[par], func=AF.Exp), reads=[TTMP[par]], writes=[TAW[par]])
            else:
                S.add("act", lambda e: e.activation(out=AW[par], in_=pb[wb][:], func=AF.Exp), reads=[pbT[wb]], writes=[TAW[par]])
                S.add("dve", lambda e: e.tensor_tensor(out=r3(AW[par], 128), in0=r3(AW[par], 128), in1=mstr_b, op=ALU.mult),
                      reads=[TAW[par], k.Tconst], writes=[TAW[par]])
            if j > 0:
                if j == i:
                    S.add("dve", lambda e: e.tensor_copy(out=TOT[hf], in_=pb[tb][:]), reads=[pbT[tb]], writes=[TTOT[hf]])
                else:
                    S.add("dve", lambda e: e.tensor_tensor(out=TOT[hf], in0=pb[tb][:], in1=TOT[hf], op=ALU.add),
                          reads=[pbT[tb], TTOT[hf]], writes=[TTOT[hf]])

        def stage_c(n, cur=cur, items=items, n_it=n_it):
            j, hf = items[n]
            par = n % 2

            def av(e):
                for hh in range(4):
                    h = 4 * hf + hh
                    ins = e.matmul(pb[6][:, h * 64:(h + 1) * 64], lhsT=AW[par][:, hh * 128:(hh + 1) * 128],
                                   rhs=V[:, j, h * 64:(h + 1) * 64], start=(n == 0 and hh == 0), stop=(n == n_it - 1 and hh == 3),
                                   skip_group_check=True)
                return ins
            S.add("pe", av, reads=[TAW[par], TV[j]], writes=[pbT[6]])

        for step in range(n_it + 2):
            if step < n_it:
                stage_a(step)
            if 0 <= step - 1 < n_it:
                stage_b(step - 1)
            if 0 <= step - 2 < n_it:
                stage_c(step - 2)
        S.add("act", lambda e: e.activation(out=CAT[:, 512:1024], in_=pb[6][:], func=AF.Copy), reads=[pbT[6]], writes=[TCATb])

        pbv = pb[7][:].bitcast(BF16)

        def trc(e):
            for c in range(8):
                ins = e.transpose(out=pbv[:, c * 128:(c + 1) * 128], in_=CAT[:, c * 128:(c + 1) * 128], identity=k.ident)
            return ins
        S.add("pe", trc, reads=[TCATa, TCATb, k.Tconst], writes=[pbT[7]])
        S.add("dve", lambda e: e.tensor_copy(out=CATT, in_=r3(pbv, 128)), reads=[pbT[7]], writes=[TCATT])

        def womm(e):
            for n2 in range(2):
                for kc in range(8):
                    ins = e.matmul(pb[n2][:], lhsT=CATT[:, kc, :], rhs=Wout[:, kc, n2 * 512:(n2 + 1) * 512], start=(kc == 0), stop=(kc == 7))
            return ins
        S.add("pe", womm, reads=[TCATT] + TWout, writes=[pbT[0], pbT[1]])
        for n2 in range(2):
            S.add("dve", (lambda e, n2=n2, i=i: e.tensor_tensor(out=k.X[:, i, n2 * 512:(n2 + 1) * 512], in0=pb[n2][:],
                                                                 in1=k.X[:, i, n2 * 512:(n2 + 1) * 512], op=ALU.add)),
                  reads=[pbT[n2], k.Xt[i]], writes=[k.Xt[i]])


def phase_mix0(k, s):
    S, A = k.S, k.A
    if s > 0:
        S.barrier()
    A.top = k.persist_top
    pb, pbT = k.pb, k.pbT
    KT = r3(A.new_bf(4 * 2048), 2048)
    V = r3(A.new_bf(16 * 512), 512)
    QT = r3(A.new_bf(4 * 2048), 2048)
    CATa = r3(A.new_bf(16 * 512), 512)
    TKT = [T() for _ in range(NT)]
    TV = [T() for _ in range(NT)]
    TQT = [T() for _ in range(NT)]
    TCATa = [T() for _ in range(NT)]
    sub_top = A.top
    Win = r3(A.new_bf(8 * 2560), 2560)
    SGW = A.new_bf(512)
    sgw32 = A.new_f32(512)
    mincl = A.new_f32(128)
    lng = A.new_f32(512)
    lnb = A.new_f32(512)
    hn = [A.new_bf(1024) for _ in range(2)]
    junk = A.new_bf(1024)
    hT = [r3(A.new_bf(1024), 128) for _ in range(2)]
    ug = [A.new_f32(512) for _ in range(2)]
    vg = [A.new_f32(512) for _ in range(2)]
    vn32 = [A.new_f32(512) for _ in range(2)]
    vnb = [A.new_bf(512) for _ in range(2)]
    stats = [A.new_f32(24) for _ in range(2)]
    mv = [A.new_f32(8) for _ in range(2)]
    sm = [A.new_f32(12) for _ in range(2)]
    TWin = [T() for _ in range(5)]
    Tmisc, TSGW, Tjunk = T(), T(), T()
    Thn, ThT, Tug, Tvg, Tvn32, Tvnb, Tst = ([T(), T()] for _ in range(7))
    load_w_slabs(k, Win, k.ab_w_in, 2560, 512, "w0", TWin)
    S.add("sp", lambda e: e.dma_start(out=sgw32, in_=k.sgwT_d), writes=[Tmisc], chan="misc")
    S.add("sp", lambda e: e.dma_start(out=mincl, in_=k.consts_d[:, cslice("mincl")]), writes=[Tmisc], chan="misc")
    S.add("sp", lambda e: e.dma_start(out=lng, in_=k.vbc_d[:, VBC["ln_g"]:VBC["ln_g"] + 512]), writes=[Tmisc], chan="misc")
    S.add("sp", lambda e: e.dma_start(out=lnb, in_=k.vbc_d[:, VBC["ln_b"]:VBC["ln_b"] + 512]), writes=[Tmisc], chan="misc")
    S.add("dve", lambda e: e.tensor_tensor(out=r3(SGW, 128), in0=r3(sgw32, 128),
                                           in1=mincl.unsqueeze(1).to_broadcast([128, 4, 128]), op=ALU.mult),
          reads=[Tmisc], writes=[TSGW])
    sgb = k.vfm[:, VFM["sg_b"]:VFM["sg_b"] + 4]
    gcol = VFM["mix_g"]
    norm_stats_all(k, junk, Tjunk)
    tails = []
    for i in range(NT):
        par = i % 2
        ts = slice(i * 128, (i + 1) * 128)
        if i == 0:
            norm_hT2(k, 0, gcol, hn[0], Thn[0], 7, hT[0], ThT[0])
        h_ = hT[par]

        def proj_uv(e, h_=h_):
            for (b, c0) in ((0, 0), (1, 512)):
                for kc in range(8):
                    ins = e.matmul(pb[b][:], lhsT=h_[:, kc, :], rhs=Win[:, kc, c0:c0 + 512], start=(kc == 0), stop=(kc == 7))
            return ins
        S.add("pe", proj_uv, reads=[ThT[par]] + TWin, writes=[pbT[0], pbT[1]])

        def proj_qk(e, h_=h_):
            for (b, c0) in ((2, 1024), (3, 1536)):
                for c in range(4):
                    for kc in range(8):
                        ins = e.matmul(pb[b][:, c * 128:(c + 1) * 128], lhsT=Win[:, kc, c0 + c * 128:c0 + (c + 1) * 128],
                                       rhs=h_[:, kc, :], start=(kc == 0), stop=(kc == 7))
            return ins
        S.add("pe", proj_qk, reads=[ThT[par]] + TWin, writes=[pbT[2], pbT[3]])

        def proj_v(e, h_=h_):
            for kc in range(8):
                ins = e.matmul(pb[4][:], lhsT=h_[:, kc, :], rhs=Win[:, kc, 2048:2560], start=(kc == 0), stop=(kc == 7))
            return ins
        S.add("pe", proj_v, reads=[ThT[par]] + TWin, writes=[pbT[4]])
        tail_dve = None
        if tails:
            tail_dve = tails.pop(0)()
        if i + 1 < NT:
            norm_hT2(k, i + 1, gcol, hn[1 - par], Thn[1 - par], 7, hT[1 - par], ThT[1 - par])
        if tail_dve is not None:
            tail_dve()
        S.add("act", (lambda e, par=par: e.activation(out=ug[par], in_=pb[0][:], func=AF.Gelu)), reads=[pbT[0]], writes=[Tug[par]])
        S.add("act", (lambda e, par=par: e.activation(out=vg[par], in_=pb[1][:], func=AF.Gelu)), reads=[pbT[1]], writes=[Tvg[par]])
        S.add("act", (lambda e, ts=ts: e.activation(out=QT[:, :, ts], in_=r3(pb[2][:], 128), func=AF.Copy, scale=0.125)), reads=[pbT[2]], writes=[TQT[i]])
        S.add("dve", (lambda e, ts=ts: e.tensor_copy(out=KT[:, :, ts], in_=r3(pb[3][:], 128))), reads=[pbT[3]], writes=[TKT[i]])
        S.add("dve", (lambda e, i=i: e.tensor_copy(out=V[:, i, :], in_=pb[4][:])), reads=[pbT[4]], writes=[TV[i]])
        st_, mv_, sm_ = stats[par], mv[par], sm[par]
        vg_, vn_, vb_, ug_ = vg[par], vn32[par], vnb[par], ug[par]

        def bns(e, st_=st_, vg_=vg_):
            for g in range(4):
                ins = e.bn_stats(out=st_[:, g * 6:(g + 1) * 6], in_=vg_[:, g * 128:(g + 1) * 128])
            return ins
        S.add("dve", bns, reads=[Tvg[par]], writes=[Tst[par]])

        def bna(e, st_=st_, mv_=mv_):
            for g in range(4):
                ins = e.bn_aggr(out=mv_[:, 2 * g:2 * g + 2], in_=st_[:, g * 6:(g + 1) * 6])
            return ins
        S.add("dve", bna, reads=[Tst[par]], writes=[Tst[par]])
        mvv = mv_.rearrange("p (g two) -> p g two", two=2)
        S.add("dve", (lambda e, sm_=sm_, mvv=mvv: e.tensor_scalar(out=sm_[:, 0:4], in0=mvv[:, :, 1], scalar1=LN_EPS, scalar2=None, op0=ALU.add)),
              reads=[Tst[par]], writes=[Tst[par]])
        S.add("pool", (lambda e, sm_=sm_: e.tensor_tensor(out=sm_[:, 4:8], in0=sm_[:, 0:4], in1=k.neghalf.to_broadcast([128, 4]), op=ALU.pow)),
              reads=[Tst[par], k.Tconst], writes=[Tst[par]])
        S.add("dve", (lambda e, sm_=sm_, mvv=mvv: e.scalar_tensor_tensor(out=sm_[:, 8:12], in0=mvv[:, :, 0], scalar=-1.0, in1=sm_[:, 4:8], op0=ALU.mult, op1=ALU.mult)),
              reads=[Tst[par]], writes=[Tst[par]])

        def nrm(e, sm_=sm_, vg_=vg_, vn_=vn_):
            for g in range(4):
                ins = e.tensor_scalar(out=vn_[:, g * 128:(g + 1) * 128], in0=vg_[:, g * 128:(g + 1) * 128], scalar1=sm_[:, 4 + g:5 + g],
                                      scalar2=sm_[:, 8 + g:9 + g], op0=ALU.mult, op1=ALU.add)
            return ins
        S.add("dve", nrm, reads=[Tvg[par], Tst[par]], writes=[Tvn32[par]])
        S.add("pool", (lambda e, vn_=vn_: e.tensor_tensor(out=vn_, in0=vn_, in1=lng, op=ALU.mult)), reads=[Tvn32[par], Tmisc], writes=[Tvn32[par]])
        S.add("pool", (lambda e, vn_=vn_, vb_=vb_: e.tensor_tensor(out=vb_, in0=vn_, in1=lnb, op=ALU.add)), reads=[Tvn32[par], Tmisc], writes=[Tvnb[par]])

        def tail(i=i, par=par, vb_=vb_, ug_=ug_):
            def mixmm(e):
                for g in range(4):
                    ins = e.matmul(pb[5][:, g * 128:(g + 1) * 128], lhsT=SGW[:, g * 128:(g + 1) * 128], rhs=vb_[:, g * 128:(g + 1) * 128],
                                   start=True, stop=True)
                return ins
            S.add("pe", mixmm, reads=[TSGW, Tvnb[par]], writes=[pbT[5]])

            def aout(e):
                for g in range(4):
                    ins = e.scalar_tensor_tensor(out=CATa[:, i, g * 128:(g + 1) * 128], in0=pb[5][:, g * 128:(g + 1) * 128],
                                                 scalar=sgb[:, g:g + 1], in1=ug_[:, g * 128:(g + 1) * 128], op0=ALU.add, op1=ALU.mult)
                return ins
            return lambda: S.add("dve", aout, reads=[pbT[5], Tug[par], k.Tconst], writes=[TCATa[i]])
        tails.append(tail)

    while tails:
        tails.pop(0)()()
    S.barrier()
    A.top = sub_top
    Wout = r3(A.new_bf(8 * 1024), 1024)
    E32 = [A.new_f32(1024) for _ in range(2)]
    SP = [A.new_bf(1024) for _ in range(2)]
    AW = [A.new_bf(1024) for _ in range(2)]
    R = [A.new_bf(1024) for _ in range(2)]
    QA = [r3(A.new_bf(512), 128) for _ in range(2)]
    QB = [r3(A.new_bf(512), 128) for _ in range(2)]
    CATb = [A.new_bf(512) for _ in range(2)]
    CATT = [r3(A.new_bf(1024), 128) for _ in range(2)]
    TWout = [T(), T()]
    TE, TSP, TAW, TR, TQ, TCATb, TCATT = ([T(), T()] for _ in range(7))
    load_w_slabs(k, Wout, k.ab_w_out, 1024, 512, "w1", TWout)
    for b in range(2):
        S.add("pool", (lambda e, b=b: e.memset(QA[b][64:128, :, :], 0.0)), writes=[TQ[b]])
        S.add("pool", (lambda e, b=b: e.memset(QB[b][0:64, :, :], 0.0)), writes=[TQ[b]])
    mstr8 = k.mstrict.unsqueeze(1).to_broadcast([128, 8, 128])
    items = [(i, j) for i in range(NT) for j in range(i, -1, -1)]
    N = len(items)
    pbv5 = pb[5][:].bitcast(BF16)

    def build_q(i):
        tp = i % 2
        ts = slice(i * 128, (i + 1) * 128)
        S.add("pool", lambda e: e.tensor_copy(out=QA[tp][0:64, :, :], in_=QT[0:64, :, ts]), reads=[TQT[i]], writes=[TQ[tp]])
        S.add("pool", lambda e: e.tensor_copy(out=QB[tp][64:128, :, :], in_=QT[64:128, :, ts]), reads=[TQT[i]], writes=[TQ[tp]])

    def zmm(e, i, j, base, first_start):
        tp = i % 2
        ks = slice(j * 128, (j + 1) * 128)
        for h in range(8):
            c = h // 2
            q = QA[tp] if h % 2 == 0 else QB[tp]
            st = True if first_start is None else (h % 4 == 0)
            ins = e.matmul(pb[base + h // 4][:, (h % 4) * 128:(h % 4 + 1) * 128], lhsT=KT[:, c, ks], rhs=q[:, c, :], start=st,
                           stop=(first_start is None), skip_group_check=True)
        return ins

    def st_front(n):
        i, j = items[n]
        ip = n % 2
        S.add("pe", lambda e: zmm(e, i, j, 0, None), reads=[TKT[j], TQ[i % 2]], writes=[pbT[0], pbT[1]])
        S.add("act", lambda e: e.activation(out=E32[ip], in_=k.pbig[:, 0:1024], func=AF.Exp), reads=[pbT[0], pbT[1]], writes=[TE[ip]])
        S.add("act", lambda e: e.activation(out=SP[ip], in_=E32[ip], func=AF.Ln, bias=k.one_b), reads=[TE[ip]], writes=[TSP[ip]])
        if j == i:
            S.add("dve", lambda e: e.tensor_tensor(out=r3(SP[ip], 128), in0=r3(SP[ip], 128), in1=mstr8, op=ALU.mult),
                  reads=[TSP[ip], k.Tconst], writes=[TSP[ip]])

    def st_w(n):
        i, j = items[n]
        ip = n % 2
        tp = i % 2

        def wmm(e):
            zmm(e, i, j, 2, True)
            for b in range(2):
                ins = e.matmul(pb[2 + b][:], lhsT=k.negL, rhs=SP[ip][:, b * 512:(b + 1) * 512], start=False, stop=(j == i), skip_group_check=True)
                if j < i:
                    ins = e.matmul(pb[2 + b][:], lhsT=k.ones, rhs=R[tp][:, b * 512:(b + 1) * 512], start=False, stop=True, skip_group_check=True)
            return ins
        S.add("pe", wmm, reads=[TKT[j], TQ[tp], TSP[ip], k.Tconst] + ([TR[tp]] if j < i else []), writes=[pbT[2], pbT[3]])
        if j > 0:
            if j == i:
                S.add("dve", lambda e: e.tensor_scalar(out=R[tp], in0=SP[ip], scalar1=-1.0, scalar2=None, op0=ALU.mult), reads=[TSP[ip]], writes=[TR[tp]])
            else:
                S.add("dve", lambda e: e.tensor_tensor(out=R[tp], in0=R[tp], in1=SP[ip], op=ALU.subtract), reads=[TSP[ip], TR[tp]], writes=[TR[tp]])

    def st_exp2(n):
        i, j = items[n]
        ip = n % 2
        S.add("act", lambda e: e.activation(out=AW[ip], in_=k.pbig[:, 1024:2048], func=AF.Exp), reads=[pbT[2], pbT[3]], writes=[TAW[ip]])
        if j == i:
            S.add("dve", lambda e: e.tensor_tensor(out=r3(AW[ip], 128), in0=r3(AW[ip], 128), in1=mstr8, op=ALU.mult),
                  reads=[TAW[ip], k.Tconst], writes=[TAW[ip]])

    def st_av(n):
        i, j = items[n]
        ip = n % 2
        tp = i % 2

        def av(e):
            for h in range(8):
                ins = e.matmul(pb[4][:, h * 64:(h + 1) * 64], lhsT=AW[ip][:, h * 128:(h + 1) * 128], rhs=V[:, j, h * 64:(h + 1) * 64],
                               start=(j == i and h == 0), stop=(j == 0 and h == 7), skip_group_check=True)
            return ins
        S.add("pe", av, reads=[TAW[ip], TV[j]], writes=[pbT[4]])
        if j == 0:
            S.add("dve", lambda e: e.tensor_copy(out=CATb[tp], in_=pb[4][:]), reads=[pbT[4]], writes=[TCATb[tp]])

            def ch_tr():
                def trc(e):
                    for c in range(8):
                        src = CATa[:, i, c * 128:(c + 1) * 128] if c < 4 else CATb[tp][:, (c - 4) * 128:(c - 3) * 128]
                        ins = e.transpose(out=pbv5[:, c * 128:(c + 1) * 128], in_=src, identity=k.ident)
                    return ins
                S.add("pe", trc, reads=[TCATa[i], TCATb[tp], k.Tconst], writes=[pbT[5]])
                S.add("dve", lambda e: e.tensor_copy(out=CATT[tp], in_=r3(pbv5, 128)), reads=[pbT[5]], writes=[TCATT[tp]])
            deferred.append(ch_tr)
            for n2 in range(2):
                for hf in range(2):
                    def ch_wo(n2=n2, hf=hf):
                        def womm(e):
                            for kc in range(4 * hf, 4 * hf + 4):
                                ins = e.matmul(pb[6 + n2][:], lhsT=CATT[tp][:, kc, :], rhs=Wout[:, kc, n2 * 512:(n2 + 1) * 512], start=(kc == 0), stop=(kc == 7))
                            return ins
                        S.add("pe", womm, reads=[TCATT[tp]] + TWout, writes=[pbT[6 + n2]])
                        if hf == 1:
                            S.add("dve", lambda e: e.tensor_tensor(out=k.X[:, i, n2 * 512:(n2 + 1) * 512], in0=pb[6 + n2][:],
                                                                   in1=k.X[:, i, n2 * 512:(n2 + 1) * 512], op=ALU.add),
                                  reads=[pbT[6 + n2], k.Xt[i]], writes=[k.Xt[i]])
                    deferred.append(ch_wo)

    deferred = []
    build_q(0)
    build_q(1)
    st_front(0)
    for n in range(N + 1):
        if n < N:
            if n + 1 < N:
                st_front(n + 1)
            st_w(n)
        if deferred:
            deferred.pop(0)()
        if n - 1 >= 0:
            st_av(n - 1)
            ip_, jp_ = items[n - 1]
            if jp_ == 0 and ip_ + 2 < NT:
                build_q(ip_ + 2)
        if n < N:
            st_exp2(n)
    while deferred:
        deferred.pop(0)()


def phase_ffn(k, s, l):
    S, A = k.S, k.A
    S.barrier()
    A.top = k.persist_top
    pb, pbT = k.pb, k.pbT
    HT = r3(A.new_bf(8 * 2050), 2050)
    NPC = [6, 6, 5, 5]
    PST = [0, 6, 12, 17]
    GT = [r3(A.new_bf(6 * 2048), 2048) for _ in range(2)]
    WDN = [r3(A.new_bf(6 * 1024), 1024) for _ in range(2)]
    NSLOT = 4
    WUP = [r3(A.new_bf(8 * 256), 256) for _ in range(NSLOT)]
    hn = A.new_bf(1024)
    junkf = A.new_bf(1024)
    TG = [A.new_f32(412) for _ in range(3)]
    GG = [A.new_f32(412) for _ in range(3)]
    TU = [A.new_f32(412) for _ in range(3)]
    Thn = T()
    THT = [T() for _ in range(NT)]
    Thalo = T()
    TGT = [[T() for _ in range(6)] for _ in range(2)]
    TWDN = [T(), T()]
    TWUPg = [T() for _ in range(NSLOT)]
    TWUPu = [T() for _ in range(NSLOT)]
    TTG, TGG, TTU = [T(), T(), T()], [T(), T(), T()], [T(), T(), T()]
    wup = k.ffn_w_up[l].rearrange("(kc p) n -> p kc n", p=128)
    wdn = k.ffn_w_down[l].rearrange("(c p) n -> p c n", p=128)
    bounds = [0, 410, 820, 1230, 1640, 2048]

    def load_up(c):
        sl = c % NSLOT
        S.add("pool", lambda e: e.dma_start(out=WUP[sl][:, :, 0:128], in_=wup[:, :, c * 128:(c + 1) * 128]),
              writes=[TWUPg[sl]], chan="wu%d" % sl)
        S.add("pool", lambda e: e.dma_start(out=WUP[sl][:, :, 128:256], in_=wup[:, :, DFF + c * 128:DFF + (c + 1) * 128]),
              writes=[TWUPu[sl]], chan="wu%d" % sl)

    def load_dn(p):
        pp = p % 2
        S.add("pool", lambda e: e.dma_start(out=WDN[pp][:, 0:NPC[p], :], in_=wdn[:, PST[p]:PST[p] + NPC[p], :]),
              writes=[TWDN[pp]], chan="wd%d" % pp)

    for c in range(NSLOT):
        load_up(c)
    load_dn(0)
    S.add("dve", lambda e: e.memset(HT[:, :, 0:2], 0.0), writes=[Thalo])
    gcol = VFM["ffn_g"] + 8 * l
    Tjunk = T()
    norm_stats_all(k, junkf, Tjunk)
    hn2 = [hn, junkf]
    Thn2 = [Thn, Tjunk]
    ht_next = [0]

    def make_ht(upto):
        while ht_next[0] <= min(upto, NT - 1):
            i = ht_next[0]
            norm_hT2(k, i, gcol, hn2[i % 2], Thn2[i % 2], 6 + (i % 2), HT[:, :, 2 + i * 128:2 + (i + 1) * 128], THT[i])
            ht_next[0] += 1
    make_ht(3)

    cw = lambda j, idx: k.vfm[:, VFM["conv_w"] + (l * 3 + j) * 44 + idx: VFM["conv_w"] + (l * 3 + j) * 44 + idx + 1]
    cb = lambda idx: k.vfm[:, VFM["conv_b"] + l * 44 + idx: VFM["conv_b"] + l * 44 + idx + 1]
    item = 0
    pending = []

    def down_unit(m, pp, p):
        if p == 3:
            b0, b1 = 2 * (m % 4), 2 * (m % 4) + 1
        else:
            b0, b1 = 6, 7
        for (b, n2) in ((b0, 0), (b1, 1)):
            def dnmm(e, b=b, n2=n2):
                for cc in range(NPC[p]):
                    ins = e.matmul(pb[b][:], lhsT=GT[pp][:, cc, m * 128:(m + 1) * 128], rhs=WDN[pp][:, cc, n2 * 512:(n2 + 1) * 512],
                                   start=(cc == 0), stop=(cc == NPC[p] - 1))
                return ins
            S.add("pe", dnmm, reads=[TWDN[pp]] + TGT[pp][:NPC[p]], writes=[pbT[b]])
        for (b, n2) in ((b0, 0), (b1, 1)):
            S.add("dve", (lambda e, b=b, n2=n2: e.tensor_tensor(out=k.X[:, m, n2 * 512:(n2 + 1) * 512], in0=pb[b][:],
                                                                 in1=k.X[:, m, n2 * 512:(n2 + 1) * 512], op=ALU.add)),
                  reads=[pbT[b], k.Xt[m]], writes=[k.Xt[m]])
        if p == 3:
            stat_tile(k, m, junkf)

    for p in range(4):
        pp = p % 2
        for cc in range(NPC[p]):
            c = PST[p] + cc
            sl = c % NSLOT
            for tt in range(5):
                t0, t1 = bounds[tt], bounds[tt + 1]
                n = t1 - t0
                par = item % 3
                item += 1
                if pending and item % 2 == 0:
                    pending.pop(0)()
                gb, ub = 2 * par, 2 * par + 1
                tiles = sorted(set([max(t0 - 2, 0) // 128, (t1 - 1) // 128] + list(range(t0 // 128, (t1 - 1) // 128 + 1))))
                make_ht(tiles[-1] + 3)

                def upmm(e, sl=sl, t0=t0, n=n, gb=gb, ub=ub):
                    for (b, o) in ((gb, 0), (ub, 128)):
                        for kc in range(8):
                            ins = e.matmul(pb[b][:, 0:n + 2], lhsT=WUP[sl][:, kc, o:o + 128], rhs=HT[:, kc, t0:t0 + n + 2],
                                           start=(kc == 0), stop=(kc == 7))
                    return ins
                S.add("pe", upmm, reads=[TWUPg[sl], TWUPu[sl], Thalo] + [THT[x] for x in tiles], writes=[pbT[gb], pbT[ub]])
                G, U = pb[gb], pb[ub]
                tg, gg, tu = TG[par][:, 0:n], GG[par][:, 0:n], TU[par][:, 0:n]
                S.add("act", (lambda e, G=G, tg=tg, c=c, n=n: e.activation(out=tg, in_=G[:, 0:n], func=AF.Identity, scale=cw(0, c), bias=cb(c))),
                      reads=[pbT[gb], k.Tconst], writes=[TTG[par]])
                S.add("act", (lambda e, U=U, tu=tu, c=c, n=n: e.activation(out=tu, in_=U[:, 0:n], func=AF.Identity, scale=cw(0, 22 + c), bias=cb(22 + c))),
                      reads=[pbT[ub], k.Tconst], writes=[TTU[par]])
                S.add("dve", (lambda e, G=G, tg=tg, c=c, n=n: e.scalar_tensor_tensor(out=tg, in0=G[:, 1:n + 1], scalar=cw(1, c), in1=tg, op0=ALU.mult, op1=ALU.add)),
                      reads=[pbT[gb], TTG[par], k.Tconst], writes=[TTG[par]])
                S.add("dve", (lambda e, G=G, tg=tg, c=c, n=n: e.scalar_tensor_tensor(out=tg, in0=G[:, 2:n + 2], scalar=cw(2, c), in1=tg, op0=ALU.mult, op1=ALU.add)),
                      reads=[pbT[gb], TTG[par], k.Tconst], writes=[TTG[par]])
                S.add("act", (lambda e, tg=tg, gg=gg: e.activation(out=gg, in_=tg, func=AF.Gelu)), reads=[TTG[par]], writes=[TGG[par]])
                S.add("dve", (lambda e, U=U, tu=tu, c=c, n=n: e.scalar_tensor_tensor(out=tu, in0=U[:, 1:n + 1], scalar=cw(1, 22 + c), in1=tu, op0=ALU.mult, op1=ALU.add)),
                      reads=[pbT[ub], TTU[par], k.Tconst], writes=[TTU[par]])
                S.add("dve", (lambda e, U=U, tu=tu, c=c, n=n: e.scalar_tensor_tensor(out=tu, in0=U[:, 2:n + 2], scalar=cw(2, 22 + c), in1=tu, op0=ALU.mult, op1=ALU.add)),
                      reads=[pbT[ub], TTU[par], k.Tconst], writes=[TTU[par]])
                S.add("pool", (lambda e, gg=gg, tu=tu, pp=pp, cc=cc, t0=t0, t1=t1: e.tensor_tensor(out=GT[pp][:, cc, t0:t1], in0=gg, in1=tu, op=ALU.mult)),
                      reads=[TGG[par], TTU[par]], writes=[TGT[pp][cc]])
            if c + NSLOT < NFC:
                load_up(c + NSLOT)
        while pending:
            pending.pop(0)()
        if p + 1 < 4:
            load_dn(p + 1)
        for m in range(NT):
            pending.append(lambda m=m, pp=pp, p=p: down_unit(m, pp, p))
    while pending:
        pending.pop(0)()
    k.stats_ready = True


def phase_ple(k, s, l, final):
    S, A = k.S, k.A
    S.barrier()
    A.top = k.persist_top
    pb, pbT = k.pb, k.pbT
    Wg = r3(A.new_bf(8 * 1024), 1024)
    Wp = r3(A.new_bf(2 * 1024), 1024)
    postg = A.new_f32(1024)
    finalg = A.new_f32(1024)
    hn = [A.new_bf(1024) for _ in range(2)]
    hT = [r3(A.new_bf(1024), 128) for _ in range(2)]
    pbf = [A.new_bf(256) for _ in range(2)]
    PT = r3(A.new_bf(2 * SEQ), SEQ)
    sig = [A.new_f32(1024) for _ in range(2)]
    t2 = [A.new_f32(1024) for _ in range(2)]
    junk = A.new_bf(1024)
    outt = [A.new_f32(1024) for _ in range(2)]
    ssp = A.new_f32(32)
    sp1 = A.new_f32(16)
    sp2 = A.new_f32(16)
    rsp = A.new_f32(16)
    TWg = [T(), T()]
    TWp = [T()]
    Tmisc = T()
    Thn = [T(), T()]
    ThT = [T(), T()]
    Tpbf = [T(), T()]
    TPT = [T() for _ in range(NT)]
    Tsig, Tt2 = [T(), T()], [T(), T()]
    Tjunk, Tssp = T(), T()
    Tsspi = [T() for _ in range(32)]
    Tout = [T(), T()]
    load_w_slabs(k, Wp, k.ple_w_proj[l], 1024, 1024, "w1", TWp)
    S.add("sp", lambda e: e.dma_start(out=postg, in_=k.vbc_d[:, VBC["post_g"] + 1024 * l:VBC["post_g"] + 1024 * (l + 1)]), writes=[Tmisc], chan="misc")
    if final:
        S.add("sp", lambda e: e.dma_start(out=finalg, in_=k.vbc_d[:, VBC["final_g"]:VBC["final_g"] + 1024]), writes=[Tmisc], chan="misc")
    gcol = VFM["ple_g"] + 8 * l
    out_ops = []
    fin_q = []
    fms = A.new_f32(16)
    frs = A.new_f32(16)
    Tf = [T() for _ in range(NT)]
    for i in range(NT):
        cur = i % 2
        S.add("pool", (lambda e, i=i, cur=cur: e.dma_start(out=pbf[cur], in_=k.p_d[l, s, i * 128:(i + 1) * 128, :])),
              writes=[Tpbf[cur]], chan="pin%d" % cur)
        if i == 1:
            load_w_slabs(k, Wg, k.ple_w_gate[l], 1024, 512, "w0", TWg)
        tb = 6 + cur
        pbv = pb[tb][:].bitcast(BF16)

        def trp(e, cur=cur, pbv=pbv):
            for c in range(2):
                ins = e.transpose(out=pbv[:, c * 128:(c + 1) * 128], in_=pbf[cur][:, c * 128:(c + 1) * 128], identity=k.ident)
            return ins
        S.add("pe", trp, reads=[Tpbf[cur], k.Tconst], writes=[pbT[tb]])
        S.add("dve", (lambda e, i=i, pbv=pbv: e.tensor_copy(out=PT[:, :, i * 128:(i + 1) * 128], in_=r3(pbv[:, 0:256], 128))), reads=[pbT[tb]], writes=[TPT[i]])
        b0, b1 = 4 * cur, 4 * cur + 1

        def pmm(e, i=i, b0=b0, b1=b1):
            for (b, n2) in ((b0, 0), (b1, 1)):
                for kc in range(2):
                    ins = e.matmul(pb[b][:], lhsT=PT[:, kc, i * 128:(i + 1) * 128], rhs=Wp[:, kc, n2 * 512:(n2 + 1) * 512], start=(kc == 0), stop=(kc == 1))
            return ins
        S.add("pe", pmm, reads=[TPT[i]] + TWp, writes=[pbT[b0], pbT[b1]])
        for (b, n2) in ((b0, 0), (b1, 1)):
            S.add("act", (lambda e, b=b, n2=n2, i=i: e.activation(out=junk[:, 0:512], in_=pb[b][:], func=AF.Square, accum_out=ssp[:, 2 * i + n2:2 * i + n2 + 1])),
                  reads=[pbT[b]], writes=[Tsspi[2 * i + n2]])
    S.add("pool", lambda e: e.tensor_tensor(out=Wp, in0=Wp, in1=postg.unsqueeze(1).to_broadcast([128, 2, 1024]), op=ALU.mult),
          reads=[Tmisc] + TWp, writes=[TWp[0]])
    norm_stats_all(k, junk, Tjunk)
    sspv = ssp.rearrange("p (i two) -> p i two", two=2)
    S.add("dve", lambda e: e.tensor_tensor(out=sp1, in0=sspv[:, :, 0], in1=sspv[:, :, 1], op=ALU.add), reads=Tsspi, writes=[Tssp])
    S.add("act", lambda e: e.activation(out=sp2, in_=sp1, func=AF.Ln, scale=1.0 / D, bias=k.epsr), reads=[Tssp], writes=[Tssp])
    S.add("act", lambda e: e.activation(out=rsp, in_=sp2, func=AF.Exp, scale=-0.5), reads=[Tssp], writes=[Tssp])
    norm_hT2(k, 0, gcol, hn[0], Thn[0], 6, hT[0], ThT[0])
    norm_hT2(k, 1, gcol, hn[1], Thn[1], 7, hT[1], ThT[1])
    for i in range(NT):
        cur = i % 2
        g0, g1 = 2 * cur, 2 * cur + 1
        p0, p1 = 4, 5

        def gmm(e, cur=cur, g0=g0, g1=g1):
            for (b, n2) in ((g0, 0), (g1, 1)):
                for kc in range(8):
                    ins = e.matmul(pb[b][:], lhsT=hT[cur][:, kc, :], rhs=Wg[:, kc, n2 * 512:(n2 + 1) * 512], start=(kc == 0), stop=(kc == 7))
            return ins
        S.add("pe", gmm, reads=[ThT[cur]] + TWg, writes=[pbT[g0], pbT[g1]])

        def pmm2(e, i=i, p0=p0, p1=p1):
            for (b, n2) in ((p0, 0), (p1, 1)):
                for kc in range(2):
                    ins = e.matmul(pb[b][:], lhsT=PT[:, kc, i * 128:(i + 1) * 128], rhs=Wp[:, kc, n2 * 512:(n2 + 1) * 512], start=(kc == 0), stop=(kc == 1))
            return ins
        S.add("pe", pmm2, reads=[TPT[i]] + TWp, writes=[pbT[p0], pbT[p1]])
        if i + 2 < NT:
            norm_hT2(k, i + 2, gcol, hn[cur], Thn[cur], 6 + cur, hT[cur], ThT[cur])
        for (b, n2) in ((g0, 0), (g1, 1)):
            S.add("act", (lambda e, b=b, n2=n2, cur=cur: e.activation(out=sig[cur][:, n2 * 512:(n2 + 1) * 512], in_=pb[b][:], func=AF.Sigmoid)),
                  reads=[pbT[b]], writes=[Tsig[cur]])
        for (b, n2) in ((p0, 0), (p1, 1)):
            S.add("dve", (lambda e, b=b, n2=n2, cur=cur, i=i: e.scalar_tensor_tensor(out=t2[cur][:, n2 * 512:(n2 + 1) * 512], in0=pb[b][:], scalar=rsp[:, i:i + 1],
                                                                                   in1=sig[cur][:, n2 * 512:(n2 + 1) * 512], op0=ALU.mult, op1=ALU.mult)),
                  reads=[pbT[b], Tssp, Tsig[cur]], writes=[Tt2[cur]])
        S.add("pool", (lambda e, i=i, cur=cur: e.tensor_tensor(out=k.X[:, i, :], in0=t2[cur], in1=k.X[:, i, :], op=ALU.add)), reads=[Tt2[cur], k.Xt[i]], writes=[k.Xt[i]])
        if not final:
            fin_q.append(lambda i=i: stat_tile(k, i, junk))
            if len(fin_q) > 2:
                fin_q.pop(0)()
        if final:
            def fin(i=i, cur=cur):
                S.add("act", (lambda e, i=i: e.activation(out=junk, in_=k.X[:, i, :], func=AF.Square, accum_out=fms[:, i:i + 1])), reads=[k.Xt[i]], writes=[Tf[i]])
                S.add("dve", (lambda e, i=i: e.tensor_scalar(out=fms[:, i:i + 1], in0=fms[:, i:i + 1], scalar1=1.0 / D, scalar2=RMS_EPS, op0=ALU.mult, op1=ALU.add)),
                      reads=[Tf[i]], writes=[Tf[i]])
                S.add("pool", (lambda e, i=i: e.tensor_tensor(out=frs[:, i:i + 1], in0=fms[:, i:i + 1], in1=k.neghalf, op=ALU.pow)), reads=[Tf[i], k.Tconst], writes=[Tf[i]])
                S.add("act", (lambda e, i=i, cur=cur: e.activation(out=outt[cur], in_=k.X[:, i, :], func=AF.Copy, scale=frs[:, i:i + 1])),
                      reads=[k.Xt[i], Tf[i]], writes=[Tout[cur]])
                if k.next_seq is not None:
                    k.load_x_tile(k.next_seq, i)
                S.add("pool" if cur == 0 else "dve", (lambda e, cur=cur: e.tensor_tensor(out=outt[cur], in0=outt[cur], in1=finalg, op=ALU.mult)), reads=[Tout[cur], Tmisc], writes=[Tout[cur]])
                out_ops.append(S.add("sp", (lambda e, i=i, cur=cur: e.dma_start(out=k.out_d[s, i * 128:(i + 1) * 128, :], in_=outt[cur])),
                                     reads=[Tout[cur]], chan="xout%d" % cur))

            fin_q.append(fin)
            if len(fin_q) > 2:
                fin_q.pop(0)()
    while fin_q:
        fin_q.pop(0)()
    k.stats_ready = not final
    return out_ops


def phase_mix1(k, s):
    S, A = k.S, k.A
    S.barrier()
    A.top = k.persist_top
    pb, pbT = k.pb, k.pbT
    HT = r3(A.new_bf(8 * SEQ), SEQ)
    WI = [r3(A.new_bf(8 * 1536), 1536) for _ in range(2)]
    WO = [r3(A.new_bf(4 * 1024), 1024) for _ in range(2)]
    S32 = r3(A.new_f32(2 * 512), 512)
    Sb = r3(A.new_bf(2 * 512), 512)
    cs = [A.new_f32(256) for _ in range(2)]
    decT = A.new_f32(512)
    xi8 = r3(A.new_f32(1024), 128)
    zeta = A.new_f32(4)
    hn = A.new_bf(1024)
    junk = A.new_bf(1024)
    QK = [A.new_bf(512) for _ in range(2)]
    Vt = [A.new_bf(512) for _ in range(2)]
    SGt = [A.new_bf(512) for _ in range(2)]
    qx = [r3(A.new_bf(256), 128) for _ in range(2)]
    ktm = [A.new_bf(256) for _ in range(2)]
    innT = [A.new_bf(128) for _ in range(2)]
    RT = [[A.new_f32(256) for _ in range(4)] for _ in range(2)]
    Y = [A.new_bf(512) for _ in range(2)]
    YT = [r3(A.new_bf(512), 128) for _ in range(2)]
    stats = [A.new_f32(8) for _ in range(2)]
    mv = [A.new_f32(4) for _ in range(2)]
    THT = [T() for _ in range(NT)]
    TWI = [[T() for _ in range(4)] for _ in range(2)]
    TWO = [T(), T()]
    TS32 = [T(), T()]
    TSb = [T(), T()]
    Tcs = [T(), T()]
    Ttab, Thn, Tjunk = T(), T(), T()
    TQK, TVt, TSGt, Tqx, Tktm, TinnT = ([T(), T()] for _ in range(6))
    TRT = [[T() for _ in range(4)] for _ in range(2)]
    TY, TYT, Tst, Trs = ([T(), T()] for _ in range(4))
    win = k.ret_w_in.rearrange("(kc p) n -> p kc n", p=128)
    wo = k.ret_w_out.rearrange("(kc p) n -> p kc n", p=128)

    def load_head(h):
        sl = h % 2
        parts = [(0, 256, h * 256), (256, 256, 1024 + h * 256), (512, 512, 2048 + h * 512), (1024, 512, 4096 + h * 512)]
        for pi, (o, n, c0) in enumerate(parts):
            S.add("pool", (lambda e, sl=sl, o=o, n=n, c0=c0: e.dma_start(out=WI[sl][:, :, o:o + n], in_=win[:, :, c0:c0 + n])),
                  writes=[TWI[sl][pi]], chan="wi%d" % sl)
        S.add("pool", (lambda e, h=h, sl=sl: e.dma_start(out=WO[sl], in_=wo[:, 4 * h:4 * h + 4, :])), writes=[TWO[sl]], chan="wo%d" % sl)

    def scale_wo(h, kc):
        sl = h % 2
        gcolv = k.vfm[:, VFM["gn_g"] + 4 * h + kc:VFM["gn_g"] + 4 * h + kc + 1]
        S.add("dve", lambda e: e.tensor_scalar(out=WO[sl][:, kc, :], in0=WO[sl][:, kc, :], scalar1=gcolv, scalar2=None, op0=ALU.mult),
              reads=[TWO[sl], k.Tconst], writes=[TWO[sl]])

    load_head(0)
    for kc_ in range(4):
        scale_wo(0, kc_)
    for (dst, nm) in ((decT, "decayT"), (xi8.rearrange("p a b -> p (a b)"), "xi"), (zeta, "zeta")):
        S.add("sp", (lambda e, dst=dst, nm=nm: e.dma_start(out=dst, in_=k.consts_d[:, cslice(nm)])), writes=[Ttab], chan="misc")
    gcol = VFM["mix_g"] + 8
    co, _ = CONST_OFF["cos"]
    so, _ = CONST_OFF["sin"]
    norm_stats_all(k, junk, Tjunk)
    for i in range(2):
        norm_hT2(k, i, gcol, hn, Thn, 3, HT[:, :, i * 128:(i + 1) * 128], THT[i])
    items = [(h, i) for h in range(4) for i in range(NT)]
    stat_q = []
    pbv3 = pb[3][:].bitcast(BF16)
    pbv0 = pb[0][:].bitcast(BF16)

    def qk4_(par):
        return QK[par].rearrange("p (a b c) -> p a b c", a=2, b=2)

    def stage_a(n):
        h, i = items[n]
        par = n % 2
        sl = h % 2
        if i == 4 and h + 1 < 4:
            load_head(h + 1)
        if 8 <= i < 12 and h + 1 < 4:
            scale_wo(h + 1, i - 8)
        W = WI[sl]
        tsl = slice(i * 128, (i + 1) * 128)
        S.add("sp", lambda e: e.dma_start(out=cs[par][:, 0:128], in_=k.consts_d[:, co + i * 128:co + (i + 1) * 128]), writes=[Tcs[par]], chan="cs%d" % par)
        S.add("sp", lambda e: e.dma_start(out=cs[par][:, 128:256], in_=k.consts_d[:, so + i * 128:so + (i + 1) * 128]), writes=[Tcs[par]], chan="cs%d" % par)

        def pqk(e):
            for ci in range(4):
                for kc in range(8):
                    ins = e.matmul(pb[0][:, ci * 128:(ci + 1) * 128], lhsT=W[:, kc, ci * 128:(ci + 1) * 128], rhs=HT[:, kc, tsl], start=(kc == 0), stop=(kc == 7))
            return ins
        S.add("pe", pqk, reads=TWI[sl] + [THT[i]], writes=[pbT[0]])
        v4 = pb[0][:].rearrange("p (a b c) -> p a b c", a=2, b=2)
        x1, x2 = v4[:, :, 0, :], v4[:, :, 1, :]
        cosb = cs[par][:, 0:128].unsqueeze(1).to_broadcast([128, 2, 128])
        sinb = cs[par][:, 128:256].unsqueeze(1).to_broadcast([128, 2, 128])
        rt = [r3(x, 128) for x in RT[par]]
        for (ri, xin, tab) in ((0, x1, cosb), (1, x2, sinb), (2, x2, cosb), (3, x1, sinb)):
            S.add("dve", (lambda e, ri=ri, xin=xin, tab=tab: e.tensor_tensor(out=rt[ri], in0=xin, in1=tab, op=ALU.mult)),
                  reads=[pbT[0], Tcs[par]], writes=[TRT[par][ri]])
        qk4 = qk4_(par)
        S.add("pool", lambda e: e.tensor_tensor(out=qk4[:, :, 0, :], in0=rt[0], in1=rt[1], op=ALU.subtract), reads=[TRT[par][0], TRT[par][1]], writes=[TQK[par]])
        S.add("pool", lambda e: e.tensor_tensor(out=qk4[:, :, 1, :], in0=rt[2], in1=rt[3], op=ALU.add), reads=[TRT[par][2], TRT[par][3]], writes=[TQK[par]])
        if i > 0:
            xib = xi8[:, 2 * h, :].unsqueeze(1).to_broadcast([128, 2, 128])
            S.add("pool", lambda e: e.tensor_tensor(out=qx[par], in0=qk4[:, 0, :, :], in1=xib, op=ALU.mult), reads=[TQK[par], Ttab], writes=[Tqx[par]])
        if h == 0 and i + 2 < NT:
            norm_hT2(k, i + 2, gcol, hn, Thn, 3, HT[:, :, (i + 2) * 128:(i + 3) * 128], THT[i + 2])

    def stage_a2(n):
        h, i = items[n]
        par = n % 2
        sl = h % 2
        W = WI[sl]
        tsl = slice(i * 128, (i + 1) * 128)

        def pv(e):
            for (b, o) in ((1, 512), (2, 1024)):
                for kc in range(8):
                    ins = e.matmul(pb[b][:], lhsT=HT[:, kc, tsl], rhs=W[:, kc, o:o + 512], start=(kc == 0), stop=(kc == 7))
            return ins
        S.add("pe", pv, reads=TWI[sl] + [THT[i]], writes=[pbT[1], pbT[2]])
        S.add("act", lambda e: e.activation(out=Vt[par], in_=pb[1][:], func=AF.Copy), reads=[pbT[1]], writes=[TVt[par]])
        S.add("act", lambda e: e.activation(out=SGt[par], in_=pb[2][:], func=AF.Silu), reads=[pbT[2]], writes=[TSGt[par]])

    def stage_b1(n):
        h, i = items[n]
        par = n % 2
        last = (i == NT - 1)
        qk4 = qk4_(par)

        def inmm(e):
            for dc in range(2):
                ins = e.matmul(pb[4][:, 0:128], lhsT=qk4[:, 1, dc, :], rhs=qk4[:, 0, dc, :], start=(dc == 0), stop=(dc == 1))
            return ins
        S.add("pe", inmm, reads=[TQK[par]], writes=[pbT[4]])
        S.add("dve", lambda e: e.tensor_tensor(out=innT[par], in0=pb[4][:, 0:128], in1=decT[:, h * 128:(h + 1) * 128], op=ALU.mult),
              reads=[pbT[4], Ttab], writes=[TinnT[par]])
        if not last:
            def trk(e):
                for dc in range(2):
                    ins = e.transpose(out=pbv3[:, dc * 128:(dc + 1) * 128], in_=qk4[:, 1, dc, :], identity=k.ident)
                return ins
            S.add("pe", trk, reads=[TQK[par], k.Tconst], writes=[pbT[3]])
            S.add("act", lambda e: e.activation(out=ktm[par], in_=pbv3[:, 0:256], func=AF.Copy, scale=zeta[:, h:h + 1]), reads=[pbT[3], Ttab], writes=[Tktm[par]])

    def stage_b2(n):
        h, i = items[n]
        par = n % 2
        ob = 5 + par
        first = (i == 0)
        last = (i == NT - 1)

        def omm(e):
            ins = e.matmul(pb[ob][:], lhsT=innT[par], rhs=Vt[par], start=True, stop=first)
            if not first:
                for dc in range(2):
                    ins = e.matmul(pb[ob][:], lhsT=qx[par][:, dc, :], rhs=Sb[:, dc, :], start=False, stop=(dc == 1))
            return ins
        S.add("pe", omm, reads=[TinnT[par], TVt[par]] + ([] if first else [Tqx[par], TSb[0], TSb[1]]), writes=[pbT[ob]])
        if not last:
            for dc in range(2):
                kb = 7 if dc == 0 else 0
                S.add("pe", (lambda e, dc=dc, kb=kb: e.matmul(pb[kb][:], lhsT=ktm[par][:, dc * 128:(dc + 1) * 128], rhs=Vt[par], start=True, stop=True)),
                      reads=[Tktm[par], TVt[par]], writes=[pbT[kb]])
                if first:
                    S.add("dve", (lambda e, dc=dc, kb=kb: e.tensor_copy(out=S32[:, dc, :], in_=pb[kb][:])), reads=[pbT[kb]], writes=[TS32[dc]])
                else:
                    S.add("dve", (lambda e, dc=dc, kb=kb: e.scalar_tensor_tensor(out=S32[:, dc, :], in0=S32[:, dc, :], scalar=GAM128[h], in1=pb[kb][:],
                                                                                op0=ALU.mult, op1=ALU.add)),
                          reads=[pbT[kb], TS32[dc]], writes=[TS32[dc]])
                S.add("act", (lambda e, dc=dc: e.activation(out=Sb[:, dc, :], in_=S32[:, dc, :], func=AF.Copy)), reads=[TS32[dc]], writes=[TSb[dc]])

    def stage_c1(n):
        h, i = items[n]
        par = n % 2
        ob = 5 + par
        st_, mv_ = stats[par], mv[par]
        S.add("dve", lambda e: e.bn_stats(out=st_[:, 0:6], in_=pb[ob][:]), reads=[pbT[ob]], writes=[Tst[par]])
        S.add("dve", lambda e: e.bn_aggr(out=mv_[:, 0:2], in_=st_[:, 0:6]), reads=[Tst[par]], writes=[Tst[par]])
        S.add("dve", lambda e: e.scalar_tensor_tensor(out=Y[par], in0=pb[ob][:], scalar=mv_[:, 0:1], in1=SGt[par], op0=ALU.subtract, op1=ALU.mult),
              reads=[pbT[ob], Tst[par], TSGt[par]], writes=[TY[par]])
        S.add("dve", lambda e: e.tensor_scalar(out=mv_[:, 2:3], in0=mv_[:, 1:2], scalar1=LN_EPS, scalar2=None, op0=ALU.add), reads=[Tst[par]], writes=[Trs[par]])
        S.add("pool", lambda e: e.tensor_tensor(out=mv_[:, 3:4], in0=mv_[:, 2:3], in1=k.neghalf, op=ALU.pow), reads=[Trs[par], k.Tconst], writes=[Trs[par]])

    def stage_c2(n):
        h, i = items[n]
        par = n % 2
        sl = h % 2
        mv_ = mv[par]

        def try_(e):
            for c in range(4):
                ins = e.transpose(out=pbv3[:, c * 128:(c + 1) * 128], in_=Y[par][:, c * 128:(c + 1) * 128], identity=k.ident)
            return ins
        S.add("pe", try_, reads=[TY[par], k.Tconst], writes=[pbT[3]])
        S.add("dve", lambda e: e.tensor_copy(out=YT[par], in_=r3(pbv3[:, 0:512], 128)), reads=[pbT[3]], writes=[TYT[par]])

    def stage_c2b(n):
        h, i = items[n]
        par = n % 2
        sl = h % 2
        mv_ = mv[par]

        def womm(e):
            for n2 in range(2):
                for kc in range(4):
                    ins = e.matmul(pb[1 + n2][:], lhsT=YT[par][:, kc, :], rhs=WO[sl][:, kc, n2 * 512:(n2 + 1) * 512], start=(kc == 0), stop=(kc == 3))
            return ins
        S.add("pe", womm, reads=[TYT[par], TWO[sl]], writes=[pbT[1], pbT[2]])
        for n2 in range(2):
            S.add("dve", (lambda e, n2=n2: e.scalar_tensor_tensor(out=k.X[:, i, n2 * 512:(n2 + 1) * 512], in0=pb[1 + n2][:], scalar=mv_[:, 3:4],
                                                                   in1=k.X[:, i, n2 * 512:(n2 + 1) * 512], op0=ALU.mult, op1=ALU.add)),
                  reads=[pbT[1 + n2], k.Xt[i], Trs[par]], writes=[k.Xt[i]])
        if h == 3:
            stat_q.append(lambda i=i: stat_tile(k, i, junk))
        if len(stat_q) > 2 or (stat_q and n == len(items) - 1):
            while len(stat_q) > (0 if n == len(items) - 1 else 2):
                stat_q.pop(0)()

    n_it = len(items)
    for step in range(n_it + 3):
        if 0 <= step - 3 < n_it:
            stage_c2(step - 3)
        if step < n_it:
            stage_a(step)
        if 0 <= step - 3 < n_it:
            stage_c2b(step - 3)
        if 0 <= step - 1 < n_it:
            stage_b2(step - 1)
        if 0 <= step - 2 < n_it:
            stage_c1(step - 2)
        if step < n_it:
            stage_b1(step)
            stage_a2(step)
    k.stats_ready = True


_PROG = {}


def kernel(**inputs):
    inp = {kk: np.asarray(v) for kk, v in inputs.items()}
    shared = _prep_shared(inp)
    if "nc" not in _PROG:
        _PROG["nc"] = build_program()[0]
    nc = _PROG["nc"]
    in_maps = []
    for c in range(N_CORES):
        m = dict(shared)
        m["x"] = np.ascontiguousarray(inp["x"][c * SEQ_PER_CORE:(c + 1) * SEQ_PER_CORE])
        m["p"] = np.ascontiguousarray(inp["p"][:, c * SEQ_PER_CORE:(c + 1) * SEQ_PER_CORE])
        in_maps.append(m)
    res = run_bass_kernel_spmd(nc, in_maps, core_ids=list(range(N_CORES)))
    out = np.concatenate([r["out"] for r in res.results], axis=0)
    return out.astype(np.float32, copy=False)
```

```python
import contextlib
import numpy as np
import concourse.bass as bass
import concourse.mybir as mybir
from concourse.bass_utils import run_bass_kernel_spmd

F32 = mybir.dt.float32
BF16 = mybir.dt.bfloat16
AF = mybir.ActivationFunctionType
ALU = mybir.AluOpType

D = 1024
SEQ = 2048
NT = 16
DFF = 2816
NFC = 22
RMS_EPS = 1e-6
LN_EPS = 1e-5
N_CORES = 8
SEQ_PER_CORE = 2


class T:
    __slots__ = ("name", "w", "r", "psum")

    def __init__(self, name="", psum=False):
        self.name = name
        self.w = None
        self.r = []
        self.psum = psum


class Op:
    __slots__ = ("eng", "seq", "fn", "waits", "chan", "key", "sig", "clock", "signal")

    def __init__(self, eng, seq, fn, chan):
        self.eng = eng
        self.seq = seq
        self.fn = fn
        self.chan = chan
        self.waits = []
        self.sig = None
        self.clock = None
        self.signal = False


class Sched:
    ENGS = ("pe", "act", "dve", "pool", "sp")

    def __init__(self):
        self.ops = {e: [] for e in self.ENGS}
        self.seen = {e: {} for e in self.ENGS}
        self.chan_count = {}
        self.chan_last = {}
        self.n_waits = 0

    def add(self, eng, fn, reads=(), writes=(), chan=None, extra=()):
        lst = self.ops[eng]
        op = Op(eng, len(lst), fn, chan)
        if chan is not None:
            c = self.chan_count.get(chan, 0) + 1
            self.chan_count[chan] = c
            op.key = ("c", chan)
            op.seq = c
            op.signal = True
            self.chan_last[chan] = op
        else:
            op.key = eng
        deps = {}
        for d in extra:
            deps[id(d)] = d
        for t in reads:
            if t.w is not None:
                deps[id(t.w)] = t.w
            if t.psum:
                for r in t.r:
                    if r.eng != eng:
                        deps[id(r)] = r
        for t in writes:
            if t.w is not None:
                deps[id(t.w)] = t.w
            for r in t.r:
                deps[id(r)] = r
        seen = self.seen[eng]
        for d in sorted(deps.values(), key=lambda o: -o.seq):
            if d is op:
                continue
            if d.chan is None and d.eng == "pe" and eng == "pe" and chan is None:
                continue
            need = d.seq if d.chan is not None else d.seq + 1
            if seen.get(d.key, 0) >= need:
                continue
            op.waits.append(d)
            d.signal = True
            self.n_waits += 1
            for k, v in d.clock.items():
                if seen.get(k, 0) < v:
                    seen[k] = v
        clk = dict(seen)
        clk[op.key] = op.seq if chan is not None else op.seq + 1
        op.clock = clk
        for t in reads:
            t.r.append(op)
        for t in writes:
            t.w = op
            t.r = []
        lst.append(op)
        return op

    def barrier(self):
        lasts = []
        for e in self.ENGS:
            for op in reversed(self.ops[e]):
                if op.chan is None and op.fn is not None:
                    lasts.append(op)
                    break
        lasts += list(self.chan_last.values())
        for e in self.ENGS:
            self.add(e, None, extra=lasts)

    def emit(self, nc):
        handles = {"pe": "tensor", "act": "scalar", "dve": "vector", "pool": "gpsimd", "sp": "sync"}
        with contextlib.ExitStack() as st:
            sems = {}
            for e in self.ENGS:
                sems[e] = st.enter_context(nc.semaphore("s_" + e))
            for c in self.chan_count:
                sems[("c", c)] = st.enter_context(nc.semaphore("c_" + str(c)))
            for e in self.ENGS:
                cnt = 0
                for op in self.ops[e]:
                    if op.chan is not None:
                        op.sig = 16 * op.seq
                    elif op.signal:
                        cnt += 1
                        op.sig = cnt
            block = st.enter_context(nc.Block())

            def make(e):
                def body(eng):
                    for op in self.ops[e]:
                        for d in op.waits:
                            eng.wait_ge(sems[d.key], d.sig)
                        if op.fn is None:
                            continue
                        ins = op.fn(eng)
                        if op.signal:
                            ins.then_inc(sems[op.key], 16 if op.chan is not None else 1)
                return body

            for e in self.ENGS:
                if self.ops[e]:
                    getattr(block, handles[e])(make(e))


def _const_tables():
    idx = np.arange(128)
    c = {}
    c["ident"] = np.eye(128)
    c["negL"] = -(idx[:, None] >= idx[None, :]).astype(np.float64)
    c["ones"] = np.ones((128, 128))
    c["mstrict"] = (idx[:, None] < idx[None, :]).astype(np.float64)
    c["mincl"] = (idx[:, None] <= idx[None, :]).astype(np.float64)
    lg = np.log(1.0 - 2.0 ** (-5.0 - np.arange(4)))
    dec = []
    for h in range(4):
        diff = idx[None, :] - idx[:, None]
        dec.append(np.where(diff >= 0, np.exp(diff * lg[h]), 0.0) / 16.0)
    c["decayT"] = np.concatenate(dec, axis=1)
    xi = np.exp((idx + 1.0)[None, :] * lg[:, None])
    c["xi"] = np.broadcast_to(np.repeat(xi, 2, axis=0).reshape(1, 1024), (128, 1024))
    zeta = np.exp((127 - idx)[:, None] * lg[None, :]) / 16.0
    c["zeta"] = zeta
    half = 128
    inv = 1.0 / (10000.0 ** (np.arange(half, dtype=np.float32) / half))
    ang = (np.arange(SEQ, dtype=np.float32)[None, :] * inv[:, None].astype(np.float32)).astype(np.float32)
    c["cos"] = np.cos(ang)
    c["sin"] = np.sin(ang)
    order = ["ident", "negL", "ones", "mstrict", "mincl", "decayT", "xi", "zeta", "cos", "sin"]
    offs = {}
    o = 0
    for k in order:
        offs[k] = (o, c[k].shape[1])
        o += c[k].shape[1]
    tab = np.concatenate([c[k] for k in order], axis=1).astype(np.float32)
    gam128 = [float(np.exp(128 * lg[h])) for h in range(4)]
    return tab, offs, gam128


CONST_TAB, CONST_OFF, GAM128 = _const_tables()

VFM = {}
_o = 0
for _name, _n in [("mix_g", 16), ("ffn_g", 16), ("ple_g", 16), ("conv_w", 2 * 3 * 44), ("conv_b", 2 * 44), ("sg_b", 4), ("gn_g", 16)]:
    VFM[_name] = _o
    _o += _n
NVFM = _o
VBC = {}
_o = 0
for _name, _n in [("post_g", 2048), ("final_g", 1024), ("ln_g", 512), ("ln_b", 512), ("gn_g", 2048)]:
    VBC[_name] = _o
    _o += _n
NVBC = _o


def _prep_shared(inp):
    f = np.float32
    vfm = np.zeros((128, NVFM), f)

    def fm(v):
        return np.ascontiguousarray(v.reshape(-1, 128).T)

    for l in range(2):
        vfm[:, VFM["mix_g"] + 8 * l: VFM["mix_g"] + 8 * l + 8] = fm(inp["mix_norm_g"][l])
        vfm[:, VFM["ffn_g"] + 8 * l: VFM["ffn_g"] + 8 * l + 8] = fm(inp["ffn_norm_g"][l])
        vfm[:, VFM["ple_g"] + 8 * l: VFM["ple_g"] + 8 * l + 8] = fm(inp["ple_norm_g"][l])
        for j in range(3):
            o = VFM["conv_w"] + (l * 3 + j) * 44
            vfm[:, o:o + 44] = fm(inp["ffn_conv_w"][l, j])
        o = VFM["conv_b"] + l * 44
        vfm[:, o:o + 44] = fm(inp["ffn_conv_b"][l])
    vfm[:, VFM["sg_b"]:VFM["sg_b"] + 4] = inp["sg_b"][0].T
    vfm[:, VFM["gn_g"]:VFM["gn_g"] + 16] = fm(inp["ret_gn_g"][0])
    vbc = np.zeros((128, NVBC), f)

    def bc(v):
        return np.broadcast_to(v[None, :], (128, v.shape[0]))

    for l in range(2):
        vbc[:, VBC["post_g"] + 1024 * l: VBC["post_g"] + 1024 * (l + 1)] = bc(inp["ple_post_g"][l])
    vbc[:, VBC["final_g"]:VBC["final_g"] + 1024] = bc(inp["final_norm_g"])
    vbc[:, VBC["ln_g"]:VBC["ln_g"] + 512] = bc(inp["sg_ln_g"][0])
    vbc[:, VBC["ln_b"]:VBC["ln_b"] + 512] = bc(inp["sg_ln_b"][0])
    vbc[:, VBC["gn_g"]:VBC["gn_g"] + 2048] = bc(inp["ret_gn_g"][0])
    sgwT = np.ascontiguousarray(np.transpose(inp["sg_w"][0], (2, 0, 1))).reshape(128, 512)
    shared = {
        "consts": CONST_TAB, "vfm": vfm, "vbc": vbc, "sgwT": sgwT.astype(f),
        "ab_w_in": np.ascontiguousarray(inp["ab_w_in"][0]), "ab_w_out": np.ascontiguousarray(inp["ab_w_out"][0]),
        "ret_w_in": np.ascontiguousarray(inp["ret_w_in"][0]), "ret_w_out": np.ascontiguousarray(inp["ret_w_out"][0]),
        "ffn_w_up": np.ascontiguousarray(inp["ffn_w_up"]), "ffn_w_down": np.ascontiguousarray(inp["ffn_w_down"]),
        "ple_w_gate": np.ascontiguousarray(inp["ple_w_gate"]), "ple_w_proj": np.ascontiguousarray(inp["ple_w_proj"]),
    }
    return shared


class K:
    pass


class Arena:
    def __init__(self, tensor, nbytes):
        self.t = tensor
        self.n = nbytes
        self.top = 0

    def alloc(self, nbytes):
        nbytes = (nbytes + 63) // 64 * 64
        o = self.top
        self.top += nbytes
        assert self.top <= self.n, ("arena overflow", self.top, self.n)
        return o

    def f32(self, off, n):
        return self.t[:, off // 4: off // 4 + n]

    def bf(self, off, n):
        return self.t[:, off // 4: off // 4 + (n + 1) // 2].bitcast(BF16)

    def new_f32(self, n):
        return self.f32(self.alloc(4 * n), n)

    def new_bf(self, n):
        return self.bf(self.alloc(2 * n), n)


def r3(ap, b):
    return ap.rearrange("p (a b) -> p a b", b=b)


def build_program(n_seq=SEQ_PER_CORE, stages=("mix0", "ffn0", "ple0", "mix1", "ffn1", "ple1")):
    nc = bass.Bass("TRN2", target_bir_lowering=False)
    k = K()
    k.nc = nc
    dt = lambda name, shape, kind="ExternalInput": nc.dram_tensor(name, shape, F32, kind=kind).ap()
    k.x_d = dt("x", [n_seq, SEQ, D])
    k.p_d = dt("p", [2, n_seq, SEQ, 256])
    k.consts_d = dt("consts", list(CONST_TAB.shape))
    k.vfm_d = dt("vfm", [128, NVFM])
    k.vbc_d = dt("vbc", [128, NVBC])
    k.sgwT_d = dt("sgwT", [128, 512])
    k.ab_w_in = dt("ab_w_in", [1024, 2560])
    k.ab_w_out = dt("ab_w_out", [1024, 1024])
    k.ret_w_in = dt("ret_w_in", [1024, 6144])
    k.ret_w_out = dt("ret_w_out", [2048, 1024])
    k.ffn_w_up = dt("ffn_w_up", [2, 1024, 5632])
    k.ffn_w_down = dt("ffn_w_down", [2, 2816, 1024])
    k.ple_w_gate = dt("ple_w_gate", [2, 1024, 1024])
    k.ple_w_proj = dt("ple_w_proj", [2, 256, 1024])
    k.out_d = dt("out", [n_seq, SEQ, D], kind="ExternalOutput")
    k.n_seq = n_seq
    k.stages = stages

    with contextlib.ExitStack() as st:
        ARENA_BYTES = 212736
        at = st.enter_context(nc.sbuf_tensor("arena", [128, ARENA_BYTES // 4], F32))
        k.A = Arena(at, ARENA_BYTES)
        k.pbig = st.enter_context(nc.psum_tensor("pbig", [128, 4096], F32))
        k.pb = [k.pbig[:, i * 512:(i + 1) * 512] for i in range(8)]
        k.S = Sched()
        _emit_all(k)
        k.S.emit(nc)
    k.nc = nc
    return nc, k


def cslice(name):
    o, n = CONST_OFF[name]
    return slice(o, o + n)


def _emit_all(k):
    S, A, nc = k.S, k.A, k.nc
    k.X = r3(A.new_f32(NT * D), D)
    k.Xt = [T("X%d" % i) for i in range(NT)]
    k.ident = A.new_bf(128)
    k.negL = A.new_bf(128)
    k.ones = A.new_bf(128)
    k.mstrict = A.new_bf(128)
    k.vfm = A.new_f32(NVFM)
    k.ss = A.new_f32(16)
    k.lnv = A.new_f32(16)
    k.rstd = A.new_f32(16)
    k.neghalf = A.new_f32(16)[:, 0:1]
    k.Tconst = T("const")
    k.eps_ln = LN_EPS
    k.one_b = 1.0
    k.epsr = RMS_EPS
    k.Tss = [T() for _ in range(16)]
    k.Tstat = T()
    k.persist_top = A.top
    k.pbT = [T("pb%d" % i, psum=True) for i in range(8)]
    def load_x_tile(s, i):
        S.add("sp", (lambda e: e.dma_start(out=k.X[:, i, :], in_=k.x_d[s, i * 128:(i + 1) * 128, :])), writes=[k.Xt[i]], chan="xin%d" % i)
    k.load_x_tile = load_x_tile
    for i in range(NT):
        load_x_tile(0, i)
    for name, dst in [("ident", k.ident), ("negL", k.negL), ("ones", k.ones), ("mstrict", k.mstrict)]:
        S.add("pool", (lambda e, dst=dst, name=name: e.dma_start(out=dst, in_=k.consts_d[:, cslice(name)])),
              writes=[k.Tconst], chan="const")
    S.add("sp", lambda e: e.dma_start(out=k.vfm, in_=k.vfm_d), writes=[k.Tconst], chan="constsp")
    S.add("pool", lambda e: e.memset(k.neghalf, -0.5), writes=[k.Tconst])

    out_ops = []
    def load_x_tile(s, i):
        S.add("sp", (lambda e: e.dma_start(out=k.X[:, i, :], in_=k.x_d[s, i * 128:(i + 1) * 128, :])), writes=[k.Xt[i]], chan="xin%d" % i)
    k.load_x_tile = load_x_tile
    for s in range(k.n_seq):
        k.next_seq = s + 1 if (s + 1 < k.n_seq and "ple1" in k.stages) else None
        if s > 0 and "ple1" not in k.stages:
            for i in range(NT):
                load_x_tile(s, i)
        if "mix0" in k.stages:
            phase_mix0(k, s)
        if "ffn0" in k.stages:
            phase_ffn(k, s, 0)
        if "ple0" in k.stages:
            phase_ple(k, s, 0, final=False)
        if "mix1" in k.stages:
            phase_mix1(k, s)
        if "ffn1" in k.stages:
            phase_ffn(k, s, 1)
        if "ple1" in k.stages:
            out_ops += phase_ple(k, s, 1, final=True)
        else:
            out_ops += phase_dump(k, s)
        S.barrier()
    S.add("sp", None, extra=out_ops)


def phase_dump(k, s):
    ops = []
    for i in range(NT):
        ops.append(k.S.add("sp", (lambda e, i=i: e.dma_start(out=k.out_d[s, i * 128:(i + 1) * 128, :], in_=k.X[:, i, :])),
                           reads=[k.Xt[i]], chan="xout"))
    return ops


def norm_hT(k, i, gcol, hn, hnT, tb, dst, dstT, slot):
    S = k.S
    X = k.X
    ss = k.ss[:, slot:slot + 1]
    lnv = k.lnv[:, slot:slot + 1]
    rstd = k.rstd[:, slot:slot + 1]
    tss = k.Tss[slot]
    S.add("act", lambda e: e.activation(out=hn, in_=X[:, i, :], func=AF.Square, accum_out=ss),
          reads=[k.Xt[i]], writes=[hnT, tss])
    S.add("act", lambda e: e.activation(out=lnv, in_=ss, func=AF.Ln, scale=1.0 / D, bias=k.epsr), reads=[tss, k.Tconst], writes=[tss])
    S.add("act", lambda e: e.activation(out=rstd, in_=lnv, func=AF.Exp, scale=-0.5), reads=[tss], writes=[tss])
    S.add("act", lambda e: e.activation(out=hn, in_=X[:, i, :], func=AF.Copy, scale=rstd),
          reads=[k.Xt[i], tss], writes=[hnT])
    pbv = k.pb[tb][:].bitcast(BF16)

    def tr(e):
        for c in range(8):
            ins = e.transpose(out=pbv[:, c * 128:(c + 1) * 128], in_=hn[:, c * 128:(c + 1) * 128], identity=k.ident)
        return ins
    S.add("pe", tr, reads=[hnT, k.Tconst], writes=[k.pbT[tb]])
    g = k.vfm[:, gcol:gcol + 8].unsqueeze(2).to_broadcast([128, 8, 128])
    S.add("dve", lambda e: e.tensor_tensor(out=dst, in0=r3(pbv, 128), in1=g, op=ALU.mult),
          reads=[k.pbT[tb], k.Tconst], writes=[dstT])


class JunkRing:
    def __init__(self, bufs):
        self.bufs = bufs
        self.n = 0

    def nxt(self):
        b = self.bufs[self.n % len(self.bufs)]
        self.n += 1
        return b


def norm_stats_all(k, ring):
    S = k.S
    if not getattr(k, "stats_ready", False):
        for i in range(NT):
            jb, jt = ring.nxt()
            S.add("act", (lambda e, i=i, jb=jb: e.activation(out=jb, in_=k.X[:, i, :], func=AF.Square, accum_out=k.ss[:, i:i + 1])),
                  reads=[k.Xt[i]] + ([k.Tstat] if i == 0 else []), writes=[k.Tss[i], jt])
    k.stats_ready = False
    S.add("act", lambda e: e.activation(out=k.lnv[:, 0:16], in_=k.ss[:, 0:16], func=AF.Ln, scale=1.0 / D, bias=k.epsr), reads=list(k.Tss), writes=[k.Tstat])
    S.add("act", lambda e: e.activation(out=k.rstd[:, 0:16], in_=k.lnv[:, 0:16], func=AF.Exp, scale=-0.5), reads=[k.Tstat], writes=[k.Tstat])


def stat_tile(k, i, ring):
    jb, jt = ring.nxt()
    k.S.add("act", lambda e: e.activation(out=jb, in_=k.X[:, i, :], func=AF.Square, accum_out=k.ss[:, i:i + 1]),
            reads=[k.Xt[i]], writes=[k.Tss[i], jt])


def norm_hT2(k, i, gcol, hn, hnT, tb, dst, dstT):
    S = k.S
    S.add("act", lambda e: e.activation(out=hn, in_=k.X[:, i, :], func=AF.Copy, scale=k.rstd[:, i:i + 1]),
          reads=[k.Xt[i], k.Tstat], writes=[hnT])
    pbv = k.pb[tb][:].bitcast(BF16)

    def tr(e):
        for c in range(8):
            ins = e.transpose(out=pbv[:, c * 128:(c + 1) * 128], in_=hn[:, c * 128:(c + 1) * 128], identity=k.ident)
        return ins
    S.add("pe", tr, reads=[hnT, k.Tconst], writes=[k.pbT[tb]])
    g = k.vfm[:, gcol:gcol + 8].unsqueeze(2).to_broadcast([128, 8, 128])
    S.add("dve", lambda e: e.tensor_tensor(out=dst, in0=r3(pbv, 128), in1=g, op=ALU.mult),
          reads=[k.pbT[tb], k.Tconst], writes=[dstT])


def load_w_slabs(k, dst, src, ncols, slab, chan, Ts, eng="pool"):
    srcv = src.rearrange("(kc p) n -> p kc n", p=128)
    for j, c0 in enumerate(range(0, ncols, slab)):
        c1 = min(ncols, c0 + slab)
        k.S.add(eng, (lambda e, c0=c0, c1=c1: e.dma_start(out=dst[:, :, c0:c1], in_=srcv[:, :, c0:c1])),
                writes=[Ts[j]], chan=chan)


def phase_mix0_old(k, s):
    S, A, nc = k.S, k.A, k.nc
    S.barrier()
    A.top = k.persist_top
    pb, pbT = k.pb, k.pbT
    Win = r3(A.new_bf(8 * 2560), 2560)
    Wout = r3(A.new_bf(8 * 1024), 1024)
    KT = r3(A.new_bf(4 * 2048), 2048)
    V = r3(A.new_bf(16 * 512), 512)
    SGW = A.new_bf(512)
    sgw32 = A.new_f32(512)
    mincl = A.new_f32(128)
    lng = A.new_f32(512)
    lnb = A.new_f32(512)
    hn = A.new_bf(1024)
    hT = [r3(A.new_bf(1024), 128) for _ in range(2)]
    QA = [r3(A.new_bf(512), 128) for _ in range(2)]
    QB = [r3(A.new_bf(512), 128) for _ in range(2)]
    ug = A.new_f32(512)
    vg = A.new_f32(512)
    vn32 = A.new_f32(512)
    vnb = A.new_bf(512)
    stats = A.new_f32(24)
    mv = A.new_f32(8)
    lrs = A.new_f32(4)
    lrl = A.new_f32(4)
    nmr = A.new_f32(4)
    CAT = A.new_bf(1024)
    CATT = r3(A.new_bf(1024), 128)
    E32 = [A.new_f32(512) for _ in range(2)]
    SP = [A.new_bf(512) for _ in range(2)]
    TMP = [A.new_f32(512) for _ in range(2)]
    AW = [A.new_bf(512) for _ in range(2)]
    TOT = [A.new_f32(512) for _ in range(2)]
    TWin = [T() for _ in range(5)]
    TWout = [T() for _ in range(2)]
    Tmisc = T()
    TSGW = T()
    Thn = T()
    ThT = [T(), T()]
    TQ = [T(), T()]
    TKT = [T() for _ in range(NT)]
    TV = [T() for _ in range(NT)]
    Tug, Tvg, Tvn32, Tvnb, Tst = T(), T(), T(), T(), T()
    TCATa, TCATb, TCATT = T(), T(), T()
    TE, TSP, TTMP, TAW = [T(), T()], [T(), T()], [T(), T()], [T(), T()]
    TTOT = [T(), T()]
    k.Tss = [T() for _ in range(16)]
    k.epsr = RMS_EPS

    load_w_slabs(k, Win, k.ab_w_in, 2560, 512, "w0", TWin)
    load_w_slabs(k, Wout, k.ab_w_out, 1024, 512, "w1", TWout)
    S.add("sp", lambda e: e.dma_start(out=sgw32, in_=k.sgwT_d), writes=[Tmisc], chan="misc")
    S.add("sp", lambda e: e.dma_start(out=mincl, in_=k.consts_d[:, cslice("mincl")]), writes=[Tmisc], chan="misc")
    S.add("sp", lambda e: e.dma_start(out=lng, in_=k.vbc_d[:, VBC["ln_g"]:VBC["ln_g"] + 512]), writes=[Tmisc], chan="misc")
    S.add("sp", lambda e: e.dma_start(out=lnb, in_=k.vbc_d[:, VBC["ln_b"]:VBC["ln_b"] + 512]), writes=[Tmisc], chan="misc")
    S.add("dve", lambda e: e.tensor_tensor(out=r3(SGW, 128), in0=r3(sgw32, 128),
                                           in1=mincl.unsqueeze(1).to_broadcast([128, 4, 128]), op=ALU.mult),
          reads=[Tmisc], writes=[TSGW])
    for b in range(2):
        S.add("dve", (lambda e, b=b: e.memset(QA[b][64:128, :, :], 0.0)), writes=[TQ[b]])
        S.add("dve", (lambda e, b=b: e.memset(QB[b][0:64, :, :], 0.0)), writes=[TQ[b]])
    sgb = k.vfm[:, VFM["sg_b"]:VFM["sg_b"] + 4]
    gcol = VFM["mix_g"]
    mstr_b = k.mstrict.unsqueeze(1).to_broadcast([128, 4, 128])

    norm_hT(k, 0, gcol, hn, Thn, 7, hT[0], ThT[0], 0)
    for i in range(NT):
        cur = i % 2
        h_ = hT[cur]
        ts = slice(i * 128, (i + 1) * 128)

        def proj_uv(e, h_=h_):
            for (b, c0) in ((0, 0), (1, 512)):
                for kc in range(8):
                    ins = e.matmul(pb[b][:], lhsT=h_[:, kc, :], rhs=Win[:, kc, c0:c0 + 512], start=(kc == 0), stop=(kc == 7))
            return ins
        S.add("pe", proj_uv, reads=[ThT[cur]] + TWin, writes=[pbT[0], pbT[1]])

        def proj_qk(e, h_=h_):
            for (b, c0) in ((2, 1024), (3, 1536)):
                for c in range(4):
                    for kc in range(8):
                        ins = e.matmul(pb[b][:, c * 128:(c + 1) * 128], lhsT=Win[:, kc, c0 + c * 128:c0 + (c + 1) * 128],
                                       rhs=h_[:, kc, :], start=(kc == 0), stop=(kc == 7))
            return ins
        S.add("pe", proj_qk, reads=[ThT[cur]] + TWin, writes=[pbT[2], pbT[3]])

        def proj_v(e, h_=h_):
            for kc in range(8):
                ins = e.matmul(pb[4][:], lhsT=h_[:, kc, :], rhs=Win[:, kc, 2048:2560], start=(kc == 0), stop=(kc == 7))
            return ins
        S.add("pe", proj_v, reads=[ThT[cur]] + TWin, writes=[pbT[4]])

        S.add("act", lambda e: e.activation(out=ug, in_=pb[0][:], func=AF.Gelu), reads=[pbT[0]], writes=[Tug])
        S.add("act", lambda e: e.activation(out=vg, in_=pb[1][:], func=AF.Gelu), reads=[pbT[1]], writes=[Tvg])
        S.add("act", (lambda e, cur=cur: e.activation(out=QA[cur][0:64, :, :], in_=r3(pb[2][0:64, :], 128), func=AF.Copy, scale=0.125)),
              reads=[pbT[2]], writes=[TQ[cur]])
        S.add("act", (lambda e, cur=cur: e.activation(out=QB[cur][64:128, :, :], in_=r3(pb[2][64:128, :], 128), func=AF.Copy, scale=0.125)),
              reads=[pbT[2]], writes=[TQ[cur]])
        S.add("dve", (lambda e, ts=ts: e.tensor_copy(out=KT[:, :, ts], in_=r3(pb[3][:], 128))), reads=[pbT[3]], writes=[TKT[i]])
        S.add("dve", (lambda e, i=i: e.tensor_copy(out=V[:, i, :], in_=pb[4][:])), reads=[pbT[4]], writes=[TV[i]])

        def bns(e):
            for g in range(4):
                ins = e.bn_stats(out=stats[:, g * 6:(g + 1) * 6], in_=vg[:, g * 128:(g + 1) * 128])
            return ins
        S.add("dve", bns, reads=[Tvg], writes=[Tst])

        def bna(e):
            for g in range(4):
                ins = e.bn_aggr(out=mv[:, 2 * g:2 * g + 2], in_=stats[:, g * 6:(g + 1) * 6])
            return ins
        S.add("dve", bna, reads=[Tst], writes=[Tst])
        mvv = mv.rearrange("p (g two) -> p g two", two=2)
        S.add("act", lambda e: e.activation(out=lrl, in_=mvv[:, :, 1], func=AF.Ln, bias=k.eps_ln), reads=[Tst, k.Tconst], writes=[Tst])
        S.add("act", lambda e: e.activation(out=lrs, in_=lrl, func=AF.Exp, scale=-0.5), reads=[Tst], writes=[Tst])
        S.add("dve", lambda e: e.scalar_tensor_tensor(out=nmr, in0=mvv[:, :, 0], scalar=-1.0, in1=lrs, op0=ALU.mult, op1=ALU.mult),
              reads=[Tst], writes=[Tst])

        def nrm(e):
            for g in range(4):
                ins = e.activation(out=vn32[:, g * 128:(g + 1) * 128], in_=vg[:, g * 128:(g + 1) * 128], func=AF.Identity,
                                   scale=lrs[:, g:g + 1], bias=nmr[:, g:g + 1])
            return ins
        S.add("act", nrm, reads=[Tvg, Tst], writes=[Tvn32])
        S.add("pool", lambda e: e.tensor_tensor(out=vn32, in0=vn32, in1=lng, op=ALU.mult), reads=[Tvn32, Tmisc], writes=[Tvn32])
        S.add("pool", lambda e: e.tensor_tensor(out=vnb, in0=vn32, in1=lnb, op=ALU.add), reads=[Tvn32, Tmisc], writes=[Tvnb])

        def mixmm(e):
            for g in range(4):
                ins = e.matmul(pb[5][:, g * 128:(g + 1) * 128], lhsT=SGW[:, g * 128:(g + 1) * 128], rhs=vnb[:, g * 128:(g + 1) * 128],
                               start=True, stop=True)
            return ins
        S.add("pe", mixmm, reads=[TSGW, Tvnb], writes=[pbT[5]])

        def aout(e):
            for g in range(4):
                ins = e.scalar_tensor_tensor(out=CAT[:, g * 128:(g + 1) * 128], in0=pb[5][:, g * 128:(g + 1) * 128],
                                             scalar=sgb[:, g:g + 1], in1=ug[:, g * 128:(g + 1) * 128], op0=ALU.add, op1=ALU.mult)
            return ins
        S.add("dve", aout, reads=[pbT[5], Tug, k.Tconst], writes=[TCATa])

        if i + 1 < NT:
            norm_hT(k, i + 1, gcol, hn, Thn, 7, hT[1 - cur], ThT[1 - cur], (i + 1) % 16)

        items = [(j, hf) for j in range(i, -1, -1) for hf in range(2)]
        n_it = len(items)

        def stage_a(n, cur=cur, items=items, i=i):
            j, hf = items[n]
            par = n % 2
            zb = par
            ks = slice(j * 128, (j + 1) * 128)

            def zmm(e):
                for hh in range(4):
                    h = 4 * hf + hh
                    c = h // 2
                    q = QA[cur] if h % 2 == 0 else QB[cur]
                    ins = e.matmul(pb[zb][:, hh * 128:(hh + 1) * 128], lhsT=KT[:, c, ks], rhs=q[:, c, :], start=True, stop=True)
                return ins
            S.add("pe", zmm, reads=[TKT[j], TQ[cur]], writes=[pbT[zb]])
            S.add("act", lambda e: e.activation(out=E32[par], in_=pb[zb][:], func=AF.Exp), reads=[pbT[zb]], writes=[TE[par]])
            S.add("act", lambda e: e.activation(out=SP[par], in_=E32[par], func=AF.Ln, bias=k.one_b), reads=[TE[par], k.Tconst], writes=[TSP[par]])
            if j == i:
                S.add("dve", lambda e: e.tensor_tensor(out=r3(SP[par], 128), in0=r3(SP[par], 128), in1=mstr_b, op=ALU.mult),
                      reads=[TSP[par], k.Tconst], writes=[TSP[par]])

        def stage_b(n, cur=cur, items=items, i=i):
            j, hf = items[n]
            par = n % 2
            wb = 2 + par
            tb = 4 + par
            ks = slice(j * 128, (j + 1) * 128)

            def wmm(e):
                for hh in range(4):
                    h = 4 * hf + hh
                    c = h // 2
                    q = QA[cur] if h % 2 == 0 else QB[cur]
                    e.matmul(pb[wb][:, hh * 128:(hh + 1) * 128], lhsT=KT[:, c, ks], rhs=q[:, c, :], start=(hh == 0), stop=False,
                             skip_group_check=True)
                ins = e.matmul(pb[wb][:], lhsT=k.negL, rhs=SP[par], start=False, stop=True, skip_group_check=True)
                return ins
            S.add("pe", wmm, reads=[TKT[j], TQ[cur], TSP[par], k.Tconst], writes=[pbT[wb]])
            if j > 0:
                S.add("pe", lambda e: e.matmul(pb[tb][:], lhsT=k.ones, rhs=SP[par], start=True, stop=True),
                      reads=[TSP[par], k.Tconst], writes=[pbT[tb]])
            if j < i:
                S.add("dve", lambda e: e.tensor_tensor(out=TMP[par], in0=pb[wb][:], in1=TOT[hf], op=ALU.subtract),
                      reads=[pbT[wb], TTOT[hf]], writes=[TTMP[par]])
                S.add("act", lambda e: e.activation(out=AW[par], in_=TMP[par], func=AF.Exp), reads=[TTMP[par]], writes=[TAW[par]])
            else:
                S.add("act", lambda e: e.activation(out=AW[par], in_=pb[wb][:], func=AF.Exp), reads=[pbT[wb]], writes=[TAW[par]])
                S.add("dve", lambda e: e.tensor_tensor(out=r3(AW[par], 128), in0=r3(AW[par], 128), in1=mstr_b, op=ALU.mult),
                      reads=[TAW[par], k.Tconst], writes=[TAW[par]])
            if j > 0:
                if j == i:
                    S.add("dve", lambda e: e.tensor_copy(out=TOT[hf], in_=pb[tb][:]), reads=[pbT[tb]], writes=[TTOT[hf]])
                else:
                    S.add("dve", lambda e: e.tensor_tensor(out=TOT[hf], in0=pb[tb][:], in1=TOT[hf], op=ALU.add),
                          reads=[pbT[tb], TTOT[hf]], writes=[TTOT[hf]])

        def stage_c(n, cur=cur, items=items, n_it=n_it):
            j, hf = items[n]
            par = n % 2

            def av(e):
                for hh in range(4):
                    h = 4 * hf + hh
                    ins = e.matmul(pb[6][:, h * 64:(h + 1) * 64], lhsT=AW[par][:, hh * 128:(hh + 1) * 128],
                                   rhs=V[:, j, h * 64:(h + 1) * 64], start=(n == 0 and hh == 0), stop=(n == n_it - 1 and hh == 3),
                                   skip_group_check=True)
                return ins
            S.add("pe", av, reads=[TAW[par], TV[j]], writes=[pbT[6]])

        for step in range(n_it + 2):
            if step < n_it:
                stage_a(step)
            if 0 <= step - 1 < n_it:
                stage_b(step - 1)
            if 0 <= step - 2 < n_it:
                stage_c(step - 2)
        S.add("act", lambda e: e.activation(out=CAT[:, 512:1024], in_=pb[6][:], func=AF.Copy), reads=[pbT[6]], writes=[TCATb])

        pbv = pb[7][:].bitcast(BF16)

        def trc(e):
            for c in range(8):
                ins = e.transpose(out=pbv[:, c * 128:(c + 1) * 128], in_=CAT[:, c * 128:(c + 1) * 128], identity=k.ident)
            return ins
        S.add("pe", trc, reads=[TCATa, TCATb, k.Tconst], writes=[pbT[7]])
        S.add("dve", lambda e: e.tensor_copy(out=CATT, in_=r3(pbv, 128)), reads=[pbT[7]], writes=[TCATT])

        def womm(e):
            for n2 in range(2):
                for kc in range(8):
                    ins = e.matmul(pb[n2][:], lhsT=CATT[:, kc, :], rhs=Wout[:, kc, n2 * 512:(n2 + 1) * 512], start=(kc == 0), stop=(kc == 7))
            return ins
        S.add("pe", womm, reads=[TCATT] + TWout, writes=[pbT[0], pbT[1]])
        for n2 in range(2):
            S.add("dve", (lambda e, n2=n2, i=i: e.tensor_tensor(out=k.X[:, i, n2 * 512:(n2 + 1) * 512], in0=pb[n2][:],
                                                                 in1=k.X[:, i, n2 * 512:(n2 + 1) * 512], op=ALU.add)),
                  reads=[pbT[n2], k.Xt[i]], writes=[k.Xt[i]])


def phase_mix0(k, s):
    S, A = k.S, k.A
    if s > 0:
        S.barrier()
    A.top = k.persist_top
    pb, pbT = k.pb, k.pbT
    KT = r3(A.new_bf(4 * 2048), 2048)
    V = r3(A.new_bf(16 * 512), 512)
    QT = r3(A.new_bf(4 * 2048), 2048)
    CATa = r3(A.new_bf(16 * 512), 512)
    TKT = [T() for _ in range(NT)]
    TV = [T() for _ in range(NT)]
    TQT = [T() for _ in range(NT)]
    TCATa = [T() for _ in range(NT)]
    sub_top = A.top
    Win = r3(A.new_bf(8 * 2560), 2560)
    SGW = A.new_bf(512)
    sgw32 = A.new_f32(512)
    mincl = A.new_f32(128)
    lng = A.new_f32(512)
    lnb = A.new_f32(512)
    hn = [A.new_bf(1024) for _ in range(2)]
    junk = A.new_bf(1024)
    junk2 = A.new_bf(1024)
    hT = [r3(A.new_bf(1024), 128) for _ in range(2)]
    ug = [A.new_f32(512) for _ in range(2)]
    vg = [A.new_f32(512) for _ in range(2)]
    vn32 = [A.new_f32(512) for _ in range(2)]
    vnb = [A.new_bf(512) for _ in range(2)]
    stats = [A.new_f32(24) for _ in range(2)]
    mv = [A.new_f32(8) for _ in range(2)]
    sm = [A.new_f32(12) for _ in range(2)]
    TWin = [T() for _ in range(5)]
    Tmisc, TSGW, Tjunk = T(), T(), T()
    Thn, ThT, Tug, Tvg, Tvn32, Tvnb, Tst = ([T(), T()] for _ in range(7))
    load_w_slabs(k, Win, k.ab_w_in, 2560, 512, "w0", TWin)
    S.add("sp", lambda e: e.dma_start(out=sgw32, in_=k.sgwT_d), writes=[Tmisc], chan="misc")
    S.add("sp", lambda e: e.dma_start(out=mincl, in_=k.consts_d[:, cslice("mincl")]), writes=[Tmisc], chan="misc")
    S.add("sp", lambda e: e.dma_start(out=lng, in_=k.vbc_d[:, VBC["ln_g"]:VBC["ln_g"] + 512]), writes=[Tmisc], chan="misc")
    S.add("sp", lambda e: e.dma_start(out=lnb, in_=k.vbc_d[:, VBC["ln_b"]:VBC["ln_b"] + 512]), writes=[Tmisc], chan="misc")
    S.add("dve", lambda e: e.tensor_tensor(out=r3(SGW, 128), in0=r3(sgw32, 128),
                                           in1=mincl.unsqueeze(1).to_broadcast([128, 4, 128]), op=ALU.mult),
          reads=[Tmisc], writes=[TSGW])
    sgb = k.vfm[:, VFM["sg_b"]:VFM["sg_b"] + 4]
    gcol = VFM["mix_g"]
    ring = JunkRing([(junk, Tjunk), (junk2, T())])
    norm_stats_all(k, ring)
    tails = []
    for i in range(NT):
        par = i % 2
        ts = slice(i * 128, (i + 1) * 128)
        if i == 0:
            norm_hT2(k, 0, gcol, hn[0], Thn[0], 7, hT[0], ThT[0])
        h_ = hT[par]

        def proj_uv(e, h_=h_):
            for (b, c0) in ((0, 0), (1, 512)):
                for kc in range(8):
                    ins = e.matmul(pb[b][:], lhsT=h_[:, kc, :], rhs=Win[:, kc, c0:c0 + 512], start=(kc == 0), stop=(kc == 7))
            return ins
        S.add("pe", proj_uv, reads=[ThT[par]] + TWin, writes=[pbT[0], pbT[1]])

        def proj_qk(e, h_=h_):
            for (b, c0) in ((2, 1024), (3, 1536)):
                for c in range(4):
                    for kc in range(8):
                        ins = e.matmul(pb[b][:, c * 128:(c + 1) * 128], lhsT=Win[:, kc, c0 + c * 128:c0 + (c + 1) * 128],
                                       rhs=h_[:, kc, :], start=(kc == 0), stop=(kc == 7))
            return ins
        S.add("pe", proj_qk, reads=[ThT[par]] + TWin, writes=[pbT[2], pbT[3]])

        def proj_v(e, h_=h_):
            for kc in range(8):
                ins = e.matmul(pb[4][:], lhsT=h_[:, kc, :], rhs=Win[:, kc, 2048:2560], start=(kc == 0), stop=(kc == 7))
            return ins
        S.add("pe", proj_v, reads=[ThT[par]] + TWin, writes=[pbT[4]])
        tail_dve = None
        if tails:
            tail_dve = tails.pop(0)()
        if i + 1 < NT:
            norm_hT2(k, i + 1, gcol, hn[1 - par], Thn[1 - par], 7, hT[1 - par], ThT[1 - par])
        if tail_dve is not None:
            tail_dve()
        S.add("act", (lambda e, par=par: e.activation(out=ug[par], in_=pb[0][:], func=AF.Gelu)), reads=[pbT[0]], writes=[Tug[par]])
        S.add("act", (lambda e, par=par: e.activation(out=vg[par], in_=pb[1][:], func=AF.Gelu)), reads=[pbT[1]], writes=[Tvg[par]])
        S.add("act", (lambda e, ts=ts: e.activation(out=QT[:, :, ts], in_=r3(pb[2][:], 128), func=AF.Copy, scale=0.125)), reads=[pbT[2]], writes=[TQT[i]])
        S.add("dve", (lambda e, ts=ts: e.tensor_copy(out=KT[:, :, ts], in_=r3(pb[3][:], 128))), reads=[pbT[3]], writes=[TKT[i]])
        S.add("dve", (lambda e, i=i: e.tensor_copy(out=V[:, i, :], in_=pb[4][:])), reads=[pbT[4]], writes=[TV[i]])
        st_, mv_, sm_ = stats[par], mv[par], sm[par]
        vg_, vn_, vb_, ug_ = vg[par], vn32[par], vnb[par], ug[par]

        def bns(e, st_=st_, vg_=vg_):
            for g in range(4):
                ins = e.bn_stats(out=st_[:, g * 6:(g + 1) * 6], in_=vg_[:, g * 128:(g + 1) * 128])
            return ins
        S.add("dve", bns, reads=[Tvg[par]], writes=[Tst[par]])

        def bna(e, st_=st_, mv_=mv_):
            for g in range(4):
                ins = e.bn_aggr(out=mv_[:, 2 * g:2 * g + 2], in_=st_[:, g * 6:(g + 1) * 6])
            return ins
        S.add("dve", bna, reads=[Tst[par]], writes=[Tst[par]])
        mvv = mv_.rearrange("p (g two) -> p g two", two=2)
        S.add("dve", (lambda e, sm_=sm_, mvv=mvv: e.tensor_scalar(out=sm_[:, 0:4], in0=mvv[:, :, 1], scalar1=LN_EPS, scalar2=None, op0=ALU.add)),
              reads=[Tst[par]], writes=[Tst[par]])
        S.add("pool", (lambda e, sm_=sm_: e.tensor_tensor(out=sm_[:, 4:8], in0=sm_[:, 0:4], in1=k.neghalf.to_broadcast([128, 4]), op=ALU.pow)),
              reads=[Tst[par], k.Tconst], writes=[Tst[par]])
        S.add("dve", (lambda e, sm_=sm_, mvv=mvv: e.scalar_tensor_tensor(out=sm_[:, 8:12], in0=mvv[:, :, 0], scalar=-1.0, in1=sm_[:, 4:8], op0=ALU.mult, op1=ALU.mult)),
              reads=[Tst[par]], writes=[Tst[par]])

        def nrm(e, sm_=sm_, vg_=vg_, vn_=vn_):
            for g in range(4):
                ins = e.tensor_scalar(out=vn_[:, g * 128:(g + 1) * 128], in0=vg_[:, g * 128:(g + 1) * 128], scalar1=sm_[:, 4 + g:5 + g],
                                      scalar2=sm_[:, 8 + g:9 + g], op0=ALU.mult, op1=ALU.add)
            return ins
        S.add("dve", nrm, reads=[Tvg[par], Tst[par]], writes=[Tvn32[par]])
        S.add("pool", (lambda e, vn_=vn_: e.tensor_tensor(out=vn_, in0=vn_, in1=lng, op=ALU.mult)), reads=[Tvn32[par], Tmisc], writes=[Tvn32[par]])
        S.add("pool", (lambda e, vn_=vn_, vb_=vb_: e.tensor_tensor(out=vb_, in0=vn_, in1=lnb, op=ALU.add)), reads=[Tvn32[par], Tmisc], writes=[Tvnb[par]])

        def tail(i=i, par=par, vb_=vb_, ug_=ug_):
            def mixmm(e):
                for g in range(4):
                    ins = e.matmul(pb[5][:, g * 128:(g + 1) * 128], lhsT=SGW[:, g * 128:(g + 1) * 128], rhs=vb_[:, g * 128:(g + 1) * 128],
                                   start=True, stop=True)
                return ins
            S.add("pe", mixmm, reads=[TSGW, Tvnb[par]], writes=[pbT[5]])

            def aout(e):
                for g in range(4):
                    ins = e.scalar_tensor_tensor(out=CATa[:, i, g * 128:(g + 1) * 128], in0=pb[5][:, g * 128:(g + 1) * 128],
                                                 scalar=sgb[:, g:g + 1], in1=ug_[:, g * 128:(g + 1) * 128], op0=ALU.add, op1=ALU.mult)
                return ins
            return lambda: S.add("dve", aout, reads=[pbT[5], Tug[par], k.Tconst], writes=[TCATa[i]])
        tails.append(tail)

    while tails:
        tails.pop(0)()()
    S.barrier()
    A.top = sub_top
    Wout = r3(A.new_bf(8 * 1024), 1024)
    E32 = [A.new_f32(1024) for _ in range(2)]
    SP = [A.new_bf(1024) for _ in range(2)]
    AW = [A.new_bf(1024) for _ in range(2)]
    R = [A.new_bf(1024) for _ in range(2)]
    QA = [r3(A.new_bf(512), 128) for _ in range(2)]
    QB = [r3(A.new_bf(512), 128) for _ in range(2)]
    CATb = [A.new_bf(512) for _ in range(2)]
    CATT = [r3(A.new_bf(1024), 128) for _ in range(2)]
    TWout = [T(), T()]
    TE, TSP, TAW, TR, TQ, TCATb, TCATT = ([T(), T()] for _ in range(7))
    load_w_slabs(k, Wout, k.ab_w_out, 1024, 512, "w1", TWout)
    for b in range(2):
        S.add("pool", (lambda e, b=b: e.memset(QA[b][64:128, :, :], 0.0)), writes=[TQ[b]])
        S.add("pool", (lambda e, b=b: e.memset(QB[b][0:64, :, :], 0.0)), writes=[TQ[b]])
    mstr8 = k.mstrict.unsqueeze(1).to_broadcast([128, 8, 128])
    items = [(i, j) for i in range(NT) for j in range(i, -1, -1)]
    N = len(items)
    pbv5 = pb[5][:].bitcast(BF16)

    def build_q(i):
        tp = i % 2
        ts = slice(i * 128, (i + 1) * 128)
        S.add("pool", lambda e: e.tensor_copy(out=QA[tp][0:64, :, :], in_=QT[0:64, :, ts]), reads=[TQT[i]], writes=[TQ[tp]])
        S.add("pool", lambda e: e.tensor_copy(out=QB[tp][64:128, :, :], in_=QT[64:128, :, ts]), reads=[TQT[i]], writes=[TQ[tp]])

    def zmm(e, i, j, base, first_start):
        tp = i % 2
        ks = slice(j * 128, (j + 1) * 128)
        for h in range(8):
            c = h // 2
            q = QA[tp] if h % 2 == 0 else QB[tp]
            st = True if first_start is None else (h % 4 == 0)
            ins = e.matmul(pb[base + h // 4][:, (h % 4) * 128:(h % 4 + 1) * 128], lhsT=KT[:, c, ks], rhs=q[:, c, :], start=st,
                           stop=(first_start is None), skip_group_check=True)
        return ins

    def st_front(n):
        i, j = items[n]
        ip = n % 2
        S.add("pe", lambda e: zmm(e, i, j, 0, None), reads=[TKT[j], TQ[i % 2]], writes=[pbT[0], pbT[1]])
        S.add("act", lambda e: e.activation(out=E32[ip], in_=k.pbig[:, 0:1024], func=AF.Exp), reads=[pbT[0], pbT[1]], writes=[TE[ip]])
        S.add("act", lambda e: e.activation(out=SP[ip], in_=E32[ip], func=AF.Ln, bias=k.one_b), reads=[TE[ip]], writes=[TSP[ip]])
        if j == i:
            S.add("dve", lambda e: e.tensor_tensor(out=r3(SP[ip], 128), in0=r3(SP[ip], 128), in1=mstr8, op=ALU.mult),
                  reads=[TSP[ip], k.Tconst], writes=[TSP[ip]])

    def st_w(n):
        i, j = items[n]
        ip = n % 2
        tp = i % 2

        def wmm(e):
            zmm(e, i, j, 2, True)
            for b in range(2):
                ins = e.matmul(pb[2 + b][:], lhsT=k.negL, rhs=SP[ip][:, b * 512:(b + 1) * 512], start=False, stop=(j == i), skip_group_check=True)
                if j < i:
                    ins = e.matmul(pb[2 + b][:], lhsT=k.ones, rhs=R[tp][:, b * 512:(b + 1) * 512], start=False, stop=True, skip_group_check=True)
            return ins
        S.add("pe", wmm, reads=[TKT[j], TQ[tp], TSP[ip], k.Tconst] + ([TR[tp]] if j < i else []), writes=[pbT[2], pbT[3]])
        if j > 0:
            if j == i:
                S.add("dve", lambda e: e.tensor_scalar(out=R[tp], in0=SP[ip], scalar1=-1.0, scalar2=None, op0=ALU.mult), reads=[TSP[ip]], writes=[TR[tp]])
            else:
                S.add("dve", lambda e: e.tensor_tensor(out=R[tp], in0=R[tp], in1=SP[ip], op=ALU.subtract), reads=[TSP[ip], TR[tp]], writes=[TR[tp]])

    def st_exp2(n):
        i, j = items[n]
        ip = n % 2
        S.add("act", lambda e: e.activation(out=AW[ip], in_=k.pbig[:, 1024:2048], func=AF.Exp), reads=[pbT[2], pbT[3]], writes=[TAW[ip]])
        if j == i:
            S.add("dve", lambda e: e.tensor_tensor(out=r3(AW[ip], 128), in0=r3(AW[ip], 128), in1=mstr8, op=ALU.mult),
                  reads=[TAW[ip], k.Tconst], writes=[TAW[ip]])

    def st_av(n):
        i, j = items[n]
        ip = n % 2
        tp = i % 2

        def av(e):
            for h in range(8):
                ins = e.matmul(pb[4][:, h * 64:(h + 1) * 64], lhsT=AW[ip][:, h * 128:(h + 1) * 128], rhs=V[:, j, h * 64:(h + 1) * 64],
                               start=(j == i and h == 0), stop=(j == 0 and h == 7), skip_group_check=True)
            return ins
        S.add("pe", av, reads=[TAW[ip], TV[j]], writes=[pbT[4]])
        if j == 0:
            S.add("dve", lambda e: e.tensor_copy(out=CATb[tp], in_=pb[4][:]), reads=[pbT[4]], writes=[TCATb[tp]])

            def ch_tr():
                def trc(e):
                    for c in range(8):
                        src = CATa[:, i, c * 128:(c + 1) * 128] if c < 4 else CATb[tp][:, (c - 4) * 128:(c - 3) * 128]
                        ins = e.transpose(out=pbv5[:, c * 128:(c + 1) * 128], in_=src, identity=k.ident)
                    return ins
                S.add("pe", trc, reads=[TCATa[i], TCATb[tp], k.Tconst], writes=[pbT[5]])
                S.add("dve", lambda e: e.tensor_copy(out=CATT[tp], in_=r3(pbv5, 128)), reads=[pbT[5]], writes=[TCATT[tp]])
            deferred.append(ch_tr)
            for n2 in range(2):
                for hf in range(2):
                    def ch_wo(n2=n2, hf=hf):
                        def womm(e):
                            for kc in range(4 * hf, 4 * hf + 4):
                                ins = e.matmul(pb[6 + n2][:], lhsT=CATT[tp][:, kc, :], rhs=Wout[:, kc, n2 * 512:(n2 + 1) * 512], start=(kc == 0), stop=(kc == 7))
                            return ins
                        S.add("pe", womm, reads=[TCATT[tp]] + TWout, writes=[pbT[6 + n2]])
                        if hf == 1:
                            S.add("dve", lambda e: e.tensor_tensor(out=k.X[:, i, n2 * 512:(n2 + 1) * 512], in0=pb[6 + n2][:],
                                                                   in1=k.X[:, i, n2 * 512:(n2 + 1) * 512], op=ALU.add),
                                  reads=[pbT[6 + n2], k.Xt[i]], writes=[k.Xt[i]])
                    deferred.append(ch_wo)

    deferred = []
    build_q(0)
    build_q(1)
    st_front(0)
    for n in range(N + 1):
        if n < N:
            if n + 1 < N:
                st_front(n + 1)
            st_w(n)
        if deferred:
            deferred.pop(0)()
        if n - 1 >= 0:
            st_av(n - 1)
            ip_, jp_ = items[n - 1]
            if jp_ == 0 and ip_ + 2 < NT:
                build_q(ip_ + 2)
        if n < N:
            st_exp2(n)
    while deferred:
        deferred.pop(0)()


def phase_ffn(k, s, l):
    S, A = k.S, k.A
    S.barrier()
    A.top = k.persist_top
    pb, pbT = k.pb, k.pbT
    HT = r3(A.new_bf(8 * 2050), 2050)
    NPC = [6, 6, 5, 5]
    PST = [0, 6, 12, 17]
    GT = [r3(A.new_bf(6 * 2048), 2048) for _ in range(2)]
    WDN = [r3(A.new_bf(6 * 1024), 1024) for _ in range(2)]
    NSLOT = 4
    WUP = [r3(A.new_bf(8 * 256), 256) for _ in range(NSLOT)]
    hn = A.new_bf(1024)
    junkf = A.new_bf(1024)
    TG = [A.new_f32(412) for _ in range(3)]
    GG = [A.new_f32(412) for _ in range(3)]
    TU = [A.new_f32(412) for _ in range(3)]
    Thn = T()
    THT = [T() for _ in range(NT)]
    Thalo = T()
    TGT = [[T() for _ in range(6)] for _ in range(2)]
    TWDN = [T(), T()]
    TWUPg = [T() for _ in range(NSLOT)]
    TWUPu = [T() for _ in range(NSLOT)]
    TTG, TGG, TTU = [T(), T(), T()], [T(), T(), T()], [T(), T(), T()]
    wup = k.ffn_w_up[l].rearrange("(kc p) n -> p kc n", p=128)
    wdn = k.ffn_w_down[l].rearrange("(c p) n -> p c n", p=128)
    bounds = [0, 410, 820, 1230, 1640, 2048]

    def load_up(c):
        sl = c % NSLOT
        S.add("pool", lambda e: e.dma_start(out=WUP[sl][:, :, 0:128], in_=wup[:, :, c * 128:(c + 1) * 128]),
              writes=[TWUPg[sl]], chan="wu%d" % sl)
        S.add("pool", lambda e: e.dma_start(out=WUP[sl][:, :, 128:256], in_=wup[:, :, DFF + c * 128:DFF + (c + 1) * 128]),
              writes=[TWUPu[sl]], chan="wu%d" % sl)

    def load_dn(p):
        pp = p % 2
        S.add("pool", lambda e: e.dma_start(out=WDN[pp][:, 0:NPC[p], :], in_=wdn[:, PST[p]:PST[p] + NPC[p], :]),
              writes=[TWDN[pp]], chan="wd%d" % pp)

    for c in range(NSLOT):
        load_up(c)
    load_dn(0)
    S.add("dve", lambda e: e.memset(HT[:, :, 0:2], 0.0), writes=[Thalo])
    gcol = VFM["ffn_g"] + 8 * l
    Tjunk = T()
    ring = JunkRing([(junkf, Tjunk), (hn, Thn)])
    norm_stats_all(k, ring)
    hn2 = [hn, junkf]
    Thn2 = [Thn, Tjunk]
    ht_next = [0]

    def make_ht(upto):
        while ht_next[0] <= min(upto, NT - 1):
            i = ht_next[0]
            norm_hT2(k, i, gcol, hn2[i % 2], Thn2[i % 2], 6 + (i % 2), HT[:, :, 2 + i * 128:2 + (i + 1) * 128], THT[i])
            ht_next[0] += 1
    make_ht(3)

    cw = lambda j, idx: k.vfm[:, VFM["conv_w"] + (l * 3 + j) * 44 + idx: VFM["conv_w"] + (l * 3 + j) * 44 + idx + 1]
    cb = lambda idx: k.vfm[:, VFM["conv_b"] + l * 44 + idx: VFM["conv_b"] + l * 44 + idx + 1]
    item = 0
    pending = []

    def down_unit(m, pp, p):
        if p == 3:
            b0, b1 = 2 * (m % 4), 2 * (m % 4) + 1
        else:
            b0, b1 = 6, 7
        for (b, n2) in ((b0, 0), (b1, 1)):
            def dnmm(e, b=b, n2=n2):
                for cc in range(NPC[p]):
                    ins = e.matmul(pb[b][:], lhsT=GT[pp][:, cc, m * 128:(m + 1) * 128], rhs=WDN[pp][:, cc, n2 * 512:(n2 + 1) * 512],
                                   start=(cc == 0), stop=(cc == NPC[p] - 1))
                return ins
            S.add("pe", dnmm, reads=[TWDN[pp]] + TGT[pp][:NPC[p]], writes=[pbT[b]])
        for (b, n2) in ((b0, 0), (b1, 1)):
            S.add("dve", (lambda e, b=b, n2=n2: e.tensor_tensor(out=k.X[:, m, n2 * 512:(n2 + 1) * 512], in0=pb[b][:],
                                                                 in1=k.X[:, m, n2 * 512:(n2 + 1) * 512], op=ALU.add)),
                  reads=[pbT[b], k.Xt[m]], writes=[k.Xt[m]])
        if p == 3:
            stat_tile(k, m, ring)

    for p in range(4):
        pp = p % 2
        for cc in range(NPC[p]):
            c = PST[p] + cc
            sl = c % NSLOT
            for tt in range(5):
                t0, t1 = bounds[tt], bounds[tt + 1]
                n = t1 - t0
                par = item % 3
                item += 1
                if pending and item % 2 == 0:
                    pending.pop(0)()
                gb, ub = 2 * par, 2 * par + 1
                tiles = sorted(set([max(t0 - 2, 0) // 128, (t1 - 1) // 128] + list(range(t0 // 128, (t1 - 1) // 128 + 1))))
                make_ht(tiles[-1] + 3)

                def upmm(e, sl=sl, t0=t0, n=n, gb=gb, ub=ub):
                    for (b, o) in ((gb, 0), (ub, 128)):
                        for kc in range(8):
                            ins = e.matmul(pb[b][:, 0:n + 2], lhsT=WUP[sl][:, kc, o:o + 128], rhs=HT[:, kc, t0:t0 + n + 2],
                                           start=(kc == 0), stop=(kc == 7))
                    return ins
                S.add("pe", upmm, reads=[TWUPg[sl], TWUPu[sl], Thalo] + [THT[x] for x in tiles], writes=[pbT[gb], pbT[ub]])
                G, U = pb[gb], pb[ub]
                tg, gg, tu = TG[par][:, 0:n], GG[par][:, 0:n], TU[par][:, 0:n]
                S.add("act", (lambda e, G=G, tg=tg, c=c, n=n: e.activation(out=tg, in_=G[:, 0:n], func=AF.Identity, scale=cw(0, c), bias=cb(c))),
                      reads=[pbT[gb], k.Tconst], writes=[TTG[par]])
                S.add("act", (lambda e, U=U, tu=tu, c=c, n=n: e.activation(out=tu, in_=U[:, 0:n], func=AF.Identity, scale=cw(0, 22 + c), bias=cb(22 + c))),
                      reads=[pbT[ub], k.Tconst], writes=[TTU[par]])
                S.add("dve", (lambda e, G=G, tg=tg, c=c, n=n: e.scalar_tensor_tensor(out=tg, in0=G[:, 1:n + 1], scalar=cw(1, c), in1=tg, op0=ALU.mult, op1=ALU.add)),
                      reads=[pbT[gb], TTG[par], k.Tconst], writes=[TTG[par]])
                S.add("dve", (lambda e, G=G, tg=tg, c=c, n=n: e.scalar_tensor_tensor(out=tg, in0=G[:, 2:n + 2], scalar=cw(2, c), in1=tg, op0=ALU.mult, op1=ALU.add)),
                      reads=[pbT[gb], TTG[par], k.Tconst], writes=[TTG[par]])
                S.add("act", (lambda e, tg=tg, gg=gg: e.activation(out=gg, in_=tg, func=AF.Gelu)), reads=[TTG[par]], writes=[TGG[par]])
                S.add("dve", (lambda e, U=U, tu=tu, c=c, n=n: e.scalar_tensor_tensor(out=tu, in0=U[:, 1:n + 1], scalar=cw(1, 22 + c), in1=tu, op0=ALU.mult, op1=ALU.add)),
                      reads=[pbT[ub], TTU[par], k.Tconst], writes=[TTU[par]])
                S.add("dve", (lambda e, U=U, tu=tu, c=c, n=n: e.scalar_tensor_tensor(out=tu, in0=U[:, 2:n + 2], scalar=cw(2, 22 + c), in1=tu, op0=ALU.mult, op1=ALU.add)),
                      reads=[pbT[ub], TTU[par], k.Tconst], writes=[TTU[par]])
                S.add("pool", (lambda e, gg=gg, tu=tu, pp=pp, cc=cc, t0=t0, t1=t1: e.tensor_tensor(out=GT[pp][:, cc, t0:t1], in0=gg, in1=tu, op=ALU.mult)),
                      reads=[TGG[par], TTU[par]], writes=[TGT[pp][cc]])
            if c + NSLOT < NFC:
                load_up(c + NSLOT)
        while pending:
            pending.pop(0)()
        if p + 1 < 4:
            load_dn(p + 1)
        for m in range(NT):
            pending.append(lambda m=m, pp=pp, p=p: down_unit(m, pp, p))
    while pending:
        pending.pop(0)()
    k.stats_ready = True


def phase_ple(k, s, l, final):
    S, A = k.S, k.A
    S.barrier()
    A.top = k.persist_top
    pb, pbT = k.pb, k.pbT
    Wg = r3(A.new_bf(8 * 1024), 1024)
    Wp = r3(A.new_bf(2 * 1024), 1024)
    postg = A.new_f32(1024)
    finalg = A.new_f32(1024)
    hn = [A.new_bf(1024) for _ in range(2)]
    hT = [r3(A.new_bf(1024), 128) for _ in range(2)]
    pbf = [A.new_bf(256) for _ in range(2)]
    PT = r3(A.new_bf(2 * SEQ), SEQ)
    sig = [A.new_f32(1024) for _ in range(2)]
    t2 = [A.new_f32(1024) for _ in range(2)]
    junk = A.new_bf(1024)
    junk2 = A.new_bf(1024)
    outt = [A.new_f32(1024) for _ in range(2)]
    ssp = A.new_f32(32)
    sp1 = A.new_f32(16)
    sp2 = A.new_f32(16)
    rsp = A.new_f32(16)
    TWg = [T(), T()]
    TWp = [T()]
    Tmisc = T()
    Thn = [T(), T()]
    ThT = [T(), T()]
    Tpbf = [T(), T()]
    TPT = [T() for _ in range(NT)]
    Tsig, Tt2 = [T(), T()], [T(), T()]
    Tjunk, Tssp = T(), T()
    ring = JunkRing([(junk, Tjunk), (junk2, T())])
    Tsspi = [T() for _ in range(32)]
    Tout = [T(), T()]
    load_w_slabs(k, Wp, k.ple_w_proj[l], 1024, 1024, "w1", TWp)
    S.add("sp", lambda e: e.dma_start(out=postg, in_=k.vbc_d[:, VBC["post_g"] + 1024 * l:VBC["post_g"] + 1024 * (l + 1)]), writes=[Tmisc], chan="misc")
    if final:
        S.add("sp", lambda e: e.dma_start(out=finalg, in_=k.vbc_d[:, VBC["final_g"]:VBC["final_g"] + 1024]), writes=[Tmisc], chan="misc")
    gcol = VFM["ple_g"] + 8 * l
    out_ops = []
    fin_q = []
    fms = A.new_f32(16)
    frs = A.new_f32(16)
    Tf = [T() for _ in range(NT)]
    for i in range(NT):
        cur = i % 2
        S.add("pool", (lambda e, i=i, cur=cur: e.dma_start(out=pbf[cur], in_=k.p_d[l, s, i * 128:(i + 1) * 128, :])),
              writes=[Tpbf[cur]], chan="pin%d" % cur)
        if i == 1:
            load_w_slabs(k, Wg, k.ple_w_gate[l], 1024, 512, "w0", TWg)
        tb = 6 + cur
        pbv = pb[tb][:].bitcast(BF16)

        def trp(e, cur=cur, pbv=pbv):
            for c in range(2):
                ins = e.transpose(out=pbv[:, c * 128:(c + 1) * 128], in_=pbf[cur][:, c * 128:(c + 1) * 128], identity=k.ident)
            return ins
        S.add("pe", trp, reads=[Tpbf[cur], k.Tconst], writes=[pbT[tb]])
        S.add("dve", (lambda e, i=i, pbv=pbv: e.tensor_copy(out=PT[:, :, i * 128:(i + 1) * 128], in_=r3(pbv[:, 0:256], 128))), reads=[pbT[tb]], writes=[TPT[i]])
        b0, b1 = 4 * cur, 4 * cur + 1

        def pmm(e, i=i, b0=b0, b1=b1):
            for (b, n2) in ((b0, 0), (b1, 1)):
                for kc in range(2):
                    ins = e.matmul(pb[b][:], lhsT=PT[:, kc, i * 128:(i + 1) * 128], rhs=Wp[:, kc, n2 * 512:(n2 + 1) * 512], start=(kc == 0), stop=(kc == 1))
            return ins
        S.add("pe", pmm, reads=[TPT[i]] + TWp, writes=[pbT[b0], pbT[b1]])
        for (b, n2) in ((b0, 0), (b1, 1)):
            jb, jt = ring.nxt()
            S.add("act", (lambda e, b=b, n2=n2, i=i, jb=jb: e.activation(out=jb[:, 0:512], in_=pb[b][:], func=AF.Square, accum_out=ssp[:, 2 * i + n2:2 * i + n2 + 1])),
                  reads=[pbT[b]], writes=[Tsspi[2 * i + n2], jt])
    S.add("pool", lambda e: e.tensor_tensor(out=Wp, in0=Wp, in1=postg.unsqueeze(1).to_broadcast([128, 2, 1024]), op=ALU.mult),
          reads=[Tmisc] + TWp, writes=[TWp[0]])
    norm_stats_all(k, ring)
    sspv = ssp.rearrange("p (i two) -> p i two", two=2)
    S.add("dve", lambda e: e.tensor_tensor(out=sp1, in0=sspv[:, :, 0], in1=sspv[:, :, 1], op=ALU.add), reads=Tsspi, writes=[Tssp])
    S.add("act", lambda e: e.activation(out=sp2, in_=sp1, func=AF.Ln, scale=1.0 / D, bias=k.epsr), reads=[Tssp], writes=[Tssp])
    S.add("act", lambda e: e.activation(out=rsp, in_=sp2, func=AF.Exp, scale=-0.5), reads=[Tssp], writes=[Tssp])
    norm_hT2(k, 0, gcol, hn[0], Thn[0], 6, hT[0], ThT[0])
    norm_hT2(k, 1, gcol, hn[1], Thn[1], 7, hT[1], ThT[1])
    for i in range(NT):
        cur = i % 2
        g0, g1 = 2 * cur, 2 * cur + 1
        p0, p1 = 4, 5

        def gmm(e, cur=cur, g0=g0, g1=g1):
            for (b, n2) in ((g0, 0), (g1, 1)):
                for kc in range(8):
                    ins = e.matmul(pb[b][:], lhsT=hT[cur][:, kc, :], rhs=Wg[:, kc, n2 * 512:(n2 + 1) * 512], start=(kc == 0), stop=(kc == 7))
            return ins
        S.add("pe", gmm, reads=[ThT[cur]] + TWg, writes=[pbT[g0], pbT[g1]])

        def pmm2(e, i=i, p0=p0, p1=p1):
            for (b, n2) in ((p0, 0), (p1, 1)):
                for kc in range(2):
                    ins = e.matmul(pb[b][:], lhsT=PT[:, kc, i * 128:(i + 1) * 128], rhs=Wp[:, kc, n2 * 512:(n2 + 1) * 512], start=(kc == 0), stop=(kc == 1))
            return ins
        S.add("pe", pmm2, reads=[TPT[i]] + TWp, writes=[pbT[p0], pbT[p1]])
        if i + 2 < NT:
            norm_hT2(k, i + 2, gcol, hn[cur], Thn[cur], 6 + cur, hT[cur], ThT[cur])
        for (b, n2) in ((g0, 0), (g1, 1)):
            S.add("act", (lambda e, b=b, n2=n2, cur=cur: e.activation(out=sig[cur][:, n2 * 512:(n2 + 1) * 512], in_=pb[b][:], func=AF.Sigmoid)),
                  reads=[pbT[b]], writes=[Tsig[cur]])
        for (b, n2) in ((p0, 0), (p1, 1)):
            S.add("dve", (lambda e, b=b, n2=n2, cur=cur, i=i: e.scalar_tensor_tensor(out=t2[cur][:, n2 * 512:(n2 + 1) * 512], in0=pb[b][:], scalar=rsp[:, i:i + 1],
                                                                                   in1=sig[cur][:, n2 * 512:(n2 + 1) * 512], op0=ALU.mult, op1=ALU.mult)),
                  reads=[pbT[b], Tssp, Tsig[cur]], writes=[Tt2[cur]])
        S.add("pool", (lambda e, i=i, cur=cur: e.tensor_tensor(out=k.X[:, i, :], in0=t2[cur], in1=k.X[:, i, :], op=ALU.add)), reads=[Tt2[cur], k.Xt[i]], writes=[k.Xt[i]])
        if not final:
            fin_q.append(lambda i=i: stat_tile(k, i, ring))
            if len(fin_q) > 2:
                fin_q.pop(0)()
        if final:
            def fin(i=i, cur=cur):
                jb, jt = ring.nxt()
                S.add("act", (lambda e, i=i, jb=jb: e.activation(out=jb, in_=k.X[:, i, :], func=AF.Square, accum_out=fms[:, i:i + 1])), reads=[k.Xt[i]], writes=[Tf[i], jt])
                S.add("dve", (lambda e, i=i: e.tensor_scalar(out=fms[:, i:i + 1], in0=fms[:, i:i + 1], scalar1=1.0 / D, scalar2=RMS_EPS, op0=ALU.mult, op1=ALU.add)),
                      reads=[Tf[i]], writes=[Tf[i]])
                S.add("pool", (lambda e, i=i: e.tensor_tensor(out=frs[:, i:i + 1], in0=fms[:, i:i + 1], in1=k.neghalf, op=ALU.pow)), reads=[Tf[i], k.Tconst], writes=[Tf[i]])
                S.add("act", (lambda e, i=i, cur=cur: e.activation(out=outt[cur], in_=k.X[:, i, :], func=AF.Copy, scale=frs[:, i:i + 1])),
                      reads=[k.Xt[i], Tf[i]], writes=[Tout[cur]])
                if k.next_seq is not None:
                    k.load_x_tile(k.next_seq, i)
                S.add("pool" if cur == 0 else "dve", (lambda e, cur=cur: e.tensor_tensor(out=outt[cur], in0=outt[cur], in1=finalg, op=ALU.mult)), reads=[Tout[cur], Tmisc], writes=[Tout[cur]])
                out_ops.append(S.add("sp", (lambda e, i=i, cur=cur: e.dma_start(out=k.out_d[s, i * 128:(i + 1) * 128, :], in_=outt[cur])),
                                     reads=[Tout[cur]], chan="xout%d" % cur))

            fin_q.append(fin)
            if len(fin_q) > 2:
                fin_q.pop(0)()
    while fin_q:
        fin_q.pop(0)()
    k.stats_ready = not final
    return out_ops


def phase_mix1(k, s):
    S, A = k.S, k.A
    S.barrier()
    A.top = k.persist_top
    pb, pbT = k.pb, k.pbT
    HT = r3(A.new_bf(8 * SEQ), SEQ)
    WI = [r3(A.new_bf(8 * 1536), 1536) for _ in range(2)]
    WO = [r3(A.new_bf(4 * 1024), 1024) for _ in range(2)]
    S32 = r3(A.new_f32(2 * 512), 512)
    Sb = r3(A.new_bf(2 * 512), 512)
    cs = [A.new_f32(256) for _ in range(2)]
    decT = A.new_f32(512)
    xi8 = r3(A.new_f32(1024), 128)
    zeta = A.new_f32(4)
    hn = A.new_bf(1024)
    junk = A.new_bf(1024)
    junk2 = A.new_bf(1024)
    QK = [A.new_bf(512) for _ in range(2)]
    Vt = [A.new_bf(512) for _ in range(2)]
    SGt = [A.new_bf(512) for _ in range(2)]
    qx = [r3(A.new_bf(256), 128) for _ in range(2)]
    ktm = [A.new_bf(256) for _ in range(2)]
    innT = [A.new_bf(128) for _ in range(2)]
    RT = [[A.new_f32(256) for _ in range(4)] for _ in range(2)]
    Y = [A.new_bf(512) for _ in range(2)]
    YT = [r3(A.new_bf(512), 128) for _ in range(2)]
    stats = [A.new_f32(8) for _ in range(2)]
    mv = [A.new_f32(4) for _ in range(2)]
    THT = [T() for _ in range(NT)]
    TWI = [[T() for _ in range(4)] for _ in range(2)]
    TWO = [T(), T()]
    TS32 = [T(), T()]
    TSb = [T(), T()]
    Tcs = [T(), T()]
    Ttab, Thn, Tjunk = T(), T(), T()
    TQK, TVt, TSGt, Tqx, Tktm, TinnT = ([T(), T()] for _ in range(6))
    TRT = [[T() for _ in range(4)] for _ in range(2)]
    TY, TYT, Tst, Trs = ([T(), T()] for _ in range(4))
    win = k.ret_w_in.rearrange("(kc p) n -> p kc n", p=128)
    wo = k.ret_w_out.rearrange("(kc p) n -> p kc n", p=128)

    def load_head(h):
        sl = h % 2
        parts = [(0, 256, h * 256), (256, 256, 1024 + h * 256), (512, 512, 2048 + h * 512), (1024, 512, 4096 + h * 512)]
        for pi, (o, n, c0) in enumerate(parts):
            S.add("pool", (lambda e, sl=sl, o=o, n=n, c0=c0: e.dma_start(out=WI[sl][:, :, o:o + n], in_=win[:, :, c0:c0 + n])),
                  writes=[TWI[sl][pi]], chan="wi%d" % sl)
        S.add("pool", (lambda e, h=h, sl=sl: e.dma_start(out=WO[sl], in_=wo[:, 4 * h:4 * h + 4, :])), writes=[TWO[sl]], chan="wo%d" % sl)

    def scale_wo(h, kc):
        sl = h % 2
        gcolv = k.vfm[:, VFM["gn_g"] + 4 * h + kc:VFM["gn_g"] + 4 * h + kc + 1]
        S.add("dve", lambda e: e.tensor_scalar(out=WO[sl][:, kc, :], in0=WO[sl][:, kc, :], scalar1=gcolv, scalar2=None, op0=ALU.mult),
              reads=[TWO[sl], k.Tconst], writes=[TWO[sl]])

    load_head(0)
    for kc_ in range(4):
        scale_wo(0, kc_)
    for (dst, nm) in ((decT, "decayT"), (xi8.rearrange("p a b -> p (a b)"), "xi"), (zeta, "zeta")):
        S.add("sp", (lambda e, dst=dst, nm=nm: e.dma_start(out=dst, in_=k.consts_d[:, cslice(nm)])), writes=[Ttab], chan="misc")
    gcol = VFM["mix_g"] + 8
    co, _ = CONST_OFF["cos"]
    so, _ = CONST_OFF["sin"]
    ring = JunkRing([(junk, Tjunk), (junk2, T())])
    norm_stats_all(k, ring)
    for i in range(2):
        norm_hT2(k, i, gcol, hn, Thn, 3, HT[:, :, i * 128:(i + 1) * 128], THT[i])
    items = [(h, i) for h in range(4) for i in range(NT)]
    stat_q = []
    pbv3 = pb[3][:].bitcast(BF16)
    pbv0 = pb[0][:].bitcast(BF16)

    def qk4_(par):
        return QK[par].rearrange("p (a b c) -> p a b c", a=2, b=2)

    def stage_a(n):
        h, i = items[n]
        par = n % 2
        sl = h % 2
        if i == 4 and h + 1 < 4:
            load_head(h + 1)
        if 8 <= i < 12 and h + 1 < 4:
            scale_wo(h + 1, i - 8)
        W = WI[sl]
        tsl = slice(i * 128, (i + 1) * 128)
        S.add("sp", lambda e: e.dma_start(out=cs[par][:, 0:128], in_=k.consts_d[:, co + i * 128:co + (i + 1) * 128]), writes=[Tcs[par]], chan="cs%d" % par)
        S.add("sp", lambda e: e.dma_start(out=cs[par][:, 128:256], in_=k.consts_d[:, so + i * 128:so + (i + 1) * 128]), writes=[Tcs[par]], chan="cs%d" % par)

        def pqk(e):
            for ci in range(4):
                for kc in range(8):
                    ins = e.matmul(pb[0][:, ci * 128:(ci + 1) * 128], lhsT=W[:, kc, ci * 128:(ci + 1) * 128], rhs=HT[:, kc, tsl], start=(kc == 0), stop=(kc == 7))
            return ins
        S.add("pe", pqk, reads=TWI[sl] + [THT[i]], writes=[pbT[0]])
        v4 = pb[0][:].rearrange("p (a b c) -> p a b c", a=2, b=2)
        x1, x2 = v4[:, :, 0, :], v4[:, :, 1, :]
        cosb = cs[par][:, 0:128].unsqueeze(1).to_broadcast([128, 2, 128])
        sinb = cs[par][:, 128:256].unsqueeze(1).to_broadcast([128, 2, 128])
        rt = [r3(x, 128) for x in RT[par]]
        for (ri, xin, tab) in ((0, x1, cosb), (1, x2, sinb), (2, x2, cosb), (3, x1, sinb)):
            S.add("dve", (lambda e, ri=ri, xin=xin, tab=tab: e.tensor_tensor(out=rt[ri], in0=xin, in1=tab, op=ALU.mult)),
                  reads=[pbT[0], Tcs[par]], writes=[TRT[par][ri]])
        qk4 = qk4_(par)
        S.add("pool", lambda e: e.tensor_tensor(out=qk4[:, :, 0, :], in0=rt[0], in1=rt[1], op=ALU.subtract), reads=[TRT[par][0], TRT[par][1]], writes=[TQK[par]])
        S.add("pool", lambda e: e.tensor_tensor(out=qk4[:, :, 1, :], in0=rt[2], in1=rt[3], op=ALU.add), reads=[TRT[par][2], TRT[par][3]], writes=[TQK[par]])
        if i > 0:
            xib = xi8[:, 2 * h, :].unsqueeze(1).to_broadcast([128, 2, 128])
            S.add("pool", lambda e: e.tensor_tensor(out=qx[par], in0=qk4[:, 0, :, :], in1=xib, op=ALU.mult), reads=[TQK[par], Ttab], writes=[Tqx[par]])
        if h == 0 and i + 2 < NT:
            norm_hT2(k, i + 2, gcol, hn, Thn, 3, HT[:, :, (i + 2) * 128:(i + 3) * 128], THT[i + 2])

    def stage_a2(n):
        h, i = items[n]
        par = n % 2
        sl = h % 2
        W = WI[sl]
        tsl = slice(i * 128, (i + 1) * 128)

        def pv(e):
            for (b, o) in ((1, 512), (2, 1024)):
                for kc in range(8):
                    ins = e.matmul(pb[b][:], lhsT=HT[:, kc, tsl], rhs=W[:, kc, o:o + 512], start=(kc == 0), stop=(kc == 7))
            return ins
        S.add("pe", pv, reads=TWI[sl] + [THT[i]], writes=[pbT[1], pbT[2]])
        S.add("act", lambda e: e.activation(out=Vt[par], in_=pb[1][:], func=AF.Copy), reads=[pbT[1]], writes=[TVt[par]])
        S.add("act", lambda e: e.activation(out=SGt[par], in_=pb[2][:], func=AF.Silu), reads=[pbT[2]], writes=[TSGt[par]])

    def stage_b1(n):
        h, i = items[n]
        par = n % 2
        last = (i == NT - 1)
        qk4 = qk4_(par)

        def inmm(e):
            for dc in range(2):
                ins = e.matmul(pb[4][:, 0:128], lhsT=qk4[:, 1, dc, :], rhs=qk4[:, 0, dc, :], start=(dc == 0), stop=(dc == 1))
            return ins
        S.add("pe", inmm, reads=[TQK[par]], writes=[pbT[4]])
        S.add("dve", lambda e: e.tensor_tensor(out=innT[par], in0=pb[4][:, 0:128], in1=decT[:, h * 128:(h + 1) * 128], op=ALU.mult),
              reads=[pbT[4], Ttab], writes=[TinnT[par]])
        if not last:
            def trk(e):
                for dc in range(2):
                    ins = e.transpose(out=pbv3[:, dc * 128:(dc + 1) * 128], in_=qk4[:, 1, dc, :], identity=k.ident)
                return ins
            S.add("pe", trk, reads=[TQK[par], k.Tconst], writes=[pbT[3]])
            S.add("act", lambda e: e.activation(out=ktm[par], in_=pbv3[:, 0:256], func=AF.Copy, scale=zeta[:, h:h + 1]), reads=[pbT[3], Ttab], writes=[Tktm[par]])

    def stage_b2(n):
        h, i = items[n]
        par = n % 2
        ob = 5 + par
        first = (i == 0)
        last = (i == NT - 1)

        def omm(e):
            ins = e.matmul(pb[ob][:], lhsT=innT[par], rhs=Vt[par], start=True, stop=first)
            if not first:
                for dc in range(2):
                    ins = e.matmul(pb[ob][:], lhsT=qx[par][:, dc, :], rhs=Sb[:, dc, :], start=False, stop=(dc == 1))
            return ins
        S.add("pe", omm, reads=[TinnT[par], TVt[par]] + ([] if first else [Tqx[par], TSb[0], TSb[1]]), writes=[pbT[ob]])
        if not last:
            for dc in range(2):
                kb = 7 if dc == 0 else 0
                S.add("pe", (lambda e, dc=dc, kb=kb: e.matmul(pb[kb][:], lhsT=ktm[par][:, dc * 128:(dc + 1) * 128], rhs=Vt[par], start=True, stop=True)),
                      reads=[Tktm[par], TVt[par]], writes=[pbT[kb]])
                if first:
                    S.add("dve", (lambda e, dc=dc, kb=kb: e.tensor_copy(out=S32[:, dc, :], in_=pb[kb][:])), reads=[pbT[kb]], writes=[TS32[dc]])
                else:
                    S.add("dve", (lambda e, dc=dc, kb=kb: e.scalar_tensor_tensor(out=S32[:, dc, :], in0=S32[:, dc, :], scalar=GAM128[h], in1=pb[kb][:],
                                                                                op0=ALU.mult, op1=ALU.add)),
                          reads=[pbT[kb], TS32[dc]], writes=[TS32[dc]])
                S.add("act", (lambda e, dc=dc: e.activation(out=Sb[:, dc, :], in_=S32[:, dc, :], func=AF.Copy)), reads=[TS32[dc]], writes=[TSb[dc]])

    def stage_c1(n):
        h, i = items[n]
        par = n % 2
        ob = 5 + par
        st_, mv_ = stats[par], mv[par]
        S.add("dve", lambda e: e.bn_stats(out=st_[:, 0:6], in_=pb[ob][:]), reads=[pbT[ob]], writes=[Tst[par]])
        S.add("dve", lambda e: e.bn_aggr(out=mv_[:, 0:2], in_=st_[:, 0:6]), reads=[Tst[par]], writes=[Tst[par]])
        S.add("dve", lambda e: e.scalar_tensor_tensor(out=Y[par], in0=pb[ob][:], scalar=mv_[:, 0:1], in1=SGt[par], op0=ALU.subtract, op1=ALU.mult),
              reads=[pbT[ob], Tst[par], TSGt[par]], writes=[TY[par]])
        S.add("dve", lambda e: e.tensor_scalar(out=mv_[:, 2:3], in0=mv_[:, 1:2], scalar1=LN_EPS, scalar2=None, op0=ALU.add), reads=[Tst[par]], writes=[Trs[par]])
        S.add("pool", lambda e: e.tensor_tensor(out=mv_[:, 3:4], in0=mv_[:, 2:3], in1=k.neghalf, op=ALU.pow), reads=[Trs[par], k.Tconst], writes=[Trs[par]])

    def stage_c2(n):
        h, i = items[n]
        par = n % 2
        sl = h % 2
        mv_ = mv[par]

        def try_(e):
            for c in range(4):
                ins = e.transpose(out=pbv3[:, c * 128:(c + 1) * 128], in_=Y[par][:, c * 128:(c + 1) * 128], identity=k.ident)
            return ins
        S.add("pe", try_, reads=[TY[par], k.Tconst], writes=[pbT[3]])
        S.add("dve", lambda e: e.tensor_copy(out=YT[par], in_=r3(pbv3[:, 0:512], 128)), reads=[pbT[3]], writes=[TYT[par]])

    def stage_c2b(n):
        h, i = items[n]
        par = n % 2
        sl = h % 2
        mv_ = mv[par]

        def womm(e):
            for n2 in range(2):
                for kc in range(4):
                    ins = e.matmul(pb[1 + n2][:], lhsT=YT[par][:, kc, :], rhs=WO[sl][:, kc, n2 * 512:(n2 + 1) * 512], start=(kc == 0), stop=(kc == 3))
            return ins
        S.add("pe", womm, reads=[TYT[par], TWO[sl]], writes=[pbT[1], pbT[2]])
        for n2 in range(2):
            S.add("dve", (lambda e, n2=n2: e.scalar_tensor_tensor(out=k.X[:, i, n2 * 512:(n2 + 1) * 512], in0=pb[1 + n2][:], scalar=mv_[:, 3:4],
                                                                   in1=k.X[:, i, n2 * 512:(n2 + 1) * 512], op0=ALU.mult, op1=ALU.add)),
                  reads=[pbT[1 + n2], k.Xt[i], Trs[par]], writes=[k.Xt[i]])
        if h == 3:
            stat_q.append(lambda i=i: stat_tile(k, i, ring))
        if len(stat_q) > 2 or (stat_q and n == len(items) - 1):
            while len(stat_q) > (0 if n == len(items) - 1 else 2):
                stat_q.pop(0)()

    n_it = len(items)
    for step in range(n_it + 3):
        if 0 <= step - 3 < n_it:
            stage_c2(step - 3)
        if step < n_it:
            stage_a(step)
        if 0 <= step - 3 < n_it:
            stage_c2b(step - 3)
        if 0 <= step - 1 < n_it:
            stage_b2(step - 1)
        if 0 <= step - 2 < n_it:
            stage_c1(step - 2)
        if step < n_it:
            stage_b1(step)
            stage_a2(step)
    k.stats_ready = True


_PROG = {}


def kernel(**inputs):
    inp = {kk: np.asarray(v) for kk, v in inputs.items()}
    shared = _prep_shared(inp)
    if "nc" not in _PROG:
        _PROG["nc"] = build_program()[0]
    nc = _PROG["nc"]
    in_maps = []
    for c in range(N_CORES):
        m = dict(shared)
        m["x"] = np.ascontiguousarray(inp["x"][c * SEQ_PER_CORE:(c + 1) * SEQ_PER_CORE])
        m["p"] = np.ascontiguousarray(inp["p"][:, c * SEQ_PER_CORE:(c + 1) * SEQ_PER_CORE])
        in_maps.append(m)
    res = run_bass_kernel_spmd(nc, in_maps, core_ids=list(range(N_CORES)))
    out = np.concatenate([r["out"] for r in res.results], axis=0)
    return out.astype(np.float32, copy=False)
```

```python
import contextlib
import numpy as np
import concourse.bass as bass
import concourse.mybir as mybir
from concourse.bass_utils import run_bass_kernel_spmd

F32 = mybir.dt.float32
BF16 = mybir.dt.bfloat16
AF = mybir.ActivationFunctionType
ALU = mybir.AluOpType

D = 1024
SEQ = 2048
NT = 16
DFF = 2816
NFC = 22
RMS_EPS = 1e-6
LN_EPS = 1e-5
N_CORES = 8
SEQ_PER_CORE = 2


class T:
    __slots__ = ("name", "w", "r", "psum")

    def __init__(self, name="", psum=False):
        self.name = name
        self.w = None
        self.r = []
        self.psum = psum


class Op:
    __slots__ = ("eng", "seq", "fn", "waits", "chan", "key", "sig", "clock", "signal")

    def __init__(self, eng, seq, fn, chan):
        self.eng = eng
        self.seq = seq
        self.fn = fn
        self.chan = chan
        self.waits = []
        self.sig = None
        self.clock = None
        self.signal = False


class Sched:
    ENGS = ("pe", "act", "dve", "pool", "sp")

    def __init__(self):
        self.ops = {e: [] for e in self.ENGS}
        self.seen = {e: {} for e in self.ENGS}
        self.chan_count = {}
        self.chan_last = {}
        self.n_waits = 0

    def add(self, eng, fn, reads=(), writes=(), chan=None, extra=()):
        lst = self.ops[eng]
        op = Op(eng, len(lst), fn, chan)
        if chan is not None:
            c = self.chan_count.get(chan, 0) + 1
            self.chan_count[chan] = c
            op.key = ("c", chan)
            op.seq = c
            op.signal = True
            self.chan_last[chan] = op
        else:
            op.key = eng
        deps = {}
        for d in extra:
            deps[id(d)] = d
        for t in reads:
            if t.w is not None:
                deps[id(t.w)] = t.w
            if t.psum:
                for r in t.r:
                    if r.eng != eng:
                        deps[id(r)] = r
        for t in writes:
            if t.w is not None:
                deps[id(t.w)] = t.w
            for r in t.r:
                deps[id(r)] = r
        seen = self.seen[eng]
        for d in sorted(deps.values(), key=lambda o: -o.seq):
            if d is op:
                continue
            if d.chan is None and d.eng == "pe" and eng == "pe" and chan is None:
                continue
            need = d.seq if d.chan is not None else d.seq + 1
            if seen.get(d.key, 0) >= need:
                continue
            op.waits.append(d)
            d.signal = True
            self.n_waits += 1
            for k, v in d.clock.items():
                if seen.get(k, 0) < v:
                    seen[k] = v
        clk = dict(seen)
        clk[op.key] = op.seq if chan is not None else op.seq + 1
        op.clock = clk
        for t in reads:
            t.r.append(op)
        for t in writes:
            t.w = op
            t.r = []
        lst.append(op)
        return op

    def barrier(self):
        lasts = []
        for e in self.ENGS:
            for op in reversed(self.ops[e]):
                if op.chan is None and op.fn is not None:
                    lasts.append(op)
                    break
        lasts += list(self.chan_last.values())
        for e in self.ENGS:
            self.add(e, None, extra=lasts)

    def emit(self, nc):
        handles = {"pe": "tensor", "act": "scalar", "dve": "vector", "pool": "gpsimd", "sp": "sync"}
        with contextlib.ExitStack() as st:
            sems = {}
            for e in self.ENGS:
                sems[e] = st.enter_context(nc.semaphore("s_" + e))
            for c in self.chan_count:
                sems[("c", c)] = st.enter_context(nc.semaphore("c_" + str(c)))
            for e in self.ENGS:
                cnt = 0
                for op in self.ops[e]:
                    if op.chan is not None:
                        op.sig = 16 * op.seq
                    elif op.signal:
                        cnt += 1
                        op.sig = cnt
            block = st.enter_context(nc.Block())

            def make(e):
                def body(eng):
                    for op in self.ops[e]:
                        for d in op.waits:
                            eng.wait_ge(sems[d.key], d.sig)
                        if op.fn is None:
                            continue
                        ins = op.fn(eng)
                        if op.signal:
                            ins.then_inc(sems[op.key], 16 if op.chan is not None else 1)
                return body

            for e in self.ENGS:
                if self.ops[e]:
                    getattr(block, handles[e])(make(e))


def _const_tables():
    idx = np.arange(128)
    c = {}
    c["ident"] = np.eye(128)
    c["negL"] = -(idx[:, None] >= idx[None, :]).astype(np.float64)
    c["ones"] = np.ones((128, 128))
    c["mstrict"] = (idx[:, None] < idx[None, :]).astype(np.float64)
    c["mincl"] = (idx[:, None] <= idx[None, :]).astype(np.float64)
    lg = np.log(1.0 - 2.0 ** (-5.0 - np.arange(4)))
    dec = []
    for h in range(4):
        diff = idx[None, :] - idx[:, None]
        dec.append(np.where(diff >= 0, np.exp(diff * lg[h]), 0.0) / 16.0)
    c["decayT"] = np.concatenate(dec, axis=1)
    xi = np.exp((idx + 1.0)[None, :] * lg[:, None])
    c["xi"] = np.broadcast_to(np.repeat(xi, 2, axis=0).reshape(1, 1024), (128, 1024))
    zeta = np.exp((127 - idx)[:, None] * lg[None, :]) / 16.0
    c["zeta"] = zeta
    half = 128
    inv = 1.0 / (10000.0 ** (np.arange(half, dtype=np.float32) / half))
    ang = (np.arange(SEQ, dtype=np.float32)[None, :] * inv[:, None].astype(np.float32)).astype(np.float32)
    c["cos"] = np.cos(ang)
    c["sin"] = np.sin(ang)
    order = ["ident", "negL", "ones", "mstrict", "mincl", "decayT", "xi", "zeta", "cos", "sin"]
    offs = {}
    o = 0
    for k in order:
        offs[k] = (o, c[k].shape[1])
        o += c[k].shape[1]
    tab = np.concatenate([c[k] for k in order], axis=1).astype(np.float32)
    gam128 = [float(np.exp(128 * lg[h])) for h in range(4)]
    return tab, offs, gam128


CONST_TAB, CONST_OFF, GAM128 = _const_tables()

VFM = {}
_o = 0
for _name, _n in [("mix_g", 16), ("ffn_g", 16), ("ple_g", 16), ("conv_w", 2 * 3 * 44), ("conv_b", 2 * 44), ("sg_b", 4), ("gn_g", 16)]:
    VFM[_name] = _o
    _o += _n
NVFM = _o
VBC = {}
_o = 0
for _name, _n in [("post_g", 2048), ("final_g", 1024), ("ln_g", 512), ("ln_b", 512), ("gn_g", 2048)]:
    VBC[_name] = _o
    _o += _n
NVBC = _o


def _prep_shared(inp):
    f = np.float32
    vfm = np.zeros((128, NVFM), f)

    def fm(v):
        return np.ascontiguousarray(v.reshape(-1, 128).T)

    for l in range(2):
        vfm[:, VFM["mix_g"] + 8 * l: VFM["mix_g"] + 8 * l + 8] = fm(inp["mix_norm_g"][l])
        vfm[:, VFM["ffn_g"] + 8 * l: VFM["ffn_g"] + 8 * l + 8] = fm(inp["ffn_norm_g"][l])
        vfm[:, VFM["ple_g"] + 8 * l: VFM["ple_g"] + 8 * l + 8] = fm(inp["ple_norm_g"][l])
        for j in range(3):
            o = VFM["conv_w"] + (l * 3 + j) * 44
            vfm[:, o:o + 44] = fm(inp["ffn_conv_w"][l, j])
        o = VFM["conv_b"] + l * 44
        vfm[:, o:o + 44] = fm(inp["ffn_conv_b"][l])
    vfm[:, VFM["sg_b"]:VFM["sg_b"] + 4] = inp["sg_b"][0].T
    vfm[:, VFM["gn_g"]:VFM["gn_g"] + 16] = fm(inp["ret_gn_g"][0])
    vbc = np.zeros((128, NVBC), f)

    def bc(v):
        return np.broadcast_to(v[None, :], (128, v.shape[0]))

    for l in range(2):
        vbc[:, VBC["post_g"] + 1024 * l: VBC["post_g"] + 1024 * (l + 1)] = bc(inp["ple_post_g"][l])
    vbc[:, VBC["final_g"]:VBC["final_g"] + 1024] = bc(inp["final_norm_g"])
    vbc[:, VBC["ln_g"]:VBC["ln_g"] + 512] = bc(inp["sg_ln_g"][0])
    vbc[:, VBC["ln_b"]:VBC["ln_b"] + 512] = bc(inp["sg_ln_b"][0])
    vbc[:, VBC["gn_g"]:VBC["gn_g"] + 2048] = bc(inp["ret_gn_g"][0])
    sgwT = np.ascontiguousarray(np.transpose(inp["sg_w"][0], (2, 0, 1))).reshape(128, 512)
    shared = {
        "consts": CONST_TAB, "vfm": vfm, "vbc": vbc, "sgwT": sgwT.astype(f),
        "ab_w_in": np.ascontiguousarray(inp["ab_w_in"][0]), "ab_w_out": np.ascontiguousarray(inp["ab_w_out"][0]),
        "ret_w_in": np.ascontiguousarray(inp["ret_w_in"][0]), "ret_w_out": np.ascontiguousarray(inp["ret_w_out"][0]),
        "ffn_w_up": np.ascontiguousarray(inp["ffn_w_up"]), "ffn_w_down": np.ascontiguousarray(inp["ffn_w_down"]),
        "ple_w_gate": np.ascontiguousarray(inp["ple_w_gate"]), "ple_w_proj": np.ascontiguousarray(inp["ple_w_proj"]),
    }
    return shared


class K:
    pass


class Arena:
    def __init__(self, tensor, nbytes):
        self.t = tensor
        self.n = nbytes
        self.top = 0

    def alloc(self, nbytes):
        nbytes = (nbytes + 63) // 64 * 64
        o = self.top
        self.top += nbytes
        assert self.top <= self.n, ("arena overflow", self.top, self.n)
        return o

    def f32(self, off, n):
        return self.t[:, off // 4: off // 4 + n]

    def bf(self, off, n):
        return self.t[:, off // 4: off // 4 + (n + 1) // 2].bitcast(BF16)

    def new_f32(self, n):
        return self.f32(self.alloc(4 * n), n)

    def new_bf(self, n):
        return self.bf(self.alloc(2 * n), n)


def r3(ap, b):
    return ap.rearrange("p (a b) -> p a b", b=b)


def build_program(n_seq=SEQ_PER_CORE, stages=("mix0", "ffn0", "ple0", "mix1", "ffn1", "ple1")):
    nc = bass.Bass("TRN2", target_bir_lowering=False)
    k = K()
    k.nc = nc
    dt = lambda name, shape, kind="ExternalInput": nc.dram_tensor(name, shape, F32, kind=kind).ap()
    k.x_d = dt("x", [n_seq, SEQ, D])
    k.p_d = dt("p", [2, n_seq, SEQ, 256])
    k.consts_d = dt("consts", list(CONST_TAB.shape))
    k.vfm_d = dt("vfm", [128, NVFM])
    k.vbc_d = dt("vbc", [128, NVBC])
    k.sgwT_d = dt("sgwT", [128, 512])
    k.ab_w_in = dt("ab_w_in", [1024, 2560])
    k.ab_w_out = dt("ab_w_out", [1024, 1024])
    k.ret_w_in = dt("ret_w_in", [1024, 6144])
    k.ret_w_out = dt("ret_w_out", [2048, 1024])
    k.ffn_w_up = dt("ffn_w_up", [2, 1024, 5632])
    k.ffn_w_down = dt("ffn_w_down", [2, 2816, 1024])
    k.ple_w_gate = dt("ple_w_gate", [2, 1024, 1024])
    k.ple_w_proj = dt("ple_w_proj", [2, 256, 1024])
    k.out_d = dt("out", [n_seq, SEQ, D], kind="ExternalOutput")
    k.n_seq = n_seq
    k.stages = stages

    with contextlib.ExitStack() as st:
        ARENA_BYTES = 212736
        at = st.enter_context(nc.sbuf_tensor("arena", [128, ARENA_BYTES // 4], F32))
        k.A = Arena(at, ARENA_BYTES)
        k.pbig = st.enter_context(nc.psum_tensor("pbig", [128, 4096], F32))
        k.pb = [k.pbig[:, i * 512:(i + 1) * 512] for i in range(8)]
        k.S = Sched()
        _emit_all(k)
        k.S.emit(nc)
    k.nc = nc
    return nc, k


def cslice(name):
    o, n = CONST_OFF[name]
    return slice(o, o + n)


def _emit_all(k):
    S, A, nc = k.S, k.A, k.nc
    k.X = r3(A.new_f32(NT * D), D)
    k.Xt = [T("X%d" % i) for i in range(NT)]
    k.ident = A.new_bf(128)
    k.negL = A.new_bf(128)
    k.ones = A.new_bf(128)
    k.mstrict = A.new_bf(128)
    k.vfm = A.new_f32(NVFM)
    k.ss = A.new_f32(16)
    k.lnv = A.new_f32(16)
    k.rstd = A.new_f32(16)
    k.neghalf = A.new_f32(16)[:, 0:1]
    k.Tconst = T("const")
    k.eps_ln = LN_EPS
    k.one_b = 1.0
    k.epsr = RMS_EPS
    k.Tss = [T() for _ in range(16)]
    k.Tstat = T()
    k.persist_top = A.top
    k.pbT = [T("pb%d" % i, psum=True) for i in range(8)]
    def load_x_tile(s, i):
        S.add("sp", (lambda e: e.dma_start(out=k.X[:, i, :], in_=k.x_d[s, i * 128:(i + 1) * 128, :])), writes=[k.Xt[i]], chan="xin%d" % i)
    k.load_x_tile = load_x_tile
    for i in range(NT):
        load_x_tile(0, i)
    for name, dst in [("ident", k.ident), ("negL", k.negL), ("ones", k.ones), ("mstrict", k.mstrict)]:
        S.add("pool", (lambda e, dst=dst, name=name: e.dma_start(out=dst, in_=k.consts_d[:, cslice(name)])),
              writes=[k.Tconst], chan="const")
    S.add("sp", lambda e: e.dma_start(out=k.vfm, in_=k.vfm_d), writes=[k.Tconst], chan="constsp")
    S.add("pool", lambda e: e.memset(k.neghalf, -0.5), writes=[k.Tconst])

    out_ops = []
    def load_x_tile(s, i):
        S.add("sp", (lambda e: e.dma_start(out=k.X[:, i, :], in_=k.x_d[s, i * 128:(i + 1) * 128, :])), writes=[k.Xt[i]], chan="xin%d" % i)
    k.load_x_tile = load_x_tile
    for s in range(k.n_seq):
        k.next_seq = s + 1 if (s + 1 < k.n_seq and "ple1" in k.stages) else None
        if s > 0 and "ple1" not in k.stages:
            for i in range(NT):
                load_x_tile(s, i)
        if "mix0" in k.stages:
            phase_mix0(k, s)
        if "ffn0" in k.stages:
            phase_ffn(k, s, 0)
        if "ple0" in k.stages:
            phase_ple(k, s, 0, final=False)
        if "mix1" in k.stages:
            phase_mix1(k, s)
        if "ffn1" in k.stages:
            phase_ffn(k, s, 1)
        if "ple1" in k.stages:
            out_ops += phase_ple(k, s, 1, final=True)
        else:
            out_ops += phase_dump(k, s)
        S.barrier()
    S.add("sp", None, extra=out_ops)


def phase_dump(k, s):
    ops = []
    for i in range(NT):
        ops.append(k.S.add("sp", (lambda e, i=i: e.dma_start(out=k.out_d[s, i * 128:(i + 1) * 128, :], in_=k.X[:, i, :])),
                           reads=[k.Xt[i]], chan="xout"))
    return ops


def norm_hT(k, i, gcol, hn, hnT, tb, dst, dstT, slot):
    S = k.S
    X = k.X
    ss = k.ss[:, slot:slot + 1]
    lnv = k.lnv[:, slot:slot + 1]
    rstd = k.rstd[:, slot:slot + 1]
    tss = k.Tss[slot]
    S.add("act", lambda e: e.activation(out=hn, in_=X[:, i, :], func=AF.Square, accum_out=ss),
          reads=[k.Xt[i]], writes=[hnT, tss])
    S.add("act", lambda e: e.activation(out=lnv, in_=ss, func=AF.Ln, scale=1.0 / D, bias=k.epsr), reads=[tss, k.Tconst], writes=[tss])
    S.add("act", lambda e: e.activation(out=rstd, in_=lnv, func=AF.Exp, scale=-0.5), reads=[tss], writes=[tss])
    S.add("act", lambda e: e.activation(out=hn, in_=X[:, i, :], func=AF.Copy, scale=rstd),
          reads=[k.Xt[i], tss], writes=[hnT])
    pbv = k.pb[tb][:].bitcast(BF16)

    def tr(e):
        for c in range(8):
            ins = e.transpose(out=pbv[:, c * 128:(c + 1) * 128], in_=hn[:, c * 128:(c + 1) * 128], identity=k.ident)
        return ins
    S.add("pe", tr, reads=[hnT, k.Tconst], writes=[k.pbT[tb]])
    g = k.vfm[:, gcol:gcol + 8].unsqueeze(2).to_broadcast([128, 8, 128])
    S.add("dve", lambda e: e.tensor_tensor(out=dst, in0=r3(pbv, 128), in1=g, op=ALU.mult),
          reads=[k.pbT[tb], k.Tconst], writes=[dstT])


class JunkRing:
    def __init__(self, bufs):
        self.bufs = bufs
        self.n = 0

    def nxt(self):
        b = self.bufs[self.n % len(self.bufs)]
        self.n += 1
        return b


def norm_stats_all(k, ring):
    S = k.S
    if not getattr(k, "stats_ready", False):
        for i in range(NT):
            jb, jt = ring.nxt()
            S.add("act", (lambda e, i=i, jb=jb: e.activation(out=jb, in_=k.X[:, i, :], func=AF.Square, accum_out=k.ss[:, i:i + 1])),
                  reads=[k.Xt[i]] + ([k.Tstat] if i == 0 else []), writes=[k.Tss[i], jt])
    k.stats_ready = False
    S.add("act", lambda e: e.activation(out=k.lnv[:, 0:16], in_=k.ss[:, 0:16], func=AF.Ln, scale=1.0 / D, bias=k.epsr), reads=list(k.Tss), writes=[k.Tstat])
    S.add("act", lambda e: e.activation(out=k.rstd[:, 0:16], in_=k.lnv[:, 0:16], func=AF.Exp, scale=-0.5), reads=[k.Tstat], writes=[k.Tstat])


def stat_tile(k, i, ring):
    jb, jt = ring.nxt()
    k.S.add("act", lambda e: e.activation(out=jb, in_=k.X[:, i, :], func=AF.Square, accum_out=k.ss[:, i:i + 1]),
            reads=[k.Xt[i]], writes=[k.Tss[i], jt])


def norm_hT2(k, i, gcol, hn, hnT, tb, dst, dstT):
    S = k.S
    S.add("act", lambda e: e.activation(out=hn, in_=k.X[:, i, :], func=AF.Copy, scale=k.rstd[:, i:i + 1]),
          reads=[k.Xt[i], k.Tstat], writes=[hnT])
    pbv = k.pb[tb][:].bitcast(BF16)

    def tr(e):
        for c in range(8):
            ins = e.transpose(out=pbv[:, c * 128:(c + 1) * 128], in_=hn[:, c * 128:(c + 1) * 128], identity=k.ident)
        return ins
    S.add("pe", tr, reads=[hnT, k.Tconst], writes=[k.pbT[tb]])
    g = k.vfm[:, gcol:gcol + 8].unsqueeze(2).to_broadcast([128, 8, 128])
    S.add("dve", lambda e: e.tensor_tensor(out=dst, in0=r3(pbv, 128), in1=g, op=ALU.mult),
          reads=[k.pbT[tb], k.Tconst], writes=[dstT])


def load_w_slabs(k, dst, src, ncols, slab, chan, Ts, eng="pool"):
    srcv = src.rearrange("(kc p) n -> p kc n", p=128)
    for j, c0 in enumerate(range(0, ncols, slab)):
        c1 = min(ncols, c0 + slab)
        k.S.add(eng, (lambda e, c0=c0, c1=c1: e.dma_start(out=dst[:, :, c0:c1], in_=srcv[:, :, c0:c1])),
                writes=[Ts[j]], chan=chan)


def phase_mix0_old(k, s):
    S, A, nc = k.S, k.A, k.nc
    S.barrier()
    A.top = k.persist_top
    pb, pbT = k.pb, k.pbT
    Win = r3(A.new_bf(8 * 2560), 2560)
    Wout = r3(A.new_bf(8 * 1024), 1024)
    KT = r3(A.new_bf(4 * 2048), 2048)
    V = r3(A.new_bf(16 * 512), 512)
    SGW = A.new_bf(512)
    sgw32 = A.new_f32(512)
    mincl = A.new_f32(128)
    lng = A.new_f32(512)
    lnb = A.new_f32(512)
    hn = A.new_bf(1024)
    hT = [r3(A.new_bf(1024), 128) for _ in range(2)]
    QA = [r3(A.new_bf(512), 128) for _ in range(2)]
    QB = [r3(A.new_bf(512), 128) for _ in range(2)]
    ug = A.new_f32(512)
    vg = A.new_f32(512)
    vn32 = A.new_f32(512)
    vnb = A.new_bf(512)
    stats = A.new_f32(24)
    mv = A.new_f32(8)
    lrs = A.new_f32(4)
    lrl = A.new_f32(4)
    nmr = A.new_f32(4)
    CAT = A.new_bf(1024)
    CATT = r3(A.new_bf(1024), 128)
    E32 = [A.new_f32(512) for _ in range(2)]
    SP = [A.new_bf(512) for _ in range(2)]
    TMP = [A.new_f32(512) for _ in range(2)]
    AW = [A.new_bf(512) for _ in range(2)]
    TOT = [A.new_f32(512) for _ in range(2)]
    TWin = [T() for _ in range(5)]
    TWout = [T() for _ in range(2)]
    Tmisc = T()
    TSGW = T()
    Thn = T()
    ThT = [T(), T()]
    TQ = [T(), T()]
    TKT = [T() for _ in range(NT)]
    TV = [T() for _ in range(NT)]
    Tug, Tvg, Tvn32, Tvnb, Tst = T(), T(), T(), T(), T()
    TCATa, TCATb, TCATT = T(), T(), T()
    TE, TSP, TTMP, TAW = [T(), T()], [T(), T()], [T(), T()], [T(), T()]
    TTOT = [T(), T()]
    k.Tss = [T() for _ in range(16)]
    k.epsr = RMS_EPS

    load_w_slabs(k, Win, k.ab_w_in, 2560, 512, "w0", TWin)
    load_w_slabs(k, Wout, k.ab_w_out, 1024, 512, "w1", TWout)
    S.add("sp", lambda e: e.dma_start(out=sgw32, in_=k.sgwT_d), writes=[Tmisc], chan="misc")
    S.add("sp", lambda e: e.dma_start(out=mincl, in_=k.consts_d[:, cslice("mincl")]), writes=[Tmisc], chan="misc")
    S.add("sp", lambda e: e.dma_start(out=lng, in_=k.vbc_d[:, VBC["ln_g"]:VBC["ln_g"] + 512]), writes=[Tmisc], chan="misc")
    S.add("sp", lambda e: e.dma_start(out=lnb, in_=k.vbc_d[:, VBC["ln_b"]:VBC["ln_b"] + 512]), writes=[Tmisc], chan="misc")
    S.add("dve", lambda e: e.tensor_tensor(out=r3(SGW, 128), in0=r3(sgw32, 128),
                                           in1=mincl.unsqueeze(1).to_broadcast([128, 4, 128]), op=ALU.mult),
          reads=[Tmisc], writes=[TSGW])
    for b in range(2):
        S.add("dve", (lambda e, b=b: e.memset(QA[b][64:128, :, :], 0.0)), writes=[TQ[b]])
        S.add("dve", (lambda e, b=b: e.memset(QB[b][0:64, :, :], 0.0)), writes=[TQ[b]])
    sgb = k.vfm[:, VFM["sg_b"]:VFM["sg_b"] + 4]
    gcol = VFM["mix_g"]
    mstr_b = k.mstrict.unsqueeze(1).to_broadcast([128, 4, 128])

    norm_hT(k, 0, gcol, hn, Thn, 7, hT[0], ThT[0], 0)
    for i in range(NT):
        cur = i % 2
        h_ = hT[cur]
        ts = slice(i * 128, (i + 1) * 128)

        def proj_uv(e, h_=h_):
            for (b, c0) in ((0, 0), (1, 512)):
                for kc in range(8):
                    ins = e.matmul(pb[b][:], lhsT=h_[:, kc, :], rhs=Win[:, kc, c0:c0 + 512], start=(kc == 0), stop=(kc == 7))
            return ins
        S.add("pe", proj_uv, reads=[ThT[cur]] + TWin, writes=[pbT[0], pbT[1]])

        def proj_qk(e, h_=h_):
            for (b, c0) in ((2, 1024), (3, 1536)):
                for c in range(4):
                    for kc in range(8):
                        ins = e.matmul(pb[b][:, c * 128:(c + 1) * 128], lhsT=Win[:, kc, c0 + c * 128:c0 + (c + 1) * 128],
                                       rhs=h_[:, kc, :], start=(kc == 0), stop=(kc == 7))
            return ins
        S.add("pe", proj_qk, reads=[ThT[cur]] + TWin, writes=[pbT[2], pbT[3]])

        def proj_v(e, h_=h_):
            for kc in range(8):
                ins = e.matmul(pb[4][:], lhsT=h_[:, kc, :], rhs=Win[:, kc, 2048:2560], start=(kc == 0), stop=(kc == 7))
            return ins
        S.add("pe", proj_v, reads=[ThT[cur]] + TWin, writes=[pbT[4]])

        S.add("act", lambda e: e.activation(out=ug, in_=pb[0][:], func=AF.Gelu), reads=[pbT[0]], writes=[Tug])
        S.add("act", lambda e: e.activation(out=vg, in_=pb[1][:], func=AF.Gelu), reads=[pbT[1]], writes=[Tvg])
        S.add("act", (lambda e, cur=cur: e.activation(out=QA[cur][0:64, :, :], in_=r3(pb[2][0:64, :], 128), func=AF.Copy, scale=0.125)),
              reads=[pbT[2]], writes=[TQ[cur]])
        S.add("act", (lambda e, cur=cur: e.activation(out=QB[cur][64:128, :, :], in_=r3(pb[2][64:128, :], 128), func=AF.Copy, scale=0.125)),
              reads=[pbT[2]], writes=[TQ[cur]])
        S.add("dve", (lambda e, ts=ts: e.tensor_copy(out=KT[:, :, ts], in_=r3(pb[3][:], 128))), reads=[pbT[3]], writes=[TKT[i]])
        S.add("dve", (lambda e, i=i: e.tensor_copy(out=V[:, i, :], in_=pb[4][:])), reads=[pbT[4]], writes=[TV[i]])

        def bns(e):
            for g in range(4):
                ins = e.bn_stats(out=stats[:, g * 6:(g + 1) * 6], in_=vg[:, g * 128:(g + 1) * 128])
            return ins
        S.add("dve", bns, reads=[Tvg], writes=[Tst])

        def bna(e):
            for g in range(4):
                ins = e.bn_aggr(out=mv[:, 2 * g:2 * g + 2], in_=stats[:, g * 6:(g + 1) * 6])
            return ins
        S.add("dve", bna, reads=[Tst], writes=[Tst])
        mvv = mv.rearrange("p (g two) -> p g two", two=2)
        S.add("act", lambda e: e.activation(out=lrl, in_=mvv[:, :, 1], func=AF.Ln, bias=k.eps_ln), reads=[Tst, k.Tconst], writes=[Tst])
        S.add("act", lambda e: e.activation(out=lrs, in_=lrl, func=AF.Exp, scale=-0.5), reads=[Tst], writes=[Tst])
        S.add("dve", lambda e: e.scalar_tensor_tensor(out=nmr, in0=mvv[:, :, 0], scalar=-1.0, in1=lrs, op0=ALU.mult, op1=ALU.mult),
              reads=[Tst], writes=[Tst])

        def nrm(e):
            for g in range(4):
                ins = e.activation(out=vn32[:, g * 128:(g + 1) * 128], in_=vg[:, g * 128:(g + 1) * 128], func=AF.Identity,
                                   scale=lrs[:, g:g + 1], bias=nmr[:, g:g + 1])
            return ins
        S.add("act", nrm, reads=[Tvg, Tst], writes=[Tvn32])
        S.add("pool", lambda e: e.tensor_tensor(out=vn32, in0=vn32, in1=lng, op=ALU.mult), reads=[Tvn32, Tmisc], writes=[Tvn32])
        S.add("pool", lambda e: e.tensor_tensor(out=vnb, in0=vn32, in1=lnb, op=ALU.add), reads=[Tvn32, Tmisc], writes=[Tvnb])

        def mixmm(e):
            for g in range(4):
                ins = e.matmul(pb[5][:, g * 128:(g + 1) * 128], lhsT=SGW[:, g * 128:(g + 1) * 128], rhs=vnb[:, g * 128:(g + 1) * 128],
                               start=True, stop=True)
            return ins
        S.add("pe", mixmm, reads=[TSGW, Tvnb], writes=[pbT[5]])

        def aout(e):
            for g in range(4):
                ins = e.scalar_tensor_tensor(out=CAT[:, g * 128:(g + 1) * 128], in0=pb[5][:, g * 128:(g + 1) * 128],
                                             scalar=sgb[:, g:g + 1], in1=ug[:, g * 128:(g + 1) * 128], op0=ALU.add, op1=ALU.mult)
            return ins
        S.add("dve", aout, reads=[pbT[5], Tug, k.Tconst], writes=[TCATa])

        if i + 1 < NT:
            norm_hT(k, i + 1, gcol, hn, Thn, 7, hT[1 - cur], ThT[1 - cur], (i + 1) % 16)

        items = [(j, hf) for j in range(i, -1, -1) for hf in range(2)]
        n_it = len(items)

        def stage_a(n, cur=cur, items=items, i=i):
            j, hf = items[n]
            par = n % 2
            zb = par
            ks = slice(j * 128, (j + 1) * 128)

            def zmm(e):
                for hh in range(4):
                    h = 4 * hf + hh
                    c = h // 2
                    q = QA[cur] if h % 2 == 0 else QB[cur]
                    ins = e.matmul(pb[zb][:, hh * 128:(hh + 1) * 128], lhsT=KT[:, c, ks], rhs=q[:, c, :], start=True, stop=True)
                return ins
            S.add("pe", zmm, reads=[TKT[j], TQ[cur]], writes=[pbT[zb]])
            S.add("act", lambda e: e.activation(out=E32[par], in_=pb[zb][:], func=AF.Exp), reads=[pbT[zb]], writes=[TE[par]])
            S.add("act", lambda e: e.activation(out=SP[par], in_=E32[par], func=AF.Ln, bias=k.one_b), reads=[TE[par], k.Tconst], writes=[TSP[par]])
            if j == i:
                S.add("dve", lambda e: e.tensor_tensor(out=r3(SP[par], 128), in0=r3(SP[par], 128), in1=mstr_b, op=ALU.mult),
                      reads=[TSP[par], k.Tconst], writes=[TSP[par]])

        def stage_b(n, cur=cur, items=items, i=i):
            j, hf = items[n]
            par = n % 2
            wb = 2 + par
            tb = 4 + par
            ks = slice(j * 128, (j + 1) * 128)

            def wmm(e):
                for hh in range(4):
                    h = 4 * hf + hh
                    c = h // 2
                    q = QA[cur] if h % 2 == 0 else QB[cur]
                    e.matmul(pb[wb][:, hh * 128:(hh + 1) * 128], lhsT=KT[:, c, ks], rhs=q[:, c, :], start=(hh == 0), stop=False,
                             skip_group_check=True)
                ins = e.matmul(pb[wb][:], lhsT=k.negL, rhs=SP[par], start=False, stop=True, skip_group_check=True)
                return ins
            S.add("pe", wmm, reads=[TKT[j], TQ[cur], TSP[par], k.Tconst], writes=[pbT[wb]])
            if j > 0:
                S.add("pe", lambda e: e.matmul(pb[tb][:], lhsT=k.ones, rhs=SP[par], start=True, stop=True),
                      reads=[TSP[par], k.Tconst], writes=[pbT[tb]])
            if j < i:
                S.add("dve", lambda e: e.tensor_tensor(out=TMP[par], in0=pb[wb][:], in1=TOT[hf], op=ALU.subtract),
                      reads=[pbT[wb], TTOT[hf]], writes=[TTMP[par]])
                S.add("act", lambda e: e.activation(out=AW[par], in_=TMP[par], func=AF.Exp), reads=[TTMP[par]], writes=[TAW[par]])
            else:
                S.add("act", lambda e: e.activation(out=AW[par], in_=pb[wb][:], func=AF.Exp), reads=[pbT[wb]], writes=[TAW[par]])
                S.add("dve", lambda e: e.tensor_tensor(out=r3(AW[par], 128), in0=r3(AW[par], 128), in1=mstr_b, op=ALU.mult),
                      reads=[TAW[par], k.Tconst], writes=[TAW[par]])
            if j > 0:
                if j == i:
                    S.add("dve", lambda e: e.tensor_copy(out=TOT[hf], in_=pb[tb][:]), reads=[pbT[tb]], writes=[TTOT[hf]])
                else:
                    S.add("dve", lambda e: e.tensor_tensor(out=TOT[hf], in0=pb[tb][:], in1=TOT[hf], op=ALU.add),
                          reads=[pbT[tb], TTOT[hf]], writes=[TTOT[hf]])

        def stage_c(n, cur=cur, items=items, n_it=n_it):
            j, hf = items[n]
            par = n % 2

            def av(e):
                for hh in range(4):
                    h = 4 * hf + hh
                    ins = e.matmul(pb[6][:, h * 64:(h + 1) * 64], lhsT=AW[par][:, hh * 128:(hh + 1) * 128],
                                   rhs=V[:, j, h * 64:(h + 1) * 64], start=(n == 0 and hh == 0), stop=(n == n_it - 1 and hh == 3),
                                   skip_group_check=True)
                return ins
            S.add("pe", av, reads=[TAW[par], TV[j]], writes=[pbT[6]])

        for step in range(n_it + 2):
            if step < n_it:
                stage_a(step)
            if 0 <= step - 1 < n_it:
                stage_b(step - 1)
            if 0 <= step - 2 < n_it:
                stage_c(step - 2)
        S.add("act", lambda e: e.activation(out=CAT[:, 512:1024], in_=pb[6][:], func=AF.Copy), reads=[pbT[6]], writes=[TCATb])

        pbv = pb[7][:].bitcast(BF16)

        def trc(e):
            for c in range(8):
                ins = e.transpose(out=pbv[:, c * 128:(c + 1) * 128], in_=CAT[:, c * 128:(c + 1) * 128], identity=k.ident)
            return ins
        S.add("pe", trc, reads=[TCATa, TCATb, k.Tconst], writes=[pbT[7]])
        S.add("dve", lambda e: e.tensor_copy(out=CATT, in_=r3(pbv, 128)), reads=[pbT[7]], writes=[TCATT])

        def womm(e):
            for n2 in range(2):
                for kc in range(8):
                    ins = e.matmul(pb[n2][:], lhsT=CATT[:, kc, :], rhs=Wout[:, kc, n2 * 512:(n2 + 1) * 512], start=(kc == 0), stop=(kc == 7))
            return ins
        S.add("pe", womm, reads=[TCATT] + TWout, writes=[pbT[0], pbT[1]])
        for n2 in range(2):
            S.add("dve", (lambda e, n2=n2, i=i: e.tensor_tensor(out=k.X[:, i, n2 * 512:(n2 + 1) * 512], in0=pb[n2][:],
                                                                 in1=k.X[:, i, n2 * 512:(n2 + 1) * 512], op=ALU.add)),
                  reads=[pbT[n2], k.Xt[i]], writes=[k.Xt[i]])


def phase_mix0(k, s):
    S, A = k.S, k.A
    if s > 0:
        S.barrier()
    A.top = k.persist_top
    pb, pbT = k.pb, k.pbT
    KT = r3(A.new_bf(4 * 2048), 2048)
    V = r3(A.new_bf(16 * 512), 512)
    QT = r3(A.new_bf(4 * 2048), 2048)
    CATa = r3(A.new_bf(16 * 512), 512)
    TKT = [T() for _ in range(NT)]
    TV = [T() for _ in range(NT)]
    TQT = [T() for _ in range(NT)]
    TCATa = [T() for _ in range(NT)]
    sub_top = A.top
    Win = r3(A.new_bf(8 * 2560), 2560)
    SGW = A.new_bf(512)
    sgw32 = A.new_f32(512)
    mincl = A.new_f32(128)
    lng = A.new_f32(512)
    lnb = A.new_f32(512)
    hn = [A.new_bf(1024) for _ in range(2)]
    junk = A.new_bf(1024)
    junk2 = A.new_bf(1024)
    hT = [r3(A.new_bf(1024), 128) for _ in range(2)]
    ug = [A.new_f32(512) for _ in range(2)]
    vg = [A.new_f32(512) for _ in range(2)]
    vn32 = [A.new_f32(512) for _ in range(2)]
    vnb = [A.new_bf(512) for _ in range(2)]
    stats = [A.new_f32(24) for _ in range(2)]
    mv = [A.new_f32(8) for _ in range(2)]
    sm = [A.new_f32(12) for _ in range(2)]
    TWin = [T() for _ in range(5)]
    Tmisc, TSGW, Tjunk = T(), T(), T()
    Thn, ThT, Tug, Tvg, Tvn32, Tvnb, Tst = ([T(), T()] for _ in range(7))
    load_w_slabs(k, Win, k.ab_w_in, 2560, 512, "w0", TWin)
    S.add("sp", lambda e: e.dma_start(out=sgw32, in_=k.sgwT_d), writes=[Tmisc], chan="misc")
    S.add("sp", lambda e: e.dma_start(out=mincl, in_=k.consts_d[:, cslice("mincl")]), writes=[Tmisc], chan="misc")
    S.add("sp", lambda e: e.dma_start(out=lng, in_=k.vbc_d[:, VBC["ln_g"]:VBC["ln_g"] + 512]), writes=[Tmisc], chan="misc")
    S.add("sp", lambda e: e.dma_start(out=lnb, in_=k.vbc_d[:, VBC["ln_b"]:VBC["ln_b"] + 512]), writes=[Tmisc], chan="misc")
    S.add("dve", lambda e: e.tensor_tensor(out=r3(SGW, 128), in0=r3(sgw32, 128),
                                           in1=mincl.unsqueeze(1).to_broadcast([128, 4, 128]), op=ALU.mult),
          reads=[Tmisc], writes=[TSGW])
    sgb = k.vfm[:, VFM["sg_b"]:VFM["sg_b"] + 4]
    gcol = VFM["mix_g"]
    ring = JunkRing([(junk, Tjunk), (junk2, T())])
    norm_stats_all(k, ring)
    tails = []
    for i in range(NT):
        par = i % 2
        ts = slice(i * 128, (i + 1) * 128)
        if i == 0:
            norm_hT2(k, 0, gcol, hn[0], Thn[0], 7, hT[0], ThT[0])
        h_ = hT[par]

        def proj_uv(e, h_=h_):
            for (b, c0) in ((0, 0), (1, 512)):
                for kc in range(8):
                    ins = e.matmul(pb[b][:], lhsT=h_[:, kc, :], rhs=Win[:, kc, c0:c0 + 512], start=(kc == 0), stop=(kc == 7))
            return ins
        S.add("pe", proj_uv, reads=[ThT[par]] + TWin, writes=[pbT[0], pbT[1]])

        def proj_qk(e, h_=h_):
            for (b, c0) in ((2, 1024), (3, 1536)):
                for c in range(4):
                    for kc in range(8):
                        ins = e.matmul(pb[b][:, c * 128:(c + 1) * 128], lhsT=Win[:, kc, c0 + c * 128:c0 + (c + 1) * 128],
                                       rhs=h_[:, kc, :], start=(kc == 0), stop=(kc == 7))
            return ins
        S.add("pe", proj_qk, reads=[ThT[par]] + TWin, writes=[pbT[2], pbT[3]])

        def proj_v(e, h_=h_):
            for kc in range(8):
                ins = e.matmul(pb[4][:], lhsT=h_[:, kc, :], rhs=Win[:, kc, 2048:2560], start=(kc == 0), stop=(kc == 7))
            return ins
        S.add("pe", proj_v, reads=[ThT[par]] + TWin, writes=[pbT[4]])
        tail_dve = None
        if tails:
            tail_dve = tails.pop(0)()
        if i + 1 < NT:
            norm_hT2(k, i + 1, gcol, hn[1 - par], Thn[1 - par], 7, hT[1 - par], ThT[1 - par])
        if tail_dve is not None:
            tail_dve()
        S.add("act", (lambda e, par=par: e.activation(out=ug[par], in_=pb[0][:], func=AF.Gelu)), reads=[pbT[0]], writes=[Tug[par]])
        S.add("act", (lambda e, par=par: e.activation(out=vg[par], in_=pb[1][:], func=AF.Gelu)), reads=[pbT[1]], writes=[Tvg[par]])
        S.add("act", (lambda e, ts=ts: e.activation(out=QT[:, :, ts], in_=r3(pb[2][:], 128), func=AF.Copy, scale=0.125)), reads=[pbT[2]], writes=[TQT[i]])
        S.add("dve", (lambda e, ts=ts: e.tensor_copy(out=KT[:, :, ts], in_=r3(pb[3][:], 128))), reads=[pbT[3]], writes=[TKT[i]])
        S.add("dve", (lambda e, i=i: e.tensor_copy(out=V[:, i, :], in_=pb[4][:])), reads=[pbT[4]], writes=[TV[i]])
        st_, mv_, sm_ = stats[par], mv[par], sm[par]
        vg_, vn_, vb_, ug_ = vg[par], vn32[par], vnb[par], ug[par]

        def bns(e, st_=st_, vg_=vg_):
            for g in range(4):
                ins = e.bn_stats(out=st_[:, g * 6:(g + 1) * 6], in_=vg_[:, g * 128:(g + 1) * 128])
            return ins
        S.add("dve", bns, reads=[Tvg[par]], writes=[Tst[par]])

        def bna(e, st_=st_, mv_=mv_):
            for g in range(4):
                ins = e.bn_aggr(out=mv_[:, 2 * g:2 * g + 2], in_=st_[:, g * 6:(g + 1) * 6])
            return ins
        S.add("dve", bna, reads=[Tst[par]], writes=[Tst[par]])
        mvv = mv_.rearrange("p (g two) -> p g two", two=2)
        S.add("dve", (lambda e, sm_=sm_, mvv=mvv: e.tensor_scalar(out=sm_[:, 0:4], in0=mvv[:, :, 1], scalar1=LN_EPS, scalar2=None, op0=ALU.add)),
              reads=[Tst[par]], writes=[Tst[par]])
        S.add("pool", (lambda e, sm_=sm_: e.tensor_tensor(out=sm_[:, 4:8], in0=sm_[:, 0:4], in1=k.neghalf.to_broadcast([128, 4]), op=ALU.pow)),
              reads=[Tst[par], k.Tconst], writes=[Tst[par]])
        S.add("dve", (lambda e, sm_=sm_, mvv=mvv: e.scalar_tensor_tensor(out=sm_[:, 8:12], in0=mvv[:, :, 0], scalar=-1.0, in1=sm_[:, 4:8], op0=ALU.mult, op1=ALU.mult)),
              reads=[Tst[par]], writes=[Tst[par]])

        def nrm(e, sm_=sm_, vg_=vg_, vn_=vn_):
            for g in range(4):
                ins = e.tensor_scalar(out=vn_[:, g * 128:(g + 1) * 128], in0=vg_[:, g * 128:(g + 1) * 128], scalar1=sm_[:, 4 + g:5 + g],
                                      scalar2=sm_[:, 8 + g:9 + g], op0=ALU.mult, op1=ALU.add)
            return ins
        S.add("dve", nrm, reads=[Tvg[par], Tst[par]], writes=[Tvn32[par]])
        S.add("pool", (lambda e, vn_=vn_: e.tensor_tensor(out=vn_, in0=vn_, in1=lng, op=ALU.mult)), reads=[Tvn32[par], Tmisc], writes=[Tvn32[par]])
        S.add("pool", (lambda e, vn_=vn_, vb_=vb_: e.tensor_tensor(out=vb_, in0=vn_, in1=lnb, op=ALU.add)), reads=[Tvn32[par], Tmisc], writes=[Tvnb[par]])

        def tail(i=i, par=par, vb_=vb_, ug_=ug_):
            def mixmm(e):
                for g in range(4):
                    ins = e.matmul(pb[5][:, g * 128:(g + 1) * 128], lhsT=SGW[:, g * 128:(g + 1) * 128], rhs=vb_[:, g * 128:(g + 1) * 128],
                                   start=True, stop=True)
                return ins
            S.add("pe", mixmm, reads=[TSGW, Tvnb[par]], writes=[pbT[5]])

            def aout(e):
                for g in range(4):
                    ins = e.scalar_tensor_tensor(out=CATa[:, i, g * 128:(g + 1) * 128], in0=pb[5][:, g * 128:(g + 1) * 128],
                                                 scalar=sgb[:, g:g + 1], in1=ug_[:, g * 128:(g + 1) * 128], op0=ALU.add, op1=ALU.mult)
                return ins
            return lambda: S.add("dve", aout, reads=[pbT[5], Tug[par], k.Tconst], writes=[TCATa[i]])
        tails.append(tail)

    while tails:
        tails.pop(0)()()
    S.barrier()
    A.top = sub_top
    Wout = r3(A.new_bf(8 * 1024), 1024)
    E32 = [A.new_f32(1024) for _ in range(2)]
    SP = [A.new_bf(1024) for _ in range(2)]
    AW = [A.new_bf(1024) for _ in range(2)]
    R = [A.new_bf(1024) for _ in range(2)]
    QA = [r3(A.new_bf(512), 128) for _ in range(2)]
    QB = [r3(A.new_bf(512), 128) for _ in range(2)]
    CATb = [A.new_bf(512) for _ in range(2)]
    CATT = [r3(A.new_bf(1024), 128) for _ in range(2)]
    TWout = [T(), T()]
    TE, TSP, TAW, TR, TQ, TCATb, TCATT = ([T(), T()] for _ in range(7))
    ring2 = JunkRing([(A.new_f32(1024), T()), (A.new_f32(1024), T())])
    load_w_slabs(k, Wout, k.ab_w_out, 1024, 512, "w1", TWout)
    for b in range(2):
        S.add("pool", (lambda e, b=b: e.memset(QA[b][64:128, :, :], 0.0)), writes=[TQ[b]])
        S.add("pool", (lambda e, b=b: e.memset(QB[b][0:64, :, :], 0.0)), writes=[TQ[b]])
    mstr8 = k.mstrict.unsqueeze(1).to_broadcast([128, 8, 128])
    items = [(i, j) for i in range(NT) for j in range(i, -1, -1)]
    N = len(items)
    pbv5 = pb[5][:].bitcast(BF16)

    def build_q(i):
        tp = i % 2
        ts = slice(i * 128, (i + 1) * 128)
        S.add("pool", lambda e: e.tensor_copy(out=QA[tp][0:64, :, :], in_=QT[0:64, :, ts]), reads=[TQT[i]], writes=[TQ[tp]])
        S.add("pool", lambda e: e.tensor_copy(out=QB[tp][64:128, :, :], in_=QT[64:128, :, ts]), reads=[TQT[i]], writes=[TQ[tp]])

    def zmm(e, i, j, base, first_start):
        tp = i % 2
        ks = slice(j * 128, (j + 1) * 128)
        for h in range(8):
            c = h // 2
            q = QA[tp] if h % 2 == 0 else QB[tp]
            st = True if first_start is None else (h % 4 == 0)
            ins = e.matmul(pb[base + h // 4][:, (h % 4) * 128:(h % 4 + 1) * 128], lhsT=KT[:, c, ks], rhs=q[:, c, :], start=st,
                           stop=(first_start is None), skip_group_check=True)
        return ins

    def st_front(n):
        i, j = items[n]
        ip = n % 2
        S.add("pe", lambda e: zmm(e, i, j, 0, None), reads=[TKT[j], TQ[i % 2]], writes=[pbT[0], pbT[1]])
        S.add("act", lambda e: e.activation(out=E32[ip], in_=k.pbig[:, 0:1024], func=AF.Exp), reads=[pbT[0], pbT[1]], writes=[TE[ip]])
        S.add("act", lambda e: e.activation(out=SP[ip], in_=E32[ip], func=AF.Ln, bias=k.one_b), reads=[TE[ip]], writes=[TSP[ip]])
        if j == i:
            S.add("dve", lambda e: e.tensor_tensor(out=r3(SP[ip], 128), in0=r3(SP[ip], 128), in1=mstr8, op=ALU.mult),
                  reads=[TSP[ip], k.Tconst], writes=[TSP[ip]])

    def st_w(n):
        i, j = items[n]
        ip = n % 2
        tp = i % 2

        def wmm(e):
            zmm(e, i, j, 2, True)
            for b in range(2):
                ins = e.matmul(pb[2 + b][:], lhsT=k.negL, rhs=SP[ip][:, b * 512:(b + 1) * 512], start=False, stop=(j == i), skip_group_check=True)
                if j < i:
                    ins = e.matmul(pb[2 + b][:], lhsT=k.ones, rhs=R[tp][:, b * 512:(b + 1) * 512], start=False, stop=True, skip_group_check=True)
            return ins
        S.add("pe", wmm, reads=[TKT[j], TQ[tp], TSP[ip], k.Tconst] + ([TR[tp]] if j < i else []), writes=[pbT[2], pbT[3]])
        if j > 0:
            if j == i:
                S.add("dve", lambda e: e.tensor_scalar(out=R[tp], in0=SP[ip], scalar1=-1.0, scalar2=None, op0=ALU.mult), reads=[TSP[ip]], writes=[TR[tp]])
            else:
                S.add("dve", lambda e: e.tensor_tensor(out=R[tp], in0=R[tp], in1=SP[ip], op=ALU.subtract), reads=[TSP[ip], TR[tp]], writes=[TR[tp]])

    def st_exp2(n):
        i, j = items[n]
        ip = n % 2
        S.add("act", lambda e: e.activation(out=AW[ip], in_=k.pbig[:, 1024:2048], func=AF.Exp), reads=[pbT[2], pbT[3]], writes=[TAW[ip]])
        if j == i:
            S.add("dve", lambda e: e.tensor_tensor(out=r3(AW[ip], 128), in0=r3(AW[ip], 128), in1=mstr8, op=ALU.mult),
                  reads=[TAW[ip], k.Tconst], writes=[TAW[ip]])

    def st_av(n):
        i, j = items[n]
        ip = n % 2
        tp = i % 2

        def av(e):
            for h in range(8):
                ins = e.matmul(pb[4][:, h * 64:(h + 1) * 64], lhsT=AW[ip][:, h * 128:(h + 1) * 128], rhs=V[:, j, h * 64:(h + 1) * 64],
                               start=(j == i and h == 0), stop=(j == 0 and h == 7), skip_group_check=True)
            return ins
        S.add("pe", av, reads=[TAW[ip], TV[j]], writes=[pbT[4]])
        if j == 0:
            S.add("dve", lambda e: e.tensor_copy(out=CATb[tp], in_=pb[4][:]), reads=[pbT[4]], writes=[TCATb[tp]])

            def ch_tr():
                def trc(e):
                    for c in range(8):
                        src = CATa[:, i, c * 128:(c + 1) * 128] if c < 4 else CATb[tp][:, (c - 4) * 128:(c - 3) * 128]
                        ins = e.transpose(out=pbv5[:, c * 128:(c + 1) * 128], in_=src, identity=k.ident)
                    return ins
                S.add("pe", trc, reads=[TCATa[i], TCATb[tp], k.Tconst], writes=[pbT[5]])
                S.add("dve", lambda e: e.tensor_copy(out=CATT[tp], in_=r3(pbv5, 128)), reads=[pbT[5]], writes=[TCATT[tp]])
            deferred.append(ch_tr)
            for n2 in range(2):
                for hf in range(2):
                    def ch_wo(n2=n2, hf=hf):
                        def womm(e):
                            for kc in range(4 * hf, 4 * hf + 4):
                                ins = e.matmul(pb[6 + n2][:], lhsT=CATT[tp][:, kc, :], rhs=Wout[:, kc, n2 * 512:(n2 + 1) * 512], start=(kc == 0), stop=(kc == 7))
                            return ins
                        S.add("pe", womm, reads=[TCATT[tp]] + TWout, writes=[pbT[6 + n2]])
                        if hf == 1:
                            S.add("dve", lambda e: e.tensor_tensor(out=k.X[:, i, n2 * 512:(n2 + 1) * 512], in0=pb[6 + n2][:],
                                                                   in1=k.X[:, i, n2 * 512:(n2 + 1) * 512], op=ALU.add),
                                  reads=[pbT[6 + n2], k.Xt[i]], writes=[k.Xt[i]])
                            if n2 == 1:
                                jb, jt = ring2.nxt()
                                S.add("dve", lambda e: e.scalar_tensor_tensor(out=jb, in0=k.X[:, i, :], scalar=1.0, in1=k.X[:, i, :], op0=ALU.mult, op1=ALU.mult,
                                                                              accum_out=k.ss[:, i:i + 1]),
                                      reads=[k.Xt[i]], writes=[k.Tss[i], jt])
                    deferred.append(ch_wo)

    deferred = []
    build_q(0)
    build_q(1)
    st_front(0)
    for n in range(N + 1):
        if n < N:
            if n + 1 < N:
                st_front(n + 1)
            st_w(n)
        if deferred:
            deferred.pop(0)()
        if n - 1 >= 0:
            st_av(n - 1)
            ip_, jp_ = items[n - 1]
            if jp_ == 0 and ip_ + 2 < NT:
                build_q(ip_ + 2)
        if n < N:
            st_exp2(n)
    while deferred:
        deferred.pop(0)()
    k.stats_ready = True


def phase_ffn(k, s, l):
    S, A = k.S, k.A
    S.barrier()
    A.top = k.persist_top
    pb, pbT = k.pb, k.pbT
    HT = r3(A.new_bf(8 * 2050), 2050)
    NPC = [6, 6, 5, 5]
    PST = [0, 6, 12, 17]
    GT = [r3(A.new_bf(6 * 2048), 2048) for _ in range(2)]
    WDN = [r3(A.new_bf(6 * 1024), 1024) for _ in range(2)]
    NSLOT = 4
    WUP = [r3(A.new_bf(8 * 256), 256) for _ in range(NSLOT)]
    hn = A.new_bf(1024)
    junkf = A.new_bf(1024)
    TG = [A.new_f32(412) for _ in range(3)]
    GG = [A.new_f32(412) for _ in range(3)]
    TU = [A.new_f32(412) for _ in range(3)]
    Thn = T()
    THT = [T() for _ in range(NT)]
    Thalo = T()
    TGT = [[T() for _ in range(6)] for _ in range(2)]
    TWDN = [T(), T()]
    TWUPg = [T() for _ in range(NSLOT)]
    TWUPu = [T() for _ in range(NSLOT)]
    TTG, TGG, TTU = [T(), T(), T()], [T(), T(), T()], [T(), T(), T()]
    wup = k.ffn_w_up[l].rearrange("(kc p) n -> p kc n", p=128)
    wdn = k.ffn_w_down[l].rearrange("(c p) n -> p c n", p=128)
    bounds = [0, 410, 820, 1230, 1640, 2048]

    def load_up(c):
        sl = c % NSLOT
        S.add("pool", lambda e: e.dma_start(out=WUP[sl][:, :, 0:128], in_=wup[:, :, c * 128:(c + 1) * 128]),
              writes=[TWUPg[sl]], chan="wu%d" % sl)
        S.add("pool", lambda e: e.dma_start(out=WUP[sl][:, :, 128:256], in_=wup[:, :, DFF + c * 128:DFF + (c + 1) * 128]),
              writes=[TWUPu[sl]], chan="wu%d" % sl)

    def load_dn(p):
        pp = p % 2
        S.add("pool", lambda e: e.dma_start(out=WDN[pp][:, 0:NPC[p], :], in_=wdn[:, PST[p]:PST[p] + NPC[p], :]),
              writes=[TWDN[pp]], chan="wd%d" % pp)

    for c in range(NSLOT):
        load_up(c)
    load_dn(0)
    S.add("dve", lambda e: e.memset(HT[:, :, 0:2], 0.0), writes=[Thalo])
    gcol = VFM["ffn_g"] + 8 * l
    Tjunk = T()
    ring = JunkRing([(junkf, Tjunk), (hn, Thn)])
    norm_stats_all(k, ring)
    hn2 = [hn, junkf]
    Thn2 = [Thn, Tjunk]
    ht_next = [0]

    def make_ht(upto):
        while ht_next[0] <= min(upto, NT - 1):
            i = ht_next[0]
            norm_hT2(k, i, gcol, hn2[i % 2], Thn2[i % 2], 6 + (i % 2), HT[:, :, 2 + i * 128:2 + (i + 1) * 128], THT[i])
            ht_next[0] += 1
    make_ht(3)

    cw = lambda j, idx: k.vfm[:, VFM["conv_w"] + (l * 3 + j) * 44 + idx: VFM["conv_w"] + (l * 3 + j) * 44 + idx + 1]
    cb = lambda idx: k.vfm[:, VFM["conv_b"] + l * 44 + idx: VFM["conv_b"] + l * 44 + idx + 1]
    item = 0
    pending = []

    def down_unit(m, pp, p):
        if p == 3:
            b0, b1 = 2 * (m % 4), 2 * (m % 4) + 1
        else:
            b0, b1 = 6, 7
        for (b, n2) in ((b0, 0), (b1, 1)):
            def dnmm(e, b=b, n2=n2):
                for cc in range(NPC[p]):
                    ins = e.matmul(pb[b][:], lhsT=GT[pp][:, cc, m * 128:(m + 1) * 128], rhs=WDN[pp][:, cc, n2 * 512:(n2 + 1) * 512],
                                   start=(cc == 0), stop=(cc == NPC[p] - 1))
                return ins
            S.add("pe", dnmm, reads=[TWDN[pp]] + TGT[pp][:NPC[p]], writes=[pbT[b]])
        for (b, n2) in ((b0, 0), (b1, 1)):
            S.add("dve", (lambda e, b=b, n2=n2: e.tensor_tensor(out=k.X[:, m, n2 * 512:(n2 + 1) * 512], in0=pb[b][:],
                                                                 in1=k.X[:, m, n2 * 512:(n2 + 1) * 512], op=ALU.add)),
                  reads=[pbT[b], k.Xt[m]], writes=[k.Xt[m]])
        if p == 3:
            stat_tile(k, m, ring)

    for p in range(4):
        pp = p % 2
        for cc in range(NPC[p]):
            c = PST[p] + cc
            sl = c % NSLOT
            for tt in range(5):
                t0, t1 = bounds[tt], bounds[tt + 1]
                n = t1 - t0
                par = item % 3
                item += 1
                if pending and item % 2 == 0:
                    pending.pop(0)()
                gb, ub = 2 * par, 2 * par + 1
                tiles = sorted(set([max(t0 - 2, 0) // 128, (t1 - 1) // 128] + list(range(t0 // 128, (t1 - 1) // 128 + 1))))
                make_ht(tiles[-1] + 3)

                def upmm(e, sl=sl, t0=t0, n=n, gb=gb, ub=ub):
                    for (b, o) in ((gb, 0), (ub, 128)):
                        for kc in range(8):
                            ins = e.matmul(pb[b][:, 0:n + 2], lhsT=WUP[sl][:, kc, o:o + 128], rhs=HT[:, kc, t0:t0 + n + 2],
                                           start=(kc == 0), stop=(kc == 7))
                    return ins
                S.add("pe", upmm, reads=[TWUPg[sl], TWUPu[sl], Thalo] + [THT[x] for x in tiles], writes=[pbT[gb], pbT[ub]])
                G, U = pb[gb], pb[ub]
                tg, gg, tu = TG[par][:, 0:n], GG[par][:, 0:n], TU[par][:, 0:n]
                S.add("act", (lambda e, G=G, tg=tg, c=c, n=n: e.activation(out=tg, in_=G[:, 0:n], func=AF.Identity, scale=cw(0, c), bias=cb(c))),
                      reads=[pbT[gb], k.Tconst], writes=[TTG[par]])
                S.add("act", (lambda e, U=U, tu=tu, c=c, n=n: e.activation(out=tu, in_=U[:, 0:n], func=AF.Identity, scale=cw(0, 22 + c), bias=cb(22 + c))),
                      reads=[pbT[ub], k.Tconst], writes=[TTU[par]])
                S.add("dve", (lambda e, G=G, tg=tg, c=c, n=n: e.scalar_tensor_tensor(out=tg, in0=G[:, 1:n + 1], scalar=cw(1, c), in1=tg, op0=ALU.mult, op1=ALU.add)),
                      reads=[pbT[gb], TTG[par], k.Tconst], writes=[TTG[par]])
                S.add("dve", (lambda e, G=G, tg=tg, c=c, n=n: e.scalar_tensor_tensor(out=tg, in0=G[:, 2:n + 2], scalar=cw(2, c), in1=tg, op0=ALU.mult, op1=ALU.add)),
                      reads=[pbT[gb], TTG[par], k.Tconst], writes=[TTG[par]])
                S.add("act", (lambda e, tg=tg, gg=gg: e.activation(out=gg, in_=tg, func=AF.Gelu)), reads=[TTG[par]], writes=[TGG[par]])
                S.add("dve", (lambda e, U=U, tu=tu, c=c, n=n: e.scalar_tensor_tensor(out=tu, in0=U[:, 1:n + 1], scalar=cw(1, 22 + c), in1=tu, op0=ALU.mult, op1=ALU.add)),
                      reads=[pbT[ub], TTU[par], k.Tconst], writes=[TTU[par]])
                S.add("dve", (lambda e, U=U, tu=tu, c=c, n=n: e.scalar_tensor_tensor(out=tu, in0=U[:, 2:n + 2], scalar=cw(2, 22 + c), in1=tu, op0=ALU.mult, op1=ALU.add)),
                      reads=[pbT[ub], TTU[par], k.Tconst], writes=[TTU[par]])
                S.add("pool", (lambda e, gg=gg, tu=tu, pp=pp, cc=cc, t0=t0, t1=t1: e.tensor_tensor(out=GT[pp][:, cc, t0:t1], in0=gg, in1=tu, op=ALU.mult)),
                      reads=[TGG[par], TTU[par]], writes=[TGT[pp][cc]])
            if c + NSLOT < NFC:
                load_up(c + NSLOT)
        while pending:
            pending.pop(0)()
        if p + 1 < 4:
            load_dn(p + 1)
        for m in range(NT):
            pending.append(lambda m=m, pp=pp, p=p: down_unit(m, pp, p))
    while pending:
        pending.pop(0)()
    k.stats_ready = True


def phase_ple(k, s, l, final):
    S, A = k.S, k.A
    S.barrier()
    A.top = k.persist_top
    pb, pbT = k.pb, k.pbT
    Wg = r3(A.new_bf(8 * 1024), 1024)
    Wp = r3(A.new_bf(2 * 1024), 1024)
    postg = A.new_f32(1024)
    finalg = A.new_f32(1024)
    hn = [A.new_bf(1024) for _ in range(2)]
    hT = [r3(A.new_bf(1024), 128) for _ in range(2)]
    pbf = [A.new_bf(256) for _ in range(2)]
    PT = r3(A.new_bf(2 * SEQ), SEQ)
    sig = [A.new_f32(1024) for _ in range(2)]
    t2 = [A.new_f32(1024) for _ in range(2)]
    junk = A.new_bf(1024)
    junk2 = A.new_bf(1024)
    outt = [A.new_f32(1024) for _ in range(2)]
    ssp = A.new_f32(32)
    sp1 = A.new_f32(16)
    sp2 = A.new_f32(16)
    rsp = A.new_f32(16)
    TWg = [T(), T()]
    TWp = [T()]
    Tmisc = T()
    Thn = [T(), T()]
    ThT = [T(), T()]
    Tpbf = [T(), T()]
    TPT = [T() for _ in range(NT)]
    Tsig, Tt2 = [T(), T()], [T(), T()]
    Tjunk, Tssp = T(), T()
    ring = JunkRing([(junk, Tjunk), (junk2, T())])
    Tsspi = [T() for _ in range(32)]
    Tout = [T(), T()]
    load_w_slabs(k, Wp, k.ple_w_proj[l], 1024, 1024, "w1", TWp)
    S.add("sp", lambda e: e.dma_start(out=postg, in_=k.vbc_d[:, VBC["post_g"] + 1024 * l:VBC["post_g"] + 1024 * (l + 1)]), writes=[Tmisc], chan="misc")
    if final:
        S.add("sp", lambda e: e.dma_start(out=finalg, in_=k.vbc_d[:, VBC["final_g"]:VBC["final_g"] + 1024]), writes=[Tmisc], chan="misc")
    gcol = VFM["ple_g"] + 8 * l
    out_ops = []
    fin_q = []
    fms = A.new_f32(16)
    frs = A.new_f32(16)
    Tf = [T() for _ in range(NT)]
    for i in range(NT):
        cur = i % 2
        S.add("pool", (lambda e, i=i, cur=cur: e.dma_start(out=pbf[cur], in_=k.p_d[l, s, i * 128:(i + 1) * 128, :])),
              writes=[Tpbf[cur]], chan="pin%d" % cur)
        if i == 1:
            load_w_slabs(k, Wg, k.ple_w_gate[l], 1024, 512, "w0", TWg)
        tb = 6 + cur
        pbv = pb[tb][:].bitcast(BF16)

        def trp(e, cur=cur, pbv=pbv):
            for c in range(2):
                ins = e.transpose(out=pbv[:, c * 128:(c + 1) * 128], in_=pbf[cur][:, c * 128:(c + 1) * 128], identity=k.ident)
            return ins
        S.add("pe", trp, reads=[Tpbf[cur], k.Tconst], writes=[pbT[tb]])
        S.add("dve", (lambda e, i=i, pbv=pbv: e.tensor_copy(out=PT[:, :, i * 128:(i + 1) * 128], in_=r3(pbv[:, 0:256], 128))), reads=[pbT[tb]], writes=[TPT[i]])
        b0, b1 = 4 * cur, 4 * cur + 1

        def pmm(e, i=i, b0=b0, b1=b1):
            for (b, n2) in ((b0, 0), (b1, 1)):
                for kc in range(2):
                    ins = e.matmul(pb[b][:], lhsT=PT[:, kc, i * 128:(i + 1) * 128], rhs=Wp[:, kc, n2 * 512:(n2 + 1) * 512], start=(kc == 0), stop=(kc == 1))
            return ins
        S.add("pe", pmm, reads=[TPT[i]] + TWp, writes=[pbT[b0], pbT[b1]])
        for (b, n2) in ((b0, 0), (b1, 1)):
            jb, jt = ring.nxt()
            S.add("act", (lambda e, b=b, n2=n2, i=i, jb=jb: e.activation(out=jb[:, 0:512], in_=pb[b][:], func=AF.Square, accum_out=ssp[:, 2 * i + n2:2 * i + n2 + 1])),
                  reads=[pbT[b]], writes=[Tsspi[2 * i + n2], jt])
    S.add("pool", lambda e: e.tensor_tensor(out=Wp, in0=Wp, in1=postg.unsqueeze(1).to_broadcast([128, 2, 1024]), op=ALU.mult),
          reads=[Tmisc] + TWp, writes=[TWp[0]])
    norm_stats_all(k, ring)
    sspv = ssp.rearrange("p (i two) -> p i two", two=2)
    S.add("dve", lambda e: e.tensor_tensor(out=sp1, in0=sspv[:, :, 0], in1=sspv[:, :, 1], op=ALU.add), reads=Tsspi, writes=[Tssp])
    S.add("act", lambda e: e.activation(out=sp2, in_=sp1, func=AF.Ln, scale=1.0 / D, bias=k.epsr), reads=[Tssp], writes=[Tssp])
    S.add("act", lambda e: e.activation(out=rsp, in_=sp2, func=AF.Exp, scale=-0.5), reads=[Tssp], writes=[Tssp])
    norm_hT2(k, 0, gcol, hn[0], Thn[0], 6, hT[0], ThT[0])
    norm_hT2(k, 1, gcol, hn[1], Thn[1], 7, hT[1], ThT[1])
    for i in range(NT):
        cur = i % 2
        g0, g1 = 2 * cur, 2 * cur + 1
        p0, p1 = 4, 5

        def gmm(e, cur=cur, g0=g0, g1=g1):
            for (b, n2) in ((g0, 0), (g1, 1)):
                for kc in range(8):
                    ins = e.matmul(pb[b][:], lhsT=hT[cur][:, kc, :], rhs=Wg[:, kc, n2 * 512:(n2 + 1) * 512], start=(kc == 0), stop=(kc == 7))
            return ins
        S.add("pe", gmm, reads=[ThT[cur]] + TWg, writes=[pbT[g0], pbT[g1]])

        def pmm2(e, i=i, p0=p0, p1=p1):
            for (b, n2) in ((p0, 0), (p1, 1)):
                for kc in range(2):
                    ins = e.matmul(pb[b][:], lhsT=PT[:, kc, i * 128:(i + 1) * 128], rhs=Wp[:, kc, n2 * 512:(n2 + 1) * 512], start=(kc == 0), stop=(kc == 1))
            return ins
        S.add("pe", pmm2, reads=[TPT[i]] + TWp, writes=[pbT[p0], pbT[p1]])
        if i + 2 < NT:
            norm_hT2(k, i + 2, gcol, hn[cur], Thn[cur], 6 + cur, hT[cur], ThT[cur])
        for (b, n2) in ((g0, 0), (g1, 1)):
            S.add("act", (lambda e, b=b, n2=n2, cur=cur: e.activation(out=sig[cur][:, n2 * 512:(n2 + 1) * 512], in_=pb[b][:], func=AF.Sigmoid)),
                  reads=[pbT[b]], writes=[Tsig[cur]])
        for (b, n2) in ((p0, 0), (p1, 1)):
            S.add("dve", (lambda e, b=b, n2=n2, cur=cur, i=i: e.scalar_tensor_tensor(out=t2[cur][:, n2 * 512:(n2 + 1) * 512], in0=pb[b][:], scalar=rsp[:, i:i + 1],
                                                                                   in1=sig[cur][:, n2 * 512:(n2 + 1) * 512], op0=ALU.mult, op1=ALU.mult)),
                  reads=[pbT[b], Tssp, Tsig[cur]], writes=[Tt2[cur]])
        S.add("pool", (lambda e, i=i, cur=cur: e.tensor_tensor(out=k.X[:, i, :], in0=t2[cur], in1=k.X[:, i, :], op=ALU.add)), reads=[Tt2[cur], k.Xt[i]], writes=[k.Xt[i]])
        if not final:
            fin_q.append(lambda i=i: stat_tile(k, i, ring))
            if len(fin_q) > 2:
                fin_q.pop(0)()
        if final:
            def fin(i=i, cur=cur):
                jb, jt = ring.nxt()
                S.add("act", (lambda e, i=i, jb=jb: e.activation(out=jb, in_=k.X[:, i, :], func=AF.Square, accum_out=fms[:, i:i + 1])), reads=[k.Xt[i]], writes=[Tf[i], jt])
                S.add("dve", (lambda e, i=i: e.tensor_scalar(out=fms[:, i:i + 1], in0=fms[:, i:i + 1], scalar1=1.0 / D, scalar2=RMS_EPS, op0=ALU.mult, op1=ALU.add)),
                      reads=[Tf[i]], writes=[Tf[i]])
                S.add("pool", (lambda e, i=i: e.tensor_tensor(out=frs[:, i:i + 1], in0=fms[:, i:i + 1], in1=k.neghalf, op=ALU.pow)), reads=[Tf[i], k.Tconst], writes=[Tf[i]])
                S.add("act", (lambda e, i=i, cur=cur: e.activation(out=outt[cur], in_=k.X[:, i, :], func=AF.Copy, scale=frs[:, i:i + 1])),
                      reads=[k.Xt[i], Tf[i]], writes=[Tout[cur]])
                if k.next_seq is not None:
                    k.load_x_tile(k.next_seq, i)
                S.add("pool" if cur == 0 else "dve", (lambda e, cur=cur: e.tensor_tensor(out=outt[cur], in0=outt[cur], in1=finalg, op=ALU.mult)), reads=[Tout[cur], Tmisc], writes=[Tout[cur]])
                out_ops.append(S.add("sp", (lambda e, i=i, cur=cur: e.dma_start(out=k.out_d[s, i * 128:(i + 1) * 128, :], in_=outt[cur])),
                                     reads=[Tout[cur]], chan="xout%d" % cur))

            fin_q.append(fin)
            if len(fin_q) > 2:
                fin_q.pop(0)()
    while fin_q:
        fin_q.pop(0)()
    k.stats_ready = not final
    return out_ops


def phase_mix1(k, s):
    S, A = k.S, k.A
    S.barrier()
    A.top = k.persist_top
    pb, pbT = k.pb, k.pbT
    HT = r3(A.new_bf(8 * SEQ), SEQ)
    WI = [r3(A.new_bf(8 * 1536), 1536) for _ in range(2)]
    WO = [r3(A.new_bf(4 * 1024), 1024) for _ in range(2)]
    S32 = r3(A.new_f32(2 * 512), 512)
    Sb = r3(A.new_bf(2 * 512), 512)
    cs = [A.new_f32(256) for _ in range(2)]
    decT = A.new_f32(512)
    xi8 = r3(A.new_f32(1024), 128)
    zeta = A.new_f32(4)
    hn = A.new_bf(1024)
    junk = A.new_bf(1024)
    junk2 = A.new_bf(1024)
    QK = [A.new_bf(512) for _ in range(2)]
    Vt = [A.new_bf(512) for _ in range(2)]
    SGt = [A.new_bf(512) for _ in range(2)]
    qx = [r3(A.new_bf(256), 128) for _ in range(2)]
    ktm = [A.new_bf(256) for _ in range(2)]
    innT = [A.new_bf(128) for _ in range(2)]
    RT = [[A.new_f32(256) for _ in range(4)] for _ in range(2)]
    Y = [A.new_bf(512) for _ in range(2)]
    YT = [r3(A.new_bf(512), 128) for _ in range(2)]
    stats = [A.new_f32(8) for _ in range(2)]
    mv = [A.new_f32(4) for _ in range(2)]
    THT = [T() for _ in range(NT)]
    TWI = [[T() for _ in range(4)] for _ in range(2)]
    TWO = [T(), T()]
    TS32 = [T(), T()]
    TSb = [T(), T()]
    Tcs = [T(), T()]
    Ttab, Thn, Tjunk = T(), T(), T()
    TQK, TVt, TSGt, Tqx, Tktm, TinnT = ([T(), T()] for _ in range(6))
    TRT = [[T() for _ in range(4)] for _ in range(2)]
    TY, TYT, Tst, Trs = ([T(), T()] for _ in range(4))
    win = k.ret_w_in.rearrange("(kc p) n -> p kc n", p=128)
    wo = k.ret_w_out.rearrange("(kc p) n -> p kc n", p=128)

    def load_head(h):
        sl = h % 2
        parts = [(0, 256, h * 256), (256, 256, 1024 + h * 256), (512, 512, 2048 + h * 512), (1024, 512, 4096 + h * 512)]
        for pi, (o, n, c0) in enumerate(parts):
            S.add("pool", (lambda e, sl=sl, o=o, n=n, c0=c0: e.dma_start(out=WI[sl][:, :, o:o + n], in_=win[:, :, c0:c0 + n])),
                  writes=[TWI[sl][pi]], chan="wi%d" % sl)
        S.add("pool", (lambda e, h=h, sl=sl: e.dma_start(out=WO[sl], in_=wo[:, 4 * h:4 * h + 4, :])), writes=[TWO[sl]], chan="wo%d" % sl)

    def scale_wo(h, kc):
        sl = h % 2
        gcolv = k.vfm[:, VFM["gn_g"] + 4 * h + kc:VFM["gn_g"] + 4 * h + kc + 1]
        S.add("dve", lambda e: e.tensor_scalar(out=WO[sl][:, kc, :], in0=WO[sl][:, kc, :], scalar1=gcolv, scalar2=None, op0=ALU.mult),
              reads=[TWO[sl], k.Tconst], writes=[TWO[sl]])

    load_head(0)
    for kc_ in range(4):
        scale_wo(0, kc_)
    for (dst, nm) in ((decT, "decayT"), (xi8.rearrange("p a b -> p (a b)"), "xi"), (zeta, "zeta")):
        S.add("sp", (lambda e, dst=dst, nm=nm: e.dma_start(out=dst, in_=k.consts_d[:, cslice(nm)])), writes=[Ttab], chan="misc")
    gcol = VFM["mix_g"] + 8
    co, _ = CONST_OFF["cos"]
    so, _ = CONST_OFF["sin"]
    ring = JunkRing([(junk, Tjunk), (junk2, T())])
    norm_stats_all(k, ring)
    for i in range(2):
        norm_hT2(k, i, gcol, hn, Thn, 3, HT[:, :, i * 128:(i + 1) * 128], THT[i])
    items = [(h, i) for h in range(4) for i in range(NT)]
    stat_q = []
    pbv3 = pb[3][:].bitcast(BF16)
    pbv0 = pb[0][:].bitcast(BF16)

    def qk4_(par):
        return QK[par].rearrange("p (a b c) -> p a b c", a=2, b=2)

    def stage_a(n):
        h, i = items[n]
        par = n % 2
        sl = h % 2
        if i == 4 and h + 1 < 4:
            load_head(h + 1)
        if 8 <= i < 12 and h + 1 < 4:
            scale_wo(h + 1, i - 8)
        W = WI[sl]
        tsl = slice(i * 128, (i + 1) * 128)
        S.add("sp", lambda e: e.dma_start(out=cs[par][:, 0:128], in_=k.consts_d[:, co + i * 128:co + (i + 1) * 128]), writes=[Tcs[par]], chan="cs%d" % par)
        S.add("sp", lambda e: e.dma_start(out=cs[par][:, 128:256], in_=k.consts_d[:, so + i * 128:so + (i + 1) * 128]), writes=[Tcs[par]], chan="cs%d" % par)

        def pqk(e):
            for ci in range(4):
                for kc in range(8):
                    ins = e.matmul(pb[0][:, ci * 128:(ci + 1) * 128], lhsT=W[:, kc, ci * 128:(ci + 1) * 128], rhs=HT[:, kc, tsl], start=(kc == 0), stop=(kc == 7))
            return ins
        S.add("pe", pqk, reads=TWI[sl] + [THT[i]], writes=[pbT[0]])
        v4 = pb[0][:].rearrange("p (a b c) -> p a b c", a=2, b=2)
        x1, x2 = v4[:, :, 0, :], v4[:, :, 1, :]
        cosb = cs[par][:, 0:128].unsqueeze(1).to_broadcast([128, 2, 128])
        sinb = cs[par][:, 128:256].unsqueeze(1).to_broadcast([128, 2, 128])
        rt = [r3(x, 128) for x in RT[par]]
        for (ri, xin, tab) in ((0, x1, cosb), (1, x2, sinb), (2, x2, cosb), (3, x1, sinb)):
            S.add("dve", (lambda e, ri=ri, xin=xin, tab=tab: e.tensor_tensor(out=rt[ri], in0=xin, in1=tab, op=ALU.mult)),
                  reads=[pbT[0], Tcs[par]], writes=[TRT[par][ri]])
        qk4 = qk4_(par)
        S.add("pool", lambda e: e.tensor_tensor(out=qk4[:, :, 0, :], in0=rt[0], in1=rt[1], op=ALU.subtract), reads=[TRT[par][0], TRT[par][1]], writes=[TQK[par]])
        S.add("pool", lambda e: e.tensor_tensor(out=qk4[:, :, 1, :], in0=rt[2], in1=rt[3], op=ALU.add), reads=[TRT[par][2], TRT[par][3]], writes=[TQK[par]])
        if i > 0:
            xib = xi8[:, 2 * h, :].unsqueeze(1).to_broadcast([128, 2, 128])
            S.add("pool", lambda e: e.tensor_tensor(out=qx[par], in0=qk4[:, 0, :, :], in1=xib, op=ALU.mult), reads=[TQK[par], Ttab], writes=[Tqx[par]])
        if h == 0 and i + 2 < NT:
            norm_hT2(k, i + 2, gcol, hn, Thn, 3, HT[:, :, (i + 2) * 128:(i + 3) * 128], THT[i + 2])

    def stage_a2(n):
        h, i = items[n]
        par = n % 2
        sl = h % 2
        W = WI[sl]
        tsl = slice(i * 128, (i + 1) * 128)

        def pv(e):
            for (b, o) in ((1, 512), (2, 1024)):
                for kc in range(8):
                    ins = e.matmul(pb[b][:], lhsT=HT[:, kc, tsl], rhs=W[:, kc, o:o + 512], start=(kc == 0), stop=(kc == 7))
            return ins
        S.add("pe", pv, reads=TWI[sl] + [THT[i]], writes=[pbT[1], pbT[2]])
        S.add("act", lambda e: e.activation(out=Vt[par], in_=pb[1][:], func=AF.Copy), reads=[pbT[1]], writes=[TVt[par]])
        S.add("act", lambda e: e.activation(out=SGt[par], in_=pb[2][:], func=AF.Silu), reads=[pbT[2]], writes=[TSGt[par]])

    def stage_b1(n):
        h, i = items[n]
        par = n % 2
        last = (i == NT - 1)
        qk4 = qk4_(par)

        def inmm(e):
            for dc in range(2):
                ins = e.matmul(pb[4][:, 0:128], lhsT=qk4[:, 1, dc, :], rhs=qk4[:, 0, dc, :], start=(dc == 0), stop=(dc == 1))
            return ins
        S.add("pe", inmm, reads=[TQK[par]], writes=[pbT[4]])
        S.add("dve", lambda e: e.tensor_tensor(out=innT[par], in0=pb[4][:, 0:128], in1=decT[:, h * 128:(h + 1) * 128], op=ALU.mult),
              reads=[pbT[4], Ttab], writes=[TinnT[par]])
        if not last:
            def trk(e):
                for dc in range(2):
                    ins = e.transpose(out=pbv3[:, dc * 128:(dc + 1) * 128], in_=qk4[:, 1, dc, :], identity=k.ident)
                return ins
            S.add("pe", trk, reads=[TQK[par], k.Tconst], writes=[pbT[3]])
            S.add("act", lambda e: e.activation(out=ktm[par], in_=pbv3[:, 0:256], func=AF.Copy, scale=zeta[:, h:h + 1]), reads=[pbT[3], Ttab], writes=[Tktm[par]])

    def stage_b2(n):
        h, i = items[n]
        par = n % 2
        ob = 5 + par
        first = (i == 0)
        last = (i == NT - 1)

        def omm(e):
            ins = e.matmul(pb[ob][:], lhsT=innT[par], rhs=Vt[par], start=True, stop=first)
            if not first:
                for dc in range(2):
                    ins = e.matmul(pb[ob][:], lhsT=qx[par][:, dc, :], rhs=Sb[:, dc, :], start=False, stop=(dc == 1))
            return ins
        S.add("pe", omm, reads=[TinnT[par], TVt[par]] + ([] if first else [Tqx[par], TSb[0], TSb[1]]), writes=[pbT[ob]])
        if not last:
            for dc in range(2):
                kb = 7 if dc == 0 else 0
                S.add("pe", (lambda e, dc=dc, kb=kb: e.matmul(pb[kb][:], lhsT=ktm[par][:, dc * 128:(dc + 1) * 128], rhs=Vt[par], start=True, stop=True)),
                      reads=[Tktm[par], TVt[par]], writes=[pbT[kb]])
                if first:
                    S.add("dve", (lambda e, dc=dc, kb=kb: e.tensor_copy(out=S32[:, dc, :], in_=pb[kb][:])), reads=[pbT[kb]], writes=[TS32[dc]])
                else:
                    S.add("dve", (lambda e, dc=dc, kb=kb: e.scalar_tensor_tensor(out=S32[:, dc, :], in0=S32[:, dc, :], scalar=GAM128[h], in1=pb[kb][:],
                                                                                op0=ALU.mult, op1=ALU.add)),
                          reads=[pbT[kb], TS32[dc]], writes=[TS32[dc]])
                S.add("act", (lambda e, dc=dc: e.activation(out=Sb[:, dc, :], in_=S32[:, dc, :], func=AF.Copy)), reads=[TS32[dc]], writes=[TSb[dc]])

    def stage_c1(n):
        h, i = items[n]
        par = n % 2
        ob = 5 + par
        st_, mv_ = stats[par], mv[par]
        S.add("dve", lambda e: e.bn_stats(out=st_[:, 0:6], in_=pb[ob][:]), reads=[pbT[ob]], writes=[Tst[par]])
        S.add("dve", lambda e: e.bn_aggr(out=mv_[:, 0:2], in_=st_[:, 0:6]), reads=[Tst[par]], writes=[Tst[par]])
        S.add("dve", lambda e: e.scalar_tensor_tensor(out=Y[par], in0=pb[ob][:], scalar=mv_[:, 0:1], in1=SGt[par], op0=ALU.subtract, op1=ALU.mult),
              reads=[pbT[ob], Tst[par], TSGt[par]], writes=[TY[par]])
        S.add("dve", lambda e: e.tensor_scalar(out=mv_[:, 2:3], in0=mv_[:, 1:2], scalar1=LN_EPS, scalar2=None, op0=ALU.add), reads=[Tst[par]], writes=[Trs[par]])
        S.add("pool", lambda e: e.tensor_tensor(out=mv_[:, 3:4], in0=mv_[:, 2:3], in1=k.neghalf, op=ALU.pow), reads=[Trs[par], k.Tconst], writes=[Trs[par]])

    def stage_c2(n):
        h, i = items[n]
        par = n % 2
        sl = h % 2
        mv_ = mv[par]

        def try_(e):
            for c in range(4):
                ins = e.transpose(out=pbv3[:, c * 128:(c + 1) * 128], in_=Y[par][:, c * 128:(c + 1) * 128], identity=k.ident)
            return ins
        S.add("pe", try_, reads=[TY[par], k.Tconst], writes=[pbT[3]])
        S.add("dve", lambda e: e.tensor_copy(out=YT[par], in_=r3(pbv3[:, 0:512], 128)), reads=[pbT[3]], writes=[TYT[par]])

    def stage_c2b(n):
        h, i = items[n]
        par = n % 2
        sl = h % 2
        mv_ = mv[par]

        def womm(e):
            for n2 in range(2):
                for kc in range(4):
                    ins = e.matmul(pb[1 + n2][:], lhsT=YT[par][:, kc, :], rhs=WO[sl][:, kc, n2 * 512:(n2 + 1) * 512], start=(kc == 0), stop=(kc == 3))
            return ins
        S.add("pe", womm, reads=[TYT[par], TWO[sl]], writes=[pbT[1], pbT[2]])
        for n2 in range(2):
            S.add("dve", (lambda e, n2=n2: e.scalar_tensor_tensor(out=k.X[:, i, n2 * 512:(n2 + 1) * 512], in0=pb[1 + n2][:], scalar=mv_[:, 3:4],
                                                                   in1=k.X[:, i, n2 * 512:(n2 + 1) * 512], op0=ALU.mult, op1=ALU.add)),
                  reads=[pbT[1 + n2], k.Xt[i], Trs[par]], writes=[k.Xt[i]])
        if h == 3:
            stat_q.append(lambda i=i: stat_tile(k, i, ring))
        if len(stat_q) > 2 or (stat_q and n == len(items) - 1):
            while len(stat_q) > (0 if n == len(items) - 1 else 2):
                stat_q.pop(0)()

    n_it = len(items)
    for step in range(n_it + 3):
        if 0 <= step - 3 < n_it:
            stage_c2(step - 3)
        if step < n_it:
            stage_a(step)
        if 0 <= step - 3 < n_it:
            stage_c2b(step - 3)
        if 0 <= step - 1 < n_it:
            stage_b2(step - 1)
        if 0 <= step - 2 < n_it:
            stage_c1(step - 2)
        if step < n_it:
            stage_b1(step)
            stage_a2(step)
    k.stats_ready = True


_PROG = {}


def kernel(**inputs):
    inp = {kk: np.asarray(v) for kk, v in inputs.items()}
    shared = _prep_shared(inp)
    if "nc" not in _PROG:
        _PROG["nc"] = build_program()[0]
    nc = _PROG["nc"]
    in_maps = []
    for c in range(N_CORES):
        m = dict(shared)
        m["x"] = np.ascontiguousarray(inp["x"][c * SEQ_PER_CORE:(c + 1) * SEQ_PER_CORE])
        m["p"] = np.ascontiguousarray(inp["p"][:, c * SEQ_PER_CORE:(c + 1) * SEQ_PER_CORE])
        in_maps.append(m)
    res = run_bass_kernel_spmd(nc, in_maps, core_ids=list(range(N_CORES)))
    out = np.concatenate([r["out"] for r in res.results], axis=0)
    return out.astype(np.float32, copy=False)
```

```python
import contextlib
import numpy as np
import concourse.bass as bass
import concourse.mybir as mybir
from concourse.bass_utils import run_bass_kernel_spmd

F32 = mybir.dt.float32
BF16 = mybir.dt.bfloat16
AF = mybir.ActivationFunctionType
ALU = mybir.AluOpType

D = 1024
SEQ = 2048
NT = 16
DFF = 2816
NFC = 22
RMS_EPS = 1e-6
LN_EPS = 1e-5
N_CORES = 8
SEQ_PER_CORE = 2


class T:
    __slots__ = ("name", "w", "r", "psum")

    def __init__(self, name="", psum=False):
        self.name = name
        self.w = None
        self.r = []
        self.psum = psum


class Op:
    __slots__ = ("eng", "seq", "fn", "waits", "chan", "key", "sig", "clock", "signal")

    def __init__(self, eng, seq, fn, chan):
        self.eng = eng
        self.seq = seq
        self.fn = fn
        self.chan = chan
        self.waits = []
        self.sig = None
        self.clock = None
        self.signal = False


class Sched:
    ENGS = ("pe", "act", "dve", "pool", "sp")

    def __init__(self):
        self.ops = {e: [] for e in self.ENGS}
        self.seen = {e: {} for e in self.ENGS}
        self.chan_count = {}
        self.chan_last = {}
        self.n_waits = 0

    def add(self, eng, fn, reads=(), writes=(), chan=None, extra=()):
        lst = self.ops[eng]
        op = Op(eng, len(lst), fn, chan)
        if chan is not None:
            c = self.chan_count.get(chan, 0) + 1
            self.chan_count[chan] = c
            op.key = ("c", chan)
            op.seq = c
            op.signal = True
            self.chan_last[chan] = op
        else:
            op.key = eng
        deps = {}
        for d in extra:
            deps[id(d)] = d
        for t in reads:
            if t.w is not None:
                deps[id(t.w)] = t.w
            if t.psum:
                for r in t.r:
                    if r.eng != eng:
                        deps[id(r)] = r
        for t in writes:
            if t.w is not None:
                deps[id(t.w)] = t.w
            for r in t.r:
                deps[id(r)] = r
        seen = self.seen[eng]
        for d in sorted(deps.values(), key=lambda o: -o.seq):
            if d is op:
                continue
            if d.chan is None and d.eng == "pe" and eng == "pe" and chan is None:
                continue
            need = d.seq if d.chan is not None else d.seq + 1
            if seen.get(d.key, 0) >= need:
                continue
            op.waits.append(d)
            d.signal = True
            self.n_waits += 1
            for k, v in d.clock.items():
                if seen.get(k, 0) < v:
                    seen[k] = v
        clk = dict(seen)
        clk[op.key] = op.seq if chan is not None else op.seq + 1
        op.clock = clk
        for t in reads:
            t.r.append(op)
        for t in writes:
            t.w = op
            t.r = []
        lst.append(op)
        return op

    def barrier(self):
        lasts = []
        for e in self.ENGS:
            for op in reversed(self.ops[e]):
                if op.chan is None and op.fn is not None:
                    lasts.append(op)
                    break
        lasts += list(self.chan_last.values())
        for e in self.ENGS:
            self.add(e, None, extra=lasts)

    def emit(self, nc):
        handles = {"pe": "tensor", "act": "scalar", "dve": "vector", "pool": "gpsimd", "sp": "sync"}
        with contextlib.ExitStack() as st:
            sems = {}
            for e in self.ENGS:
                sems[e] = st.enter_context(nc.semaphore("s_" + e))
            for c in self.chan_count:
                sems[("c", c)] = st.enter_context(nc.semaphore("c_" + str(c)))
            for e in self.ENGS:
                cnt = 0
                for op in self.ops[e]:
                    if op.chan is not None:
                        op.sig = 16 * op.seq
                    elif op.signal:
                        cnt += 1
                        op.sig = cnt
            block = st.enter_context(nc.Block())

            def make(e):
                def body(eng):
                    for op in self.ops[e]:
                        for d in op.waits:
                            eng.wait_ge(sems[d.key], d.sig)
                        if op.fn is None:
                            continue
                        ins = op.fn(eng)
                        if op.signal:
                            ins.then_inc(sems[op.key], 16 if op.chan is not None else 1)
                return body

            for e in self.ENGS:
                if self.ops[e]:
                    getattr(block, handles[e])(make(e))


def _const_tables():
    idx = np.arange(128)
    c = {}
    c["ident"] = np.eye(128)
    c["negL"] = -(idx[:, None] >= idx[None, :]).astype(np.float64)
    c["ones"] = np.ones((128, 128))
    c["mstrict"] = (idx[:, None] < idx[None, :]).astype(np.float64)
    c["mincl"] = (idx[:, None] <= idx[None, :]).astype(np.float64)
    lg = np.log(1.0 - 2.0 ** (-5.0 - np.arange(4)))
    dec = []
    for h in range(4):
        diff = idx[None, :] - idx[:, None]
        dec.append(np.where(diff >= 0, np.exp(diff * lg[h]), 0.0) / 16.0)
    c["decayT"] = np.concatenate(dec, axis=1)
    xi = np.exp((idx + 1.0)[None, :] * lg[:, None])
    c["xi"] = np.broadcast_to(np.repeat(xi, 2, axis=0).reshape(1, 1024), (128, 1024))
    zeta = np.exp((127 - idx)[:, None] * lg[None, :]) / 16.0
    c["zeta"] = zeta
    half = 128
    inv = 1.0 / (10000.0 ** (np.arange(half, dtype=np.float32) / half))
    ang = (np.arange(SEQ, dtype=np.float32)[None, :] * inv[:, None].astype(np.float32)).astype(np.float32)
    c["cos"] = np.cos(ang)
    c["sin"] = np.sin(ang)
    order = ["ident", "negL", "ones", "mstrict", "mincl", "decayT", "xi", "zeta", "cos", "sin"]
    offs = {}
    o = 0
    for k in order:
        offs[k] = (o, c[k].shape[1])
        o += c[k].shape[1]
    tab = np.concatenate([c[k] for k in order], axis=1).astype(np.float32)
    gam128 = [float(np.exp(128 * lg[h])) for h in range(4)]
    return tab, offs, gam128


CONST_TAB, CONST_OFF, GAM128 = _const_tables()

VFM = {}
_o = 0
for _name, _n in [("mix_g", 16), ("ffn_g", 16), ("ple_g", 16), ("conv_w", 2 * 3 * 44), ("conv_b", 2 * 44), ("sg_b", 4), ("gn_g", 16)]:
    VFM[_name] = _o
    _o += _n
NVFM = _o
VBC = {}
_o = 0
for _name, _n in [("post_g", 2048), ("final_g", 1024), ("ln_g", 512), ("ln_b", 512), ("gn_g", 2048)]:
    VBC[_name] = _o
    _o += _n
NVBC = _o


def _prep_shared(inp):
    f = np.float32
    vfm = np.zeros((128, NVFM), f)

    def fm(v):
        return np.ascontiguousarray(v.reshape(-1, 128).T)

    for l in range(2):
        vfm[:, VFM["mix_g"] + 8 * l: VFM["mix_g"] + 8 * l + 8] = fm(inp["mix_norm_g"][l])
        vfm[:, VFM["ffn_g"] + 8 * l: VFM["ffn_g"] + 8 * l + 8] = fm(inp["ffn_norm_g"][l])
        vfm[:, VFM["ple_g"] + 8 * l: VFM["ple_g"] + 8 * l + 8] = fm(inp["ple_norm_g"][l])
        for j in range(3):
            o = VFM["conv_w"] + (l * 3 + j) * 44
            vfm[:, o:o + 44] = fm(inp["ffn_conv_w"][l, j])
        o = VFM["conv_b"] + l * 44
        vfm[:, o:o + 44] = fm(inp["ffn_conv_b"][l])
    vfm[:, VFM["sg_b"]:VFM["sg_b"] + 4] = inp["sg_b"][0].T
    vfm[:, VFM["gn_g"]:VFM["gn_g"] + 16] = fm(inp["ret_gn_g"][0])
    vbc = np.zeros((128, NVBC), f)

    def bc(v):
        return np.broadcast_to(v[None, :], (128, v.shape[0]))

    for l in range(2):
        vbc[:, VBC["post_g"] + 1024 * l: VBC["post_g"] + 1024 * (l + 1)] = bc(inp["ple_post_g"][l])
    vbc[:, VBC["final_g"]:VBC["final_g"] + 1024] = bc(inp["final_norm_g"])
    vbc[:, VBC["ln_g"]:VBC["ln_g"] + 512] = bc(inp["sg_ln_g"][0])
    vbc[:, VBC["ln_b"]:VBC["ln_b"] + 512] = bc(inp["sg_ln_b"][0])
    vbc[:, VBC["gn_g"]:VBC["gn_g"] + 2048] = bc(inp["ret_gn_g"][0])
    sgwT = np.ascontiguousarray(np.transpose(inp["sg_w"][0], (2, 0, 1))).reshape(128, 512)
    shared = {
        "consts": CONST_TAB, "vfm": vfm, "vbc": vbc, "sgwT": sgwT.astype(f),
        "ab_w_in": np.ascontiguousarray(inp["ab_w_in"][0]), "ab_w_out": np.ascontiguousarray(inp["ab_w_out"][0]),
        "ret_w_in": np.ascontiguousarray(inp["ret_w_in"][0]), "ret_w_out": np.ascontiguousarray(inp["ret_w_out"][0]),
        "ffn_w_up": np.ascontiguousarray(inp["ffn_w_up"]), "ffn_w_down": np.ascontiguousarray(inp["ffn_w_down"]),
        "ple_w_gate": np.ascontiguousarray(inp["ple_w_gate"]), "ple_w_proj": np.ascontiguousarray(inp["ple_w_proj"]),
    }
    return shared


class K:
    pass


class Arena:
    def __init__(self, tensor, nbytes):
        self.t = tensor
        self.n = nbytes
        self.top = 0

    def alloc(self, nbytes):
        nbytes = (nbytes + 63) // 64 * 64
        o = self.top
        self.top += nbytes
        assert self.top <= self.n, ("arena overflow", self.top, self.n)
        return o

    def f32(self, off, n):
        return self.t[:, off // 4: off // 4 + n]

    def bf(self, off, n):
        return self.t[:, off // 4: off // 4 + (n + 1) // 2].bitcast(BF16)

    def new_f32(self, n):
        return self.f32(self.alloc(4 * n), n)

    def new_bf(self, n):
        return self.bf(self.alloc(2 * n), n)


def r3(ap, b):
    return ap.rearrange("p (a b) -> p a b", b=b)


def build_program(n_seq=SEQ_PER_CORE, stages=("mix0", "ffn0", "ple0", "mix1", "ffn1", "ple1")):
    nc = bass.Bass("TRN2", target_bir_lowering=False)
    k = K()
    k.nc = nc
    dt = lambda name, shape, kind="ExternalInput": nc.dram_tensor(name, shape, F32, kind=kind).ap()
    k.x_d = dt("x", [n_seq, SEQ, D])
    k.p_d = dt("p", [2, n_seq, SEQ, 256])
    k.consts_d = dt("consts", list(CONST_TAB.shape))
    k.vfm_d = dt("vfm", [128, NVFM])
    k.vbc_d = dt("vbc", [128, NVBC])
    k.sgwT_d = dt("sgwT", [128, 512])
    k.ab_w_in = dt("ab_w_in", [1024, 2560])
    k.ab_w_out = dt("ab_w_out", [1024, 1024])
    k.ret_w_in = dt("ret_w_in", [1024, 6144])
    k.ret_w_out = dt("ret_w_out", [2048, 1024])
    k.ffn_w_up = dt("ffn_w_up", [2, 1024, 5632])
    k.ffn_w_down = dt("ffn_w_down", [2, 2816, 1024])
    k.ple_w_gate = dt("ple_w_gate", [2, 1024, 1024])
    k.ple_w_proj = dt("ple_w_proj", [2, 256, 1024])
    k.out_d = dt("out", [n_seq, SEQ, D], kind="ExternalOutput")
    k.n_seq = n_seq
    k.stages = stages

    with contextlib.ExitStack() as st:
        ARENA_BYTES = 212736
        at = st.enter_context(nc.sbuf_tensor("arena", [128, ARENA_BYTES // 4], F32))
        k.A = Arena(at, ARENA_BYTES)
        k.pbig = st.enter_context(nc.psum_tensor("pbig", [128, 4096], F32))
        k.pb = [k.pbig[:, i * 512:(i + 1) * 512] for i in range(8)]
        k.S = Sched()
        _emit_all(k)
        k.S.emit(nc)
    k.nc = nc
    return nc, k


def cslice(name):
    o, n = CONST_OFF[name]
    return slice(o, o + n)


def _emit_all(k):
    S, A, nc = k.S, k.A, k.nc
    k.X = r3(A.new_f32(NT * D), D)
    k.Xt = [T("X%d" % i) for i in range(NT)]
    k.ident = A.new_bf(128)
    k.negL = A.new_bf(128)
    k.ones = A.new_bf(128)
    k.mstrict = A.new_bf(128)
    k.vfm = A.new_f32(NVFM)
    k.ss = A.new_f32(16)
    k.lnv = A.new_f32(16)
    k.rstd = A.new_f32(16)
    k.neghalf = A.new_f32(16)[:, 0:1]
    k.Tconst = T("const")
    k.eps_ln = LN_EPS
    k.one_b = 1.0
    k.epsr = RMS_EPS
    k.Tss = [T() for _ in range(16)]
    k.Tstat = T()
    k.persist_top = A.top
    k.pbT = [T("pb%d" % i, psum=True) for i in range(8)]
    def load_x_tile(s, i):
        S.add("sp", (lambda e: e.dma_start(out=k.X[:, i, :], in_=k.x_d[s, i * 128:(i + 1) * 128, :])), writes=[k.Xt[i]], chan="xin%d" % i)
    k.load_x_tile = load_x_tile
    for i in range(NT):
        load_x_tile(0, i)
    for name, dst in [("ident", k.ident), ("negL", k.negL), ("ones", k.ones), ("mstrict", k.mstrict)]:
        S.add("pool", (lambda e, dst=dst, name=name: e.dma_start(out=dst, in_=k.consts_d[:, cslice(name)])),
              writes=[k.Tconst], chan="const")
    S.add("sp", lambda e: e.dma_start(out=k.vfm, in_=k.vfm_d), writes=[k.Tconst], chan="constsp")
    S.add("pool", lambda e: e.memset(k.neghalf, -0.5), writes=[k.Tconst])

    out_ops = []
    def load_x_tile(s, i):
        S.add("sp", (lambda e: e.dma_start(out=k.X[:, i, :], in_=k.x_d[s, i * 128:(i + 1) * 128, :])), writes=[k.Xt[i]], chan="xin%d" % i)
    k.load_x_tile = load_x_tile
    for s in range(k.n_seq):
        k.next_seq = s + 1 if (s + 1 < k.n_seq and "ple1" in k.stages) else None
        if s > 0 and "ple1" not in k.stages:
            for i in range(NT):
                load_x_tile(s, i)
        if "mix0" in k.stages:
            phase_mix0(k, s)
        if "ffn0" in k.stages:
            phase_ffn(k, s, 0)
        if "ple0" in k.stages:
            phase_ple(k, s, 0, final=False)
        if "mix1" in k.stages:
            phase_mix1(k, s)
        if "ffn1" in k.stages:
            phase_ffn(k, s, 1)
        if "ple1" in k.stages:
            out_ops += phase_ple(k, s, 1, final=True)
        else:
            out_ops += phase_dump(k, s)
        S.barrier()
    S.add("sp", None, extra=out_ops)


def phase_dump(k, s):
    ops = []
    for i in range(NT):
        ops.append(k.S.add("sp", (lambda e, i=i: e.dma_start(out=k.out_d[s, i * 128:(i + 1) * 128, :], in_=k.X[:, i, :])),
                           reads=[k.Xt[i]], chan="xout"))
    return ops


def norm_hT(k, i, gcol, hn, hnT, tb, dst, dstT, slot):
    S = k.S
    X = k.X
    ss = k.ss[:, slot:slot + 1]
    lnv = k.lnv[:, slot:slot + 1]
    rstd = k.rstd[:, slot:slot + 1]
    tss = k.Tss[slot]
    S.add("act", lambda e: e.activation(out=hn, in_=X[:, i, :], func=AF.Square, accum_out=ss),
          reads=[k.Xt[i]], writes=[hnT, tss])
    S.add("act", lambda e: e.activation(out=lnv, in_=ss, func=AF.Ln, scale=1.0 / D, bias=k.epsr), reads=[tss, k.Tconst], writes=[tss])
    S.add("act", lambda e: e.activation(out=rstd, in_=lnv, func=AF.Exp, scale=-0.5), reads=[tss], writes=[tss])
    S.add("act", lambda e: e.activation(out=hn, in_=X[:, i, :], func=AF.Copy, scale=rstd),
          reads=[k.Xt[i], tss], writes=[hnT])
    pbv = k.pb[tb][:].bitcast(BF16)

    def tr(e):
        for c in range(8):
            ins = e.transpose(out=pbv[:, c * 128:(c + 1) * 128], in_=hn[:, c * 128:(c + 1) * 128], identity=k.ident)
        return ins
    S.add("pe", tr, reads=[hnT, k.Tconst], writes=[k.pbT[tb]])
    g = k.vfm[:, gcol:gcol + 8].unsqueeze(2).to_broadcast([128, 8, 128])
    S.add("dve", lambda e: e.tensor_tensor(out=dst, in0=r3(pbv, 128), in1=g, op=ALU.mult),
          reads=[k.pbT[tb], k.Tconst], writes=[dstT])


class JunkRing:
    def __init__(self, bufs):
        self.bufs = bufs
        self.n = 0

    def nxt(self):
        b = self.bufs[self.n % len(self.bufs)]
        self.n += 1
        return b


def norm_stats_all(k, ring):
    S = k.S
    if not getattr(k, "stats_ready", False):
        for i in range(NT):
            jb, jt = ring.nxt()
            if i % 3 == 2:
                S.add("dve", (lambda e, i=i, jb=jb: e.scalar_tensor_tensor(out=jb, in0=k.X[:, i, :], scalar=1.0, in1=k.X[:, i, :], op0=ALU.mult, op1=ALU.mult,
                                                                          accum_out=k.ss[:, i:i + 1])),
                      reads=[k.Xt[i], k.Tstat], writes=[k.Tss[i], jt])
                continue
            S.add("act", (lambda e, i=i, jb=jb: e.activation(out=jb, in_=k.X[:, i, :], func=AF.Square, accum_out=k.ss[:, i:i + 1])),
                  reads=[k.Xt[i]] + ([k.Tstat] if i == 0 else []), writes=[k.Tss[i], jt])
    k.stats_ready = False
    S.add("act", lambda e: e.activation(out=k.lnv[:, 0:16], in_=k.ss[:, 0:16], func=AF.Ln, scale=1.0 / D, bias=k.epsr), reads=list(k.Tss), writes=[k.Tstat])
    S.add("act", lambda e: e.activation(out=k.rstd[:, 0:16], in_=k.lnv[:, 0:16], func=AF.Exp, scale=-0.5), reads=[k.Tstat], writes=[k.Tstat])


def stat_tile(k, i, ring):
    jb, jt = ring.nxt()
    k.S.add("act", lambda e: e.activation(out=jb, in_=k.X[:, i, :], func=AF.Square, accum_out=k.ss[:, i:i + 1]),
            reads=[k.Xt[i]], writes=[k.Tss[i], jt])


def norm_hT2(k, i, gcol, hn, hnT, tb, dst, dstT):
    S = k.S
    S.add("act", lambda e: e.activation(out=hn, in_=k.X[:, i, :], func=AF.Copy, scale=k.rstd[:, i:i + 1]),
          reads=[k.Xt[i], k.Tstat], writes=[hnT])
    pbv = k.pb[tb][:].bitcast(BF16)

    def tr(e):
        for c in range(8):
            ins = e.transpose(out=pbv[:, c * 128:(c + 1) * 128], in_=hn[:, c * 128:(c + 1) * 128], identity=k.ident)
        return ins
    S.add("pe", tr, reads=[hnT, k.Tconst], writes=[k.pbT[tb]])
    g = k.vfm[:, gcol:gcol + 8].unsqueeze(2).to_broadcast([128, 8, 128])
    S.add("dve", lambda e: e.tensor_tensor(out=dst, in0=r3(pbv, 128), in1=g, op=ALU.mult),
          reads=[k.pbT[tb], k.Tconst], writes=[dstT])


def load_w_slabs(k, dst, src, ncols, slab, chan, Ts, eng="pool"):
    srcv = src.rearrange("(kc p) n -> p kc n", p=128)
    for j, c0 in enumerate(range(0, ncols, slab)):
        c1 = min(ncols, c0 + slab)
        k.S.add(eng, (lambda e, c0=c0, c1=c1: e.dma_start(out=dst[:, :, c0:c1], in_=srcv[:, :, c0:c1])),
                writes=[Ts[j]], chan=chan)


def phase_mix0_old(k, s):
    S, A, nc = k.S, k.A, k.nc
    S.barrier()
    A.top = k.persist_top
    pb, pbT = k.pb, k.pbT
    Win = r3(A.new_bf(8 * 2560), 2560)
    Wout = r3(A.new_bf(8 * 1024), 1024)
    KT = r3(A.new_bf(4 * 2048), 2048)
    V = r3(A.new_bf(16 * 512), 512)
    SGW = A.new_bf(512)
    sgw32 = A.new_f32(512)
    mincl = A.new_f32(128)
    lng = A.new_f32(512)
    lnb = A.new_f32(512)
    hn = A.new_bf(1024)
    hT = [r3(A.new_bf(1024), 128) for _ in range(2)]
    QA = [r3(A.new_bf(512), 128) for _ in range(2)]
    QB = [r3(A.new_bf(512), 128) for _ in range(2)]
    ug = A.new_f32(512)
    vg = A.new_f32(512)
    vn32 = A.new_f32(512)
    vnb = A.new_bf(512)
    stats = A.new_f32(24)
    mv = A.new_f32(8)
    lrs = A.new_f32(4)
    lrl = A.new_f32(4)
    nmr = A.new_f32(4)
    CAT = A.new_bf(1024)
    CATT = r3(A.new_bf(1024), 128)
    E32 = [A.new_f32(512) for _ in range(2)]
    SP = [A.new_bf(512) for _ in range(2)]
    TMP = [A.new_f32(512) for _ in range(2)]
    AW = [A.new_bf(512) for _ in range(2)]
    TOT = [A.new_f32(512) for _ in range(2)]
    TWin = [T() for _ in range(5)]
    TWout = [T() for _ in range(2)]
    Tmisc = T()
    TSGW = T()
    Thn = T()
    ThT = [T(), T()]
    TQ = [T(), T()]
    TKT = [T() for _ in range(NT)]
    TV = [T() for _ in range(NT)]
    Tug, Tvg, Tvn32, Tvnb, Tst = T(), T(), T(), T(), T()
    TCATa, TCATb, TCATT = T(), T(), T()
    TE, TSP, TTMP, TAW = [T(), T()], [T(), T()], [T(), T()], [T(), T()]
    TTOT = [T(), T()]
    k.Tss = [T() for _ in range(16)]
    k.epsr = RMS_EPS

    load_w_slabs(k, Win, k.ab_w_in, 2560, 512, "w0", TWin)
    load_w_slabs(k, Wout, k.ab_w_out, 1024, 512, "w1", TWout)
    S.add("sp", lambda e: e.dma_start(out=sgw32, in_=k.sgwT_d), writes=[Tmisc], chan="misc")
    S.add("sp", lambda e: e.dma_start(out=mincl, in_=k.consts_d[:, cslice("mincl")]), writes=[Tmisc], chan="misc")
    S.add("sp", lambda e: e.dma_start(out=lng, in_=k.vbc_d[:, VBC["ln_g"]:VBC["ln_g"] + 512]), writes=[Tmisc], chan="misc")
    S.add("sp", lambda e: e.dma_start(out=lnb, in_=k.vbc_d[:, VBC["ln_b"]:VBC["ln_b"] + 512]), writes=[Tmisc], chan="misc")
    S.add("dve", lambda e: e.tensor_tensor(out=r3(SGW, 128), in0=r3(sgw32, 128),
                                           in1=mincl.unsqueeze(1).to_broadcast([128, 4, 128]), op=ALU.mult),
          reads=[Tmisc], writes=[TSGW])
    for b in range(2):
        S.add("dve", (lambda e, b=b: e.memset(QA[b][64:128, :, :], 0.0)), writes=[TQ[b]])
        S.add("dve", (lambda e, b=b: e.memset(QB[b][0:64, :, :], 0.0)), writes=[TQ[b]])
    sgb = k.vfm[:, VFM["sg_b"]:VFM["sg_b"] + 4]
    gcol = VFM["mix_g"]
    mstr_b = k.mstrict.unsqueeze(1).to_broadcast([128, 4, 128])

    norm_hT(k, 0, gcol, hn, Thn, 7, hT[0], ThT[0], 0)
    for i in range(NT):
        cur = i % 2
        h_ = hT[cur]
        ts = slice(i * 128, (i + 1) * 128)

        def proj_uv(e, h_=h_):
            for (b, c0) in ((0, 0), (1, 512)):
                for kc in range(8):
                    ins = e.matmul(pb[b][:], lhsT=h_[:, kc, :], rhs=Win[:, kc, c0:c0 + 512], start=(kc == 0), stop=(kc == 7))
            return ins
        S.add("pe", proj_uv, reads=[ThT[cur]] + TWin, writes=[pbT[0], pbT[1]])

        def proj_qk(e, h_=h_):
            for (b, c0) in ((2, 1024), (3, 1536)):
                for c in range(4):
                    for kc in range(8):
                        ins = e.matmul(pb[b][:, c * 128:(c + 1) * 128], lhsT=Win[:, kc, c0 + c * 128:c0 + (c + 1) * 128],
                                       rhs=h_[:, kc, :], start=(kc == 0), stop=(kc == 7))
            return ins
        S.add("pe", proj_qk, reads=[ThT[cur]] + TWin, writes=[pbT[2], pbT[3]])

        def proj_v(e, h_=h_):
            for kc in range(8):
                ins = e.matmul(pb[4][:], lhsT=h_[:, kc, :], rhs=Win[:, kc, 2048:2560], start=(kc == 0), stop=(kc == 7))
            return ins
        S.add("pe", proj_v, reads=[ThT[cur]] + TWin, writes=[pbT[4]])

        S.add("act", lambda e: e.activation(out=ug, in_=pb[0][:], func=AF.Gelu), reads=[pbT[0]], writes=[Tug])
        S.add("act", lambda e: e.activation(out=vg, in_=pb[1][:], func=AF.Gelu), reads=[pbT[1]], writes=[Tvg])
        S.add("act", (lambda e, cur=cur: e.activation(out=QA[cur][0:64, :, :], in_=r3(pb[2][0:64, :], 128), func=AF.Copy, scale=0.125)),
              reads=[pbT[2]], writes=[TQ[cur]])
        S.add("act", (lambda e, cur=cur: e.activation(out=QB[cur][64:128, :, :], in_=r3(pb[2][64:128, :], 128), func=AF.Copy, scale=0.125)),
              reads=[pbT[2]], writes=[TQ[cur]])
        S.add("dve", (lambda e, ts=ts: e.tensor_copy(out=KT[:, :, ts], in_=r3(pb[3][:], 128))), reads=[pbT[3]], writes=[TKT[i]])
        S.add("dve", (lambda e, i=i: e.tensor_copy(out=V[:, i, :], in_=pb[4][:])), reads=[pbT[4]], writes=[TV[i]])

        def bns(e):
            for g in range(4):
                ins = e.bn_stats(out=stats[:, g * 6:(g + 1) * 6], in_=vg[:, g * 128:(g + 1) * 128])
            return ins
        S.add("dve", bns, reads=[Tvg], writes=[Tst])

        def bna(e):
            for g in range(4):
                ins = e.bn_aggr(out=mv[:, 2 * g:2 * g + 2], in_=stats[:, g * 6:(g + 1) * 6])
            return ins
        S.add("dve", bna, reads=[Tst], writes=[Tst])
        mvv = mv.rearrange("p (g two) -> p g two", two=2)
        S.add("act", lambda e: e.activation(out=lrl, in_=mvv[:, :, 1], func=AF.Ln, bias=k.eps_ln), reads=[Tst, k.Tconst], writes=[Tst])
        S.add("act", lambda e: e.activation(out=lrs, in_=lrl, func=AF.Exp, scale=-0.5), reads=[Tst], writes=[Tst])
        S.add("dve", lambda e: e.scalar_tensor_tensor(out=nmr, in0=mvv[:, :, 0], scalar=-1.0, in1=lrs, op0=ALU.mult, op1=ALU.mult),
              reads=[Tst], writes=[Tst])

        def nrm(e):
            for g in range(4):
                ins = e.activation(out=vn32[:, g * 128:(g + 1) * 128], in_=vg[:, g * 128:(g + 1) * 128], func=AF.Identity,
                                   scale=lrs[:, g:g + 1], bias=nmr[:, g:g + 1])
            return ins
        S.add("act", nrm, reads=[Tvg, Tst], writes=[Tvn32])
        S.add("pool", lambda e: e.tensor_tensor(out=vn32, in0=vn32, in1=lng, op=ALU.mult), reads=[Tvn32, Tmisc], writes=[Tvn32])
        S.add("pool", lambda e: e.tensor_tensor(out=vnb, in0=vn32, in1=lnb, op=ALU.add), reads=[Tvn32, Tmisc], writes=[Tvnb])

        def mixmm(e):
            for g in range(4):
                ins = e.matmul(pb[5][:, g * 128:(g + 1) * 128], lhsT=SGW[:, g * 128:(g + 1) * 128], rhs=vnb[:, g * 128:(g + 1) * 128],
                               start=True, stop=True)
            return ins
        S.add("pe", mixmm, reads=[TSGW, Tvnb], writes=[pbT[5]])

        def aout(e):
            for g in range(4):
                ins = e.scalar_tensor_tensor(out=CAT[:, g * 128:(g + 1) * 128], in0=pb[5][:, g * 128:(g + 1) * 128],
                                             scalar=sgb[:, g:g + 1], in1=ug[:, g * 128:(g + 1) * 128], op0=ALU.add, op1=ALU.mult)
            return ins
        S.add("dve", aout, reads=[pbT[5], Tug, k.Tconst], writes=[TCATa])

        if i + 1 < NT:
            norm_hT(k, i + 1, gcol, hn, Thn, 7, hT[1 - cur], ThT[1 - cur], (i + 1) % 16)

        items = [(j, hf) for j in range(i, -1, -1) for hf in range(2)]
        n_it = len(items)

        def stage_a(n, cur=cur, items=items, i=i):
            j, hf = items[n]
            par = n % 2
            zb = par
            ks = slice(j * 128, (j + 1) * 128)

            def zmm(e):
                for hh in range(4):
                    h = 4 * hf + hh
                    c = h // 2
                    q = QA[cur] if h % 2 == 0 else QB[cur]
                    ins = e.matmul(pb[zb][:, hh * 128:(hh + 1) * 128], lhsT=KT[:, c, ks], rhs=q[:, c, :], start=True, stop=True)
                return ins
            S.add("pe", zmm, reads=[TKT[j], TQ[cur]], writes=[pbT[zb]])
            S.add("act", lambda e: e.activation(out=E32[par], in_=pb[zb][:], func=AF.Exp), reads=[pbT[zb]], writes=[TE[par]])
            S.add("act", lambda e: e.activation(out=SP[par], in_=E32[par], func=AF.Ln, bias=k.one_b), reads=[TE[par], k.Tconst], writes=[TSP[par]])
            if j == i:
                S.add("dve", lambda e: e.tensor_tensor(out=r3(SP[par], 128), in0=r3(SP[par], 128), in1=mstr_b, op=ALU.mult),
                      reads=[TSP[par], k.Tconst], writes=[TSP[par]])

        def stage_b(n, cur=cur, items=items, i=i):
            j, hf = items[n]
            par = n % 2
            wb = 2 + par
            tb = 4 + par
            ks = slice(j * 128, (j + 1) * 128)

            def wmm(e):
                for hh in range(4):
                    h = 4 * hf + hh
                    c = h // 2
                    q = QA[cur] if h % 2 == 0 else QB[cur]
                    e.matmul(pb[wb][:, hh * 128:(hh + 1) * 128], lhsT=KT[:, c, ks], rhs=q[:, c, :], start=(hh == 0), stop=False,
                             skip_group_check=True)
                ins = e.matmul(pb[wb][:], lhsT=k.negL, rhs=SP[par], start=False, stop=True, skip_group_check=True)
                return ins
            S.add("pe", wmm, reads=[TKT[j], TQ[cur], TSP[par], k.Tconst], writes=[pbT[wb]])
            if j > 0:
                S.add("pe", lambda e: e.matmul(pb[tb][:], lhsT=k.ones, rhs=SP[par], start=True, stop=True),
                      reads=[TSP[par], k.Tconst], writes=[pbT[tb]])
            if j < i:
                S.add("dve", lambda e: e.tensor_tensor(out=TMP[par], in0=pb[wb][:], in1=TOT[hf], op=ALU.subtract),
                      reads=[pbT[wb], TTOT[hf]], writes=[TTMP[par]])
                S.add("act", lambda e: e.activation(out=AW[par], in_=TMP[par], func=AF.Exp), reads=[TTMP[par]], writes=[TAW[par]])
            else:
                S.add("act", lambda e: e.activation(out=AW[par], in_=pb[wb][:], func=AF.Exp), reads=[pbT[wb]], writes=[TAW[par]])
                S.add("dve", lambda e: e.tensor_tensor(out=r3(AW[par], 128), in0=r3(AW[par], 128), in1=mstr_b, op=ALU.mult),
                      reads=[TAW[par], k.Tconst], writes=[TAW[par]])
            if j > 0:
                if j == i:
                    S.add("dve", lambda e: e.tensor_copy(out=TOT[hf], in_=pb[tb][:]), reads=[pbT[tb]], writes=[TTOT[hf]])
                else:
                    S.add("dve", lambda e: e.tensor_tensor(out=TOT[hf], in0=pb[tb][:], in1=TOT[hf], op=ALU.add),
                          reads=[pbT[tb], TTOT[hf]], writes=[TTOT[hf]])

        def stage_c(n, cur=cur, items=items, n_it=n_it):
            j, hf = items[n]
            par = n % 2

            def av(e):
                for hh in range(4):
                    h = 4 * hf + hh
                    ins = e.matmul(pb[6][:, h * 64:(h + 1) * 64], lhsT=AW[par][:, hh * 128:(hh + 1) * 128],
                                   rhs=V[:, j, h * 64:(h + 1) * 64], start=(n == 0 and hh == 0), stop=(n == n_it - 1 and hh == 3),
                                   skip_group_check=True)
                return ins
            S.add("pe", av, reads=[TAW[par], TV[j]], writes=[pbT[6]])

        for step in range(n_it + 2):
            if step < n_it:
                stage_a(step)
            if 0 <= step - 1 < n_it:
                stage_b(step - 1)
            if 0 <= step - 2 < n_it:
                stage_c(step - 2)
        S.add("act", lambda e: e.activation(out=CAT[:, 512:1024], in_=pb[6][:], func=AF.Copy), reads=[pbT[6]], writes=[TCATb])

        pbv = pb[7][:].bitcast(BF16)

        def trc(e):
            for c in range(8):
                ins = e.transpose(out=pbv[:, c * 128:(c + 1) * 128], in_=CAT[:, c * 128:(c + 1) * 128], identity=k.ident)
            return ins
        S.add("pe", trc, reads=[TCATa, TCATb, k.Tconst], writes=[pbT[7]])
        S.add("dve", lambda e: e.tensor_copy(out=CATT, in_=r3(pbv, 128)), reads=[pbT[7]], writes=[TCATT])

        def womm(e):
            for n2 in range(2):
                for kc in range(8):
                    ins = e.matmul(pb[n2][:], lhsT=CATT[:, kc, :], rhs=Wout[:, kc, n2 * 512:(n2 + 1) * 512], start=(kc == 0), stop=(kc == 7))
            return ins
        S.add("pe", womm, reads=[TCATT] + TWout, writes=[pbT[0], pbT[1]])
        for n2 in range(2):
            S.add("dve", (lambda e, n2=n2, i=i: e.tensor_tensor(out=k.X[:, i, n2 * 512:(n2 + 1) * 512], in0=pb[n2][:],
                                                                 in1=k.X[:, i, n2 * 512:(n2 + 1) * 512], op=ALU.add)),
                  reads=[pbT[n2], k.Xt[i]], writes=[k.Xt[i]])


def phase_mix0(k, s):
    S, A = k.S, k.A
    if s > 0:
        S.barrier()
    A.top = k.persist_top
    pb, pbT = k.pb, k.pbT
    KT = r3(A.new_bf(4 * 2048), 2048)
    V = r3(A.new_bf(16 * 512), 512)
    QT = r3(A.new_bf(4 * 2048), 2048)
    CATa = r3(A.new_bf(16 * 512), 512)
    TKT = [T() for _ in range(NT)]
    TV = [T() for _ in range(NT)]
    TQT = [T() for _ in range(NT)]
    TCATa = [T() for _ in range(NT)]
    sub_top = A.top
    Win = r3(A.new_bf(8 * 2560), 2560)
    SGW = A.new_bf(512)
    sgw32 = A.new_f32(512)
    mincl = A.new_f32(128)
    lng = A.new_f32(512)
    lnb = A.new_f32(512)
    hn = [A.new_bf(1024) for _ in range(2)]
    junk = A.new_bf(1024)
    junk2 = A.new_bf(1024)
    hT = [r3(A.new_bf(1024), 128) for _ in range(2)]
    ug = [A.new_f32(512) for _ in range(2)]
    vg = [A.new_f32(512) for _ in range(2)]
    vn32 = [A.new_f32(512) for _ in range(2)]
    vnb = [A.new_bf(512) for _ in range(2)]
    stats = [A.new_f32(24) for _ in range(2)]
    mv = [A.new_f32(8) for _ in range(2)]
    sm = [A.new_f32(12) for _ in range(2)]
    TWin = [T() for _ in range(5)]
    Tmisc, TSGW, Tjunk = T(), T(), T()
    Thn, ThT, Tug, Tvg, Tvn32, Tvnb, Tst = ([T(), T()] for _ in range(7))
    load_w_slabs(k, Win, k.ab_w_in, 2560, 512, "w0", TWin)
    S.add("sp", lambda e: e.dma_start(out=sgw32, in_=k.sgwT_d), writes=[Tmisc], chan="misc")
    S.add("sp", lambda e: e.dma_start(out=mincl, in_=k.consts_d[:, cslice("mincl")]), writes=[Tmisc], chan="misc")
    S.add("sp", lambda e: e.dma_start(out=lng, in_=k.vbc_d[:, VBC["ln_g"]:VBC["ln_g"] + 512]), writes=[Tmisc], chan="misc")
    S.add("sp", lambda e: e.dma_start(out=lnb, in_=k.vbc_d[:, VBC["ln_b"]:VBC["ln_b"] + 512]), writes=[Tmisc], chan="misc")
    S.add("dve", lambda e: e.tensor_tensor(out=r3(SGW, 128), in0=r3(sgw32, 128),
                                           in1=mincl.unsqueeze(1).to_broadcast([128, 4, 128]), op=ALU.mult),
          reads=[Tmisc], writes=[TSGW])
    sgb = k.vfm[:, VFM["sg_b"]:VFM["sg_b"] + 4]
    gcol = VFM["mix_g"]
    ring = JunkRing([(junk, Tjunk), (junk2, T())])
    norm_stats_all(k, ring)
    tails = []
    for i in range(NT):
        par = i % 2
        ts = slice(i * 128, (i + 1) * 128)
        if i == 0:
            norm_hT2(k, 0, gcol, hn[0], Thn[0], 7, hT[0], ThT[0])
        h_ = hT[par]

        def proj_uv(e, h_=h_):
            for (b, c0) in ((0, 0), (1, 512)):
                for kc in range(8):
                    ins = e.matmul(pb[b][:], lhsT=h_[:, kc, :], rhs=Win[:, kc, c0:c0 + 512], start=(kc == 0), stop=(kc == 7))
            return ins
        S.add("pe", proj_uv, reads=[ThT[par]] + TWin, writes=[pbT[0], pbT[1]])

        def proj_qk(e, h_=h_):
            for (b, c0) in ((2, 1024), (3, 1536)):
                for c in range(4):
                    for kc in range(8):
                        ins = e.matmul(pb[b][:, c * 128:(c + 1) * 128], lhsT=Win[:, kc, c0 + c * 128:c0 + (c + 1) * 128],
                                       rhs=h_[:, kc, :], start=(kc == 0), stop=(kc == 7))
            return ins
        S.add("pe", proj_qk, reads=[ThT[par]] + TWin, writes=[pbT[2], pbT[3]])

        def proj_v(e, h_=h_):
            for kc in range(8):
                ins = e.matmul(pb[4][:], lhsT=h_[:, kc, :], rhs=Win[:, kc, 2048:2560], start=(kc == 0), stop=(kc == 7))
            return ins
        S.add("pe", proj_v, reads=[ThT[par]] + TWin, writes=[pbT[4]])
        tail_dve = None
        if tails:
            tail_dve = tails.pop(0)()
        if i + 1 < NT:
            norm_hT2(k, i + 1, gcol, hn[1 - par], Thn[1 - par], 7, hT[1 - par], ThT[1 - par])
        if tail_dve is not None:
            tail_dve()
        S.add("act", (lambda e, par=par: e.activation(out=ug[par], in_=pb[0][:], func=AF.Gelu)), reads=[pbT[0]], writes=[Tug[par]])
        S.add("act", (lambda e, par=par: e.activation(out=vg[par], in_=pb[1][:], func=AF.Gelu)), reads=[pbT[1]], writes=[Tvg[par]])
        S.add("act", (lambda e, ts=ts: e.activation(out=QT[:, :, ts], in_=r3(pb[2][:], 128), func=AF.Copy, scale=0.125)), reads=[pbT[2]], writes=[TQT[i]])
        S.add("dve", (lambda e, ts=ts: e.tensor_copy(out=KT[:, :, ts], in_=r3(pb[3][:], 128))), reads=[pbT[3]], writes=[TKT[i]])
        S.add("dve", (lambda e, i=i: e.tensor_copy(out=V[:, i, :], in_=pb[4][:])), reads=[pbT[4]], writes=[TV[i]])
        st_, mv_, sm_ = stats[par], mv[par], sm[par]
        vg_, vn_, vb_, ug_ = vg[par], vn32[par], vnb[par], ug[par]

        def bns(e, st_=st_, vg_=vg_):
            for g in range(4):
                ins = e.bn_stats(out=st_[:, g * 6:(g + 1) * 6], in_=vg_[:, g * 128:(g + 1) * 128])
            return ins
        S.add("dve", bns, reads=[Tvg[par]], writes=[Tst[par]])

        def bna(e, st_=st_, mv_=mv_):
            for g in range(4):
                ins = e.bn_aggr(out=mv_[:, 2 * g:2 * g + 2], in_=st_[:, g * 6:(g + 1) * 6])
            return ins
        S.add("dve", bna, reads=[Tst[par]], writes=[Tst[par]])
        mvv = mv_.rearrange("p (g two) -> p g two", two=2)
        S.add("dve", (lambda e, sm_=sm_, mvv=mvv: e.tensor_scalar(out=sm_[:, 0:4], in0=mvv[:, :, 1], scalar1=LN_EPS, scalar2=None, op0=ALU.add)),
              reads=[Tst[par]], writes=[Tst[par]])
        S.add("pool", (lambda e, sm_=sm_: e.tensor_tensor(out=sm_[:, 4:8], in0=sm_[:, 0:4], in1=k.neghalf.to_broadcast([128, 4]), op=ALU.pow)),
              reads=[Tst[par], k.Tconst], writes=[Tst[par]])
        S.add("dve", (lambda e, sm_=sm_, mvv=mvv: e.scalar_tensor_tensor(out=sm_[:, 8:12], in0=mvv[:, :, 0], scalar=-1.0, in1=sm_[:, 4:8], op0=ALU.mult, op1=ALU.mult)),
              reads=[Tst[par]], writes=[Tst[par]])

        def nrm(e, sm_=sm_, vg_=vg_, vn_=vn_):
            for g in range(4):
                ins = e.tensor_scalar(out=vn_[:, g * 128:(g + 1) * 128], in0=vg_[:, g * 128:(g + 1) * 128], scalar1=sm_[:, 4 + g:5 + g],
                                      scalar2=sm_[:, 8 + g:9 + g], op0=ALU.mult, op1=ALU.add)
            return ins
        S.add("dve", nrm, reads=[Tvg[par], Tst[par]], writes=[Tvn32[par]])
        S.add("pool", (lambda e, vn_=vn_: e.tensor_tensor(out=vn_, in0=vn_, in1=lng, op=ALU.mult)), reads=[Tvn32[par], Tmisc], writes=[Tvn32[par]])
        S.add("pool", (lambda e, vn_=vn_, vb_=vb_: e.tensor_tensor(out=vb_, in0=vn_, in1=lnb, op=ALU.add)), reads=[Tvn32[par], Tmisc], writes=[Tvnb[par]])

        def tail(i=i, par=par, vb_=vb_, ug_=ug_):
            def mixmm(e):
                for g in range(4):
                    ins = e.matmul(pb[5][:, g * 128:(g + 1) * 128], lhsT=SGW[:, g * 128:(g + 1) * 128], rhs=vb_[:, g * 128:(g + 1) * 128],
                                   start=True, stop=True)
                return ins
            S.add("pe", mixmm, reads=[TSGW, Tvnb[par]], writes=[pbT[5]])

            def aout(e):
                for g in range(4):
                    ins = e.scalar_tensor_tensor(out=CATa[:, i, g * 128:(g + 1) * 128], in0=pb[5][:, g * 128:(g + 1) * 128],
                                                 scalar=sgb[:, g:g + 1], in1=ug_[:, g * 128:(g + 1) * 128], op0=ALU.add, op1=ALU.mult)
                return ins
            return lambda: S.add("dve", aout, reads=[pbT[5], Tug[par], k.Tconst], writes=[TCATa[i]])
        tails.append(tail)

    while tails:
        tails.pop(0)()()
    S.barrier()
    A.top = sub_top
    Wout = r3(A.new_bf(8 * 1024), 1024)
    E32 = [A.new_f32(1024) for _ in range(2)]
    SP = [A.new_bf(1024) for _ in range(2)]
    AW = [A.new_bf(1024) for _ in range(2)]
    R = [A.new_bf(1024) for _ in range(2)]
    QA = [r3(A.new_bf(512), 128) for _ in range(2)]
    QB = [r3(A.new_bf(512), 128) for _ in range(2)]
    CATb = [A.new_bf(512) for _ in range(2)]
    CATT = [r3(A.new_bf(1024), 128) for _ in range(2)]
    TWout = [T(), T()]
    TE, TSP, TAW, TR, TQ, TCATb, TCATT = ([T(), T()] for _ in range(7))
    ring2 = JunkRing([(A.new_f32(1024), T()), (A.new_f32(1024), T())])
    load_w_slabs(k, Wout, k.ab_w_out, 1024, 512, "w1", TWout)
    for b in range(2):
        S.add("pool", (lambda e, b=b: e.memset(QA[b][64:128, :, :], 0.0)), writes=[TQ[b]])
        S.add("pool", (lambda e, b=b: e.memset(QB[b][0:64, :, :], 0.0)), writes=[TQ[b]])
    mstr8 = k.mstrict.unsqueeze(1).to_broadcast([128, 8, 128])
    items = [(i, j) for i in range(NT) for j in range(i, -1, -1)]
    N = len(items)
    pbv5 = pb[5][:].bitcast(BF16)

    def build_q(i):
        tp = i % 2
        ts = slice(i * 128, (i + 1) * 128)
        S.add("pool", lambda e: e.tensor_copy(out=QA[tp][0:64, :, :], in_=QT[0:64, :, ts]), reads=[TQT[i]], writes=[TQ[tp]])
        S.add("pool", lambda e: e.tensor_copy(out=QB[tp][64:128, :, :], in_=QT[64:128, :, ts]), reads=[TQT[i]], writes=[TQ[tp]])

    def zmm(e, i, j, base, first_start):
        tp = i % 2
        ks = slice(j * 128, (j + 1) * 128)
        for h in range(8):
            c = h // 2
            q = QA[tp] if h % 2 == 0 else QB[tp]
            st = True if first_start is None else (h % 4 == 0)
            ins = e.matmul(pb[base + h // 4][:, (h % 4) * 128:(h % 4 + 1) * 128], lhsT=KT[:, c, ks], rhs=q[:, c, :], start=st,
                           stop=(first_start is None), skip_group_check=True)
        return ins

    def st_front(n):
        i, j = items[n]
        ip = n % 2
        S.add("pe", lambda e: zmm(e, i, j, 0, None), reads=[TKT[j], TQ[i % 2]], writes=[pbT[0], pbT[1]])
        S.add("act", lambda e: e.activation(out=E32[ip], in_=k.pbig[:, 0:1024], func=AF.Exp), reads=[pbT[0], pbT[1]], writes=[TE[ip]])
        S.add("act", lambda e: e.activation(out=SP[ip], in_=E32[ip], func=AF.Ln, bias=k.one_b), reads=[TE[ip]], writes=[TSP[ip]])
        if j == i:
            S.add("dve", lambda e: e.tensor_tensor(out=r3(SP[ip], 128), in0=r3(SP[ip], 128), in1=mstr8, op=ALU.mult),
                  reads=[TSP[ip], k.Tconst], writes=[TSP[ip]])

    def st_w(n):
        i, j = items[n]
        ip = n % 2
        tp = i % 2

        def wmm(e):
            zmm(e, i, j, 2, True)
            for b in range(2):
                ins = e.matmul(pb[2 + b][:], lhsT=k.negL, rhs=SP[ip][:, b * 512:(b + 1) * 512], start=False, stop=(j == i), skip_group_check=True)
                if j < i:
                    ins = e.matmul(pb[2 + b][:], lhsT=k.ones, rhs=R[tp][:, b * 512:(b + 1) * 512], start=False, stop=True, skip_group_check=True)
            return ins
        S.add("pe", wmm, reads=[TKT[j], TQ[tp], TSP[ip], k.Tconst] + ([TR[tp]] if j < i else []), writes=[pbT[2], pbT[3]])
        if j > 0:
            if j == i:
                S.add("dve", lambda e: e.tensor_scalar(out=R[tp], in0=SP[ip], scalar1=-1.0, scalar2=None, op0=ALU.mult), reads=[TSP[ip]], writes=[TR[tp]])
            else:
                S.add("dve", lambda e: e.tensor_tensor(out=R[tp], in0=R[tp], in1=SP[ip], op=ALU.subtract), reads=[TSP[ip], TR[tp]], writes=[TR[tp]])

    def st_exp2(n):
        i, j = items[n]
        ip = n % 2
        S.add("act", lambda e: e.activation(out=AW[ip], in_=k.pbig[:, 1024:2048], func=AF.Exp), reads=[pbT[2], pbT[3]], writes=[TAW[ip]])
        if j == i:
            S.add("dve", lambda e: e.tensor_tensor(out=r3(AW[ip], 128), in0=r3(AW[ip], 128), in1=mstr8, op=ALU.mult),
                  reads=[TAW[ip], k.Tconst], writes=[TAW[ip]])

    def st_av(n):
        i, j = items[n]
        ip = n % 2
        tp = i % 2

        def av(e):
            for h in range(8):
                ins = e.matmul(pb[4][:, h * 64:(h + 1) * 64], lhsT=AW[ip][:, h * 128:(h + 1) * 128], rhs=V[:, j, h * 64:(h + 1) * 64],
                               start=(j == i and h == 0), stop=(j == 0 and h == 7), skip_group_check=True)
            return ins
        S.add("pe", av, reads=[TAW[ip], TV[j]], writes=[pbT[4]])
        if j == 0:
            S.add("dve", lambda e: e.tensor_copy(out=CATb[tp], in_=pb[4][:]), reads=[pbT[4]], writes=[TCATb[tp]])

            def ch_tr():
                def trc(e):
                    for c in range(8):
                        src = CATa[:, i, c * 128:(c + 1) * 128] if c < 4 else CATb[tp][:, (c - 4) * 128:(c - 3) * 128]
                        ins = e.transpose(out=pbv5[:, c * 128:(c + 1) * 128], in_=src, identity=k.ident)
                    return ins
                S.add("pe", trc, reads=[TCATa[i], TCATb[tp], k.Tconst], writes=[pbT[5]])
                S.add("dve", lambda e: e.tensor_copy(out=CATT[tp], in_=r3(pbv5, 128)), reads=[pbT[5]], writes=[TCATT[tp]])
            deferred.append(ch_tr)
            for n2 in range(2):
                for hf in range(2):
                    def ch_wo(n2=n2, hf=hf):
                        def womm(e):
                            for kc in range(4 * hf, 4 * hf + 4):
                                ins = e.matmul(pb[6 + n2][:], lhsT=CATT[tp][:, kc, :], rhs=Wout[:, kc, n2 * 512:(n2 + 1) * 512], start=(kc == 0), stop=(kc == 7))
                            return ins
                        S.add("pe", womm, reads=[TCATT[tp]] + TWout, writes=[pbT[6 + n2]])
                        if hf == 1:
                            S.add("dve", lambda e: e.tensor_tensor(out=k.X[:, i, n2 * 512:(n2 + 1) * 512], in0=pb[6 + n2][:],
                                                                   in1=k.X[:, i, n2 * 512:(n2 + 1) * 512], op=ALU.add),
                                  reads=[pbT[6 + n2], k.Xt[i]], writes=[k.Xt[i]])
                            if n2 == 1:
                                jb, jt = ring2.nxt()
                                S.add("dve", lambda e: e.scalar_tensor_tensor(out=jb, in0=k.X[:, i, :], scalar=1.0, in1=k.X[:, i, :], op0=ALU.mult, op1=ALU.mult,
                                                                              accum_out=k.ss[:, i:i + 1]),
                                      reads=[k.Xt[i]], writes=[k.Tss[i], jt])
                    deferred.append(ch_wo)

    deferred = []
    build_q(0)
    build_q(1)
    st_front(0)
    for n in range(N + 1):
        if n < N:
            if n + 1 < N:
                st_front(n + 1)
            st_w(n)
        if deferred:
            deferred.pop(0)()
        if n - 1 >= 0:
            st_av(n - 1)
            ip_, jp_ = items[n - 1]
            if jp_ == 0 and ip_ + 2 < NT:
                build_q(ip_ + 2)
        if n < N:
            st_exp2(n)
    while deferred:
        deferred.pop(0)()
    k.stats_ready = True


def phase_ffn(k, s, l):
    S, A = k.S, k.A
    S.barrier()
    A.top = k.persist_top
    pb, pbT = k.pb, k.pbT
    HT = r3(A.new_bf(8 * 2050), 2050)
    NPC = [6, 6, 5, 5]
    PST = [0, 6, 12, 17]
    GT = [r3(A.new_bf(6 * 2048), 2048) for _ in range(2)]
    WDN = [r3(A.new_bf(6 * 1024), 1024) for _ in range(2)]
    NSLOT = 4
    WUP = [r3(A.new_bf(8 * 256), 256) for _ in range(NSLOT)]
    hn = A.new_bf(1024)
    junkf = A.new_bf(1024)
    TG = [A.new_f32(412) for _ in range(3)]
    GG = [A.new_f32(412) for _ in range(3)]
    TU = [A.new_f32(412) for _ in range(3)]
    Thn = T()
    THT = [T() for _ in range(NT)]
    Thalo = T()
    TGT = [[T() for _ in range(6)] for _ in range(2)]
    TWDN = [T(), T()]
    TWUPg = [T() for _ in range(NSLOT)]
    TWUPu = [T() for _ in range(NSLOT)]
    TTG, TGG, TTU = [T(), T(), T()], [T(), T(), T()], [T(), T(), T()]
    wup = k.ffn_w_up[l].rearrange("(kc p) n -> p kc n", p=128)
    wdn = k.ffn_w_down[l].rearrange("(c p) n -> p c n", p=128)
    bounds = [0, 410, 820, 1230, 1640, 2048]

    def load_up(c):
        sl = c % NSLOT
        S.add("pool", lambda e: e.dma_start(out=WUP[sl][:, :, 0:128], in_=wup[:, :, c * 128:(c + 1) * 128]),
              writes=[TWUPg[sl]], chan="wu%d" % sl)
        S.add("pool", lambda e: e.dma_start(out=WUP[sl][:, :, 128:256], in_=wup[:, :, DFF + c * 128:DFF + (c + 1) * 128]),
              writes=[TWUPu[sl]], chan="wu%d" % sl)

    def load_dn(p):
        pp = p % 2
        S.add("pool", lambda e: e.dma_start(out=WDN[pp][:, 0:NPC[p], :], in_=wdn[:, PST[p]:PST[p] + NPC[p], :]),
              writes=[TWDN[pp]], chan="wd%d" % pp)

    for c in range(NSLOT):
        load_up(c)
    load_dn(0)
    S.add("dve", lambda e: e.memset(HT[:, :, 0:2], 0.0), writes=[Thalo])
    gcol = VFM["ffn_g"] + 8 * l
    Tjunk = T()
    ring = JunkRing([(junkf, Tjunk), (hn, Thn)])
    norm_stats_all(k, ring)
    hn2 = [hn, junkf]
    Thn2 = [Thn, Tjunk]
    ht_next = [0]

    def make_ht(upto):
        while ht_next[0] <= min(upto, NT - 1):
            i = ht_next[0]
            norm_hT2(k, i, gcol, hn2[i % 2], Thn2[i % 2], 6 + (i % 2), HT[:, :, 2 + i * 128:2 + (i + 1) * 128], THT[i])
            ht_next[0] += 1
    make_ht(3)

    cw = lambda j, idx: k.vfm[:, VFM["conv_w"] + (l * 3 + j) * 44 + idx: VFM["conv_w"] + (l * 3 + j) * 44 + idx + 1]
    cb = lambda idx: k.vfm[:, VFM["conv_b"] + l * 44 + idx: VFM["conv_b"] + l * 44 + idx + 1]
    item = 0
    pending = []

    def down_unit(m, pp, p):
        if p == 3:
            b0, b1 = 2 * (m % 4), 2 * (m % 4) + 1
        else:
            b0, b1 = 6, 7
        for (b, n2) in ((b0, 0), (b1, 1)):
            def dnmm(e, b=b, n2=n2):
                for cc in range(NPC[p]):
                    ins = e.matmul(pb[b][:], lhsT=GT[pp][:, cc, m * 128:(m + 1) * 128], rhs=WDN[pp][:, cc, n2 * 512:(n2 + 1) * 512],
                                   start=(cc == 0), stop=(cc == NPC[p] - 1))
                return ins
            S.add("pe", dnmm, reads=[TWDN[pp]] + TGT[pp][:NPC[p]], writes=[pbT[b]])
        for (b, n2) in ((b0, 0), (b1, 1)):
            S.add("dve", (lambda e, b=b, n2=n2: e.tensor_tensor(out=k.X[:, m, n2 * 512:(n2 + 1) * 512], in0=pb[b][:],
                                                                 in1=k.X[:, m, n2 * 512:(n2 + 1) * 512], op=ALU.add)),
                  reads=[pbT[b], k.Xt[m]], writes=[k.Xt[m]])
        if p == 3:
            stat_tile(k, m, ring)

    for p in range(4):
        pp = p % 2
        for cc in range(NPC[p]):
            c = PST[p] + cc
            sl = c % NSLOT
            for tt in range(5):
                t0, t1 = bounds[tt], bounds[tt + 1]
                n = t1 - t0
                par = item % 3
                item += 1
                if pending and item % 2 == 0:
                    pending.pop(0)()
                gb, ub = 2 * par, 2 * par + 1
                tiles = sorted(set([max(t0 - 2, 0) // 128, (t1 - 1) // 128] + list(range(t0 // 128, (t1 - 1) // 128 + 1))))
                make_ht(tiles[-1] + 3)

                def upmm(e, sl=sl, t0=t0, n=n, gb=gb, ub=ub):
                    for (b, o) in ((gb, 0), (ub, 128)):
                        for kc in range(8):
                            ins = e.matmul(pb[b][:, 0:n + 2], lhsT=WUP[sl][:, kc, o:o + 128], rhs=HT[:, kc, t0:t0 + n + 2],
                                           start=(kc == 0), stop=(kc == 7))
                    return ins
                S.add("pe", upmm, reads=[TWUPg[sl], TWUPu[sl], Thalo] + [THT[x] for x in tiles], writes=[pbT[gb], pbT[ub]])
                G, U = pb[gb], pb[ub]
                tg, gg, tu = TG[par][:, 0:n], GG[par][:, 0:n], TU[par][:, 0:n]
                S.add("act", (lambda e, G=G, tg=tg, c=c, n=n: e.activation(out=tg, in_=G[:, 0:n], func=AF.Identity, scale=cw(0, c), bias=cb(c))),
                      reads=[pbT[gb], k.Tconst], writes=[TTG[par]])
                S.add("act", (lambda e, U=U, tu=tu, c=c, n=n: e.activation(out=tu, in_=U[:, 0:n], func=AF.Identity, scale=cw(0, 22 + c), bias=cb(22 + c))),
                      reads=[pbT[ub], k.Tconst], writes=[TTU[par]])
                S.add("dve", (lambda e, G=G, tg=tg, c=c, n=n: e.scalar_tensor_tensor(out=tg, in0=G[:, 1:n + 1], scalar=cw(1, c), in1=tg, op0=ALU.mult, op1=ALU.add)),
                      reads=[pbT[gb], TTG[par], k.Tconst], writes=[TTG[par]])
                S.add("dve", (lambda e, G=G, tg=tg, c=c, n=n: e.scalar_tensor_tensor(out=tg, in0=G[:, 2:n + 2], scalar=cw(2, c), in1=tg, op0=ALU.mult, op1=ALU.add)),
                      reads=[pbT[gb], TTG[par], k.Tconst], writes=[TTG[par]])
                S.add("act", (lambda e, tg=tg, gg=gg: e.activation(out=gg, in_=tg, func=AF.Gelu)), reads=[TTG[par]], writes=[TGG[par]])
                S.add("dve", (lambda e, U=U, tu=tu, c=c, n=n: e.scalar_tensor_tensor(out=tu, in0=U[:, 1:n + 1], scalar=cw(1, 22 + c), in1=tu, op0=ALU.mult, op1=ALU.add)),
                      reads=[pbT[ub], TTU[par], k.Tconst], writes=[TTU[par]])
                S.add("dve", (lambda e, U=U, tu=tu, c=c, n=n: e.scalar_tensor_tensor(out=tu, in0=U[:, 2:n + 2], scalar=cw(2, 22 + c), in1=tu, op0=ALU.mult, op1=ALU.add)),
                      reads=[pbT[ub], TTU[par], k.Tconst], writes=[TTU[par]])
                S.add("pool", (lambda e, gg=gg, tu=tu, pp=pp, cc=cc, t0=t0, t1=t1: e.tensor_tensor(out=GT[pp][:, cc, t0:t1], in0=gg, in1=tu, op=ALU.mult)),
                      reads=[TGG[par], TTU[par]], writes=[TGT[pp][cc]])
            if c + NSLOT < NFC:
                load_up(c + NSLOT)
        while pending:
            pending.pop(0)()
        if p + 1 < 4:
            load_dn(p + 1)
        for m in range(NT):
            pending.append(lambda m=m, pp=pp, p=p: down_unit(m, pp, p))
    while pending:
        pending.pop(0)()
    k.stats_ready = True


def phase_ple(k, s, l, final):
    S, A = k.S, k.A
    S.barrier()
    A.top = k.persist_top
    pb, pbT = k.pb, k.pbT
    Wg = r3(A.new_bf(8 * 1024), 1024)
    Wp = r3(A.new_bf(2 * 1024), 1024)
    postg = A.new_f32(1024)
    finalg = A.new_f32(1024)
    hn = [A.new_bf(1024) for _ in range(2)]
    hT = [r3(A.new_bf(1024), 128) for _ in range(2)]
    pbf = [A.new_bf(256) for _ in range(2)]
    PT = r3(A.new_bf(2 * SEQ), SEQ)
    sig = [A.new_f32(1024) for _ in range(2)]
    t2 = [A.new_f32(1024) for _ in range(2)]
    junk = A.new_bf(1024)
    junk2 = A.new_bf(1024)
    outt = [A.new_f32(1024) for _ in range(2)]
    ssp = A.new_f32(32)
    sp1 = A.new_f32(16)
    sp2 = A.new_f32(16)
    rsp = A.new_f32(16)
    TWg = [T(), T()]
    TWp = [T()]
    Tmisc = T()
    Thn = [T(), T()]
    ThT = [T(), T()]
    Tpbf = [T(), T()]
    TPT = [T() for _ in range(NT)]
    Tsig, Tt2 = [T(), T()], [T(), T()]
    Tjunk, Tssp = T(), T()
    ring = JunkRing([(junk, Tjunk), (junk2, T())])
    Tsspi = [T() for _ in range(32)]
    Tout = [T(), T()]
    load_w_slabs(k, Wp, k.ple_w_proj[l], 1024, 1024, "w1", TWp)
    S.add("sp", lambda e: e.dma_start(out=postg, in_=k.vbc_d[:, VBC["post_g"] + 1024 * l:VBC["post_g"] + 1024 * (l + 1)]), writes=[Tmisc], chan="misc")
    if final:
        S.add("sp", lambda e: e.dma_start(out=finalg, in_=k.vbc_d[:, VBC["final_g"]:VBC["final_g"] + 1024]), writes=[Tmisc], chan="misc")
    gcol = VFM["ple_g"] + 8 * l
    out_ops = []
    fin_q = []
    fms = A.new_f32(16)
    frs = A.new_f32(16)
    Tf = [T() for _ in range(NT)]
    for i in range(NT):
        cur = i % 2
        S.add("pool", (lambda e, i=i, cur=cur: e.dma_start(out=pbf[cur], in_=k.p_d[l, s, i * 128:(i + 1) * 128, :])),
              writes=[Tpbf[cur]], chan="pin%d" % cur)
        if i == 1:
            load_w_slabs(k, Wg, k.ple_w_gate[l], 1024, 512, "w0", TWg)
        tb = 6 + cur
        pbv = pb[tb][:].bitcast(BF16)

        def trp(e, cur=cur, pbv=pbv):
            for c in range(2):
                ins = e.transpose(out=pbv[:, c * 128:(c + 1) * 128], in_=pbf[cur][:, c * 128:(c + 1) * 128], identity=k.ident)
            return ins
        S.add("pe", trp, reads=[Tpbf[cur], k.Tconst], writes=[pbT[tb]])
        S.add("dve", (lambda e, i=i, pbv=pbv: e.tensor_copy(out=PT[:, :, i * 128:(i + 1) * 128], in_=r3(pbv[:, 0:256], 128))), reads=[pbT[tb]], writes=[TPT[i]])
        b0, b1 = 4 * cur, 4 * cur + 1

        def pmm(e, i=i, b0=b0, b1=b1):
            for (b, n2) in ((b0, 0), (b1, 1)):
                for kc in range(2):
                    ins = e.matmul(pb[b][:], lhsT=PT[:, kc, i * 128:(i + 1) * 128], rhs=Wp[:, kc, n2 * 512:(n2 + 1) * 512], start=(kc == 0), stop=(kc == 1))
            return ins
        S.add("pe", pmm, reads=[TPT[i]] + TWp, writes=[pbT[b0], pbT[b1]])
        for (b, n2) in ((b0, 0), (b1, 1)):
            jb, jt = ring.nxt()
            S.add("act", (lambda e, b=b, n2=n2, i=i, jb=jb: e.activation(out=jb[:, 0:512], in_=pb[b][:], func=AF.Square, accum_out=ssp[:, 2 * i + n2:2 * i + n2 + 1])),
                  reads=[pbT[b]], writes=[Tsspi[2 * i + n2], jt])
    S.add("pool", lambda e: e.tensor_tensor(out=Wp, in0=Wp, in1=postg.unsqueeze(1).to_broadcast([128, 2, 1024]), op=ALU.mult),
          reads=[Tmisc] + TWp, writes=[TWp[0]])
    norm_stats_all(k, ring)
    sspv = ssp.rearrange("p (i two) -> p i two", two=2)
    S.add("dve", lambda e: e.tensor_tensor(out=sp1, in0=sspv[:, :, 0], in1=sspv[:, :, 1], op=ALU.add), reads=Tsspi, writes=[Tssp])
    S.add("act", lambda e: e.activation(out=sp2, in_=sp1, func=AF.Ln, scale=1.0 / D, bias=k.epsr), reads=[Tssp], writes=[Tssp])
    S.add("act", lambda e: e.activation(out=rsp, in_=sp2, func=AF.Exp, scale=-0.5), reads=[Tssp], writes=[Tssp])
    norm_hT2(k, 0, gcol, hn[0], Thn[0], 6, hT[0], ThT[0])
    norm_hT2(k, 1, gcol, hn[1], Thn[1], 7, hT[1], ThT[1])
    for i in range(NT):
        cur = i % 2
        g0, g1 = 2 * cur, 2 * cur + 1
        p0, p1 = 4, 5

        def gmm(e, cur=cur, g0=g0, g1=g1):
            for (b, n2) in ((g0, 0), (g1, 1)):
                for kc in range(8):
                    ins = e.matmul(pb[b][:], lhsT=hT[cur][:, kc, :], rhs=Wg[:, kc, n2 * 512:(n2 + 1) * 512], start=(kc == 0), stop=(kc == 7))
            return ins
        S.add("pe", gmm, reads=[ThT[cur]] + TWg, writes=[pbT[g0], pbT[g1]])

        def pmm2(e, i=i, p0=p0, p1=p1):
            for (b, n2) in ((p0, 0), (p1, 1)):
                for kc in range(2):
                    ins = e.matmul(pb[b][:], lhsT=PT[:, kc, i * 128:(i + 1) * 128], rhs=Wp[:, kc, n2 * 512:(n2 + 1) * 512], start=(kc == 0), stop=(kc == 1))
            return ins
        S.add("pe", pmm2, reads=[TPT[i]] + TWp, writes=[pbT[p0], pbT[p1]])
        if i + 2 < NT:
            norm_hT2(k, i + 2, gcol, hn[cur], Thn[cur], 6 + cur, hT[cur], ThT[cur])
        for (b, n2) in ((g0, 0), (g1, 1)):
            S.add("act", (lambda e, b=b, n2=n2, cur=cur: e.activation(out=sig[cur][:, n2 * 512:(n2 + 1) * 512], in_=pb[b][:], func=AF.Sigmoid)),
                  reads=[pbT[b]], writes=[Tsig[cur]])
        for (b, n2) in ((p0, 0), (p1, 1)):
            S.add("dve", (lambda e, b=b, n2=n2, cur=cur, i=i: e.scalar_tensor_tensor(out=t2[cur][:, n2 * 512:(n2 + 1) * 512], in0=pb[b][:], scalar=rsp[:, i:i + 1],
                                                                                   in1=sig[cur][:, n2 * 512:(n2 + 1) * 512], op0=ALU.mult, op1=ALU.mult)),
                  reads=[pbT[b], Tssp, Tsig[cur]], writes=[Tt2[cur]])
        S.add("pool", (lambda e, i=i, cur=cur: e.tensor_tensor(out=k.X[:, i, :], in0=t2[cur], in1=k.X[:, i, :], op=ALU.add)), reads=[Tt2[cur], k.Xt[i]], writes=[k.Xt[i]])
        if not final:
            fin_q.append(lambda i=i: stat_tile(k, i, ring))
            if len(fin_q) > 2:
                fin_q.pop(0)()
        if final:
            def fin(i=i, cur=cur):
                jb, jt = ring.nxt()
                S.add("act", (lambda e, i=i, jb=jb: e.activation(out=jb, in_=k.X[:, i, :], func=AF.Square, accum_out=fms[:, i:i + 1])), reads=[k.Xt[i]], writes=[Tf[i], jt])
                S.add("dve", (lambda e, i=i: e.tensor_scalar(out=fms[:, i:i + 1], in0=fms[:, i:i + 1], scalar1=1.0 / D, scalar2=RMS_EPS, op0=ALU.mult, op1=ALU.add)),
                      reads=[Tf[i]], writes=[Tf[i]])
                S.add("pool", (lambda e, i=i: e.tensor_tensor(out=frs[:, i:i + 1], in0=fms[:, i:i + 1], in1=k.neghalf, op=ALU.pow)), reads=[Tf[i], k.Tconst], writes=[Tf[i]])
                S.add("act", (lambda e, i=i, cur=cur: e.activation(out=outt[cur], in_=k.X[:, i, :], func=AF.Copy, scale=frs[:, i:i + 1])),
                      reads=[k.Xt[i], Tf[i]], writes=[Tout[cur]])
                if k.next_seq is not None:
                    k.load_x_tile(k.next_seq, i)
                S.add("pool" if cur == 0 else "dve", (lambda e, cur=cur: e.tensor_tensor(out=outt[cur], in0=outt[cur], in1=finalg, op=ALU.mult)), reads=[Tout[cur], Tmisc], writes=[Tout[cur]])
                out_ops.append(S.add("sp", (lambda e, i=i, cur=cur: e.dma_start(out=k.out_d[s, i * 128:(i + 1) * 128, :], in_=outt[cur])),
                                     reads=[Tout[cur]], chan="xout%d" % cur))

            fin_q.append(fin)
            if len(fin_q) > 2:
                fin_q.pop(0)()
    while fin_q:
        fin_q.pop(0)()
    k.stats_ready = not final
    return out_ops


def phase_mix1(k, s):
    S, A = k.S, k.A
    S.barrier()
    A.top = k.persist_top
    pb, pbT = k.pb, k.pbT
    HT = r3(A.new_bf(8 * SEQ), SEQ)
    WI = [r3(A.new_bf(8 * 1536), 1536) for _ in range(2)]
    WO = [r3(A.new_bf(4 * 1024), 1024) for _ in range(2)]
    S32 = r3(A.new_f32(2 * 512), 512)
    Sb = r3(A.new_bf(2 * 512), 512)
    cs = [A.new_f32(256) for _ in range(2)]
    decT = A.new_f32(512)
    xi8 = r3(A.new_f32(1024), 128)
    zeta = A.new_f32(4)
    hn = A.new_bf(1024)
    junk = A.new_bf(1024)
    junk2 = A.new_bf(1024)
    QK = [A.new_bf(512) for _ in range(2)]
    Vt = [A.new_bf(512) for _ in range(2)]
    SGt = [A.new_bf(512) for _ in range(2)]
    qx = [r3(A.new_bf(256), 128) for _ in range(2)]
    ktm = [A.new_bf(256) for _ in range(2)]
    innT = [A.new_bf(128) for _ in range(2)]
    RT = [[A.new_f32(256) for _ in range(4)] for _ in range(2)]
    Y = [A.new_bf(512) for _ in range(2)]
    YT = [r3(A.new_bf(512), 128) for _ in range(2)]
    stats = [A.new_f32(8) for _ in range(2)]
    mv = [A.new_f32(4) for _ in range(2)]
    THT = [T() for _ in range(NT)]
    TWI = [[T() for _ in range(4)] for _ in range(2)]
    TWO = [T(), T()]
    TS32 = [T(), T()]
    TSb = [T(), T()]
    Tcs = [T(), T()]
    Ttab, Thn, Tjunk = T(), T(), T()
    TQK, TVt, TSGt, Tqx, Tktm, TinnT = ([T(), T()] for _ in range(6))
    TRT = [[T() for _ in range(4)] for _ in range(2)]
    TY, TYT, Tst, Trs = ([T(), T()] for _ in range(4))
    win = k.ret_w_in.rearrange("(kc p) n -> p kc n", p=128)
    wo = k.ret_w_out.rearrange("(kc p) n -> p kc n", p=128)

    def load_head(h):
        sl = h % 2
        parts = [(0, 256, h * 256), (256, 256, 1024 + h * 256), (512, 512, 2048 + h * 512), (1024, 512, 4096 + h * 512)]
        for pi, (o, n, c0) in enumerate(parts):
            S.add("pool", (lambda e, sl=sl, o=o, n=n, c0=c0: e.dma_start(out=WI[sl][:, :, o:o + n], in_=win[:, :, c0:c0 + n])),
                  writes=[TWI[sl][pi]], chan="wi%d" % sl)
        S.add("pool", (lambda e, h=h, sl=sl: e.dma_start(out=WO[sl], in_=wo[:, 4 * h:4 * h + 4, :])), writes=[TWO[sl]], chan="wo%d" % sl)

    def scale_wo(h, kc):
        sl = h % 2
        gcolv = k.vfm[:, VFM["gn_g"] + 4 * h + kc:VFM["gn_g"] + 4 * h + kc + 1]
        S.add("dve", lambda e: e.tensor_scalar(out=WO[sl][:, kc, :], in0=WO[sl][:, kc, :], scalar1=gcolv, scalar2=None, op0=ALU.mult),
              reads=[TWO[sl], k.Tconst], writes=[TWO[sl]])

    load_head(0)
    for kc_ in range(4):
        scale_wo(0, kc_)
    for (dst, nm) in ((decT, "decayT"), (xi8.rearrange("p a b -> p (a b)"), "xi"), (zeta, "zeta")):
        S.add("sp", (lambda e, dst=dst, nm=nm: e.dma_start(out=dst, in_=k.consts_d[:, cslice(nm)])), writes=[Ttab], chan="misc")
    gcol = VFM["mix_g"] + 8
    co, _ = CONST_OFF["cos"]
    so, _ = CONST_OFF["sin"]
    ring = JunkRing([(junk, Tjunk), (junk2, T())])
    norm_stats_all(k, ring)
    for i in range(2):
        norm_hT2(k, i, gcol, hn, Thn, 3, HT[:, :, i * 128:(i + 1) * 128], THT[i])
    items = [(h, i) for h in range(4) for i in range(NT)]
    stat_q = []
    pbv3 = pb[3][:].bitcast(BF16)
    pbv0 = pb[0][:].bitcast(BF16)

    def qk4_(par):
        return QK[par].rearrange("p (a b c) -> p a b c", a=2, b=2)

    def stage_a(n):
        h, i = items[n]
        par = n % 2
        sl = h % 2
        if i == 4 and h + 1 < 4:
            load_head(h + 1)
        if 8 <= i < 12 and h + 1 < 4:
            scale_wo(h + 1, i - 8)
        W = WI[sl]
        tsl = slice(i * 128, (i + 1) * 128)
        S.add("sp", lambda e: e.dma_start(out=cs[par][:, 0:128], in_=k.consts_d[:, co + i * 128:co + (i + 1) * 128]), writes=[Tcs[par]], chan="cs%d" % par)
        S.add("sp", lambda e: e.dma_start(out=cs[par][:, 128:256], in_=k.consts_d[:, so + i * 128:so + (i + 1) * 128]), writes=[Tcs[par]], chan="cs%d" % par)

        def pqk(e):
            for ci in range(4):
                for kc in range(8):
                    ins = e.matmul(pb[0][:, ci * 128:(ci + 1) * 128], lhsT=W[:, kc, ci * 128:(ci + 1) * 128], rhs=HT[:, kc, tsl], start=(kc == 0), stop=(kc == 7))
            return ins
        S.add("pe", pqk, reads=TWI[sl] + [THT[i]], writes=[pbT[0]])
        v4 = pb[0][:].rearrange("p (a b c) -> p a b c", a=2, b=2)
        x1, x2 = v4[:, :, 0, :], v4[:, :, 1, :]
        cosb = cs[par][:, 0:128].unsqueeze(1).to_broadcast([128, 2, 128])
        sinb = cs[par][:, 128:256].unsqueeze(1).to_broadcast([128, 2, 128])
        rt = [r3(x, 128) for x in RT[par]]
        for (ri, xin, tab) in ((0, x1, cosb), (1, x2, sinb), (2, x2, cosb), (3, x1, sinb)):
            S.add("dve", (lambda e, ri=ri, xin=xin, tab=tab: e.tensor_tensor(out=rt[ri], in0=xin, in1=tab, op=ALU.mult)),
                  reads=[pbT[0], Tcs[par]], writes=[TRT[par][ri]])
        qk4 = qk4_(par)
        S.add("pool", lambda e: e.tensor_tensor(out=qk4[:, :, 0, :], in0=rt[0], in1=rt[1], op=ALU.subtract), reads=[TRT[par][0], TRT[par][1]], writes=[TQK[par]])
        S.add("pool", lambda e: e.tensor_tensor(out=qk4[:, :, 1, :], in0=rt[2], in1=rt[3], op=ALU.add), reads=[TRT[par][2], TRT[par][3]], writes=[TQK[par]])
        if i > 0:
            xib = xi8[:, 2 * h, :].unsqueeze(1).to_broadcast([128, 2, 128])
            S.add("pool", lambda e: e.tensor_tensor(out=qx[par], in0=qk4[:, 0, :, :], in1=xib, op=ALU.mult), reads=[TQK[par], Ttab], writes=[Tqx[par]])
        if h == 0 and i + 2 < NT:
            norm_hT2(k, i + 2, gcol, hn, Thn, 3, HT[:, :, (i + 2) * 128:(i + 3) * 128], THT[i + 2])

    def stage_a2(n):
        h, i = items[n]
        par = n % 2
        sl = h % 2
        W = WI[sl]
        tsl = slice(i * 128, (i + 1) * 128)

        def pv(e):
            for (b, o) in ((1, 512), (2, 1024)):
                for kc in range(8):
                    ins = e.matmul(pb[b][:], lhsT=HT[:, kc, tsl], rhs=W[:, kc, o:o + 512], start=(kc == 0), stop=(kc == 7))
            return ins
        S.add("pe", pv, reads=TWI[sl] + [THT[i]], writes=[pbT[1], pbT[2]])
        S.add("act", lambda e: e.activation(out=Vt[par], in_=pb[1][:], func=AF.Copy), reads=[pbT[1]], writes=[TVt[par]])
        S.add("act", lambda e: e.activation(out=SGt[par], in_=pb[2][:], func=AF.Silu), reads=[pbT[2]], writes=[TSGt[par]])

    def stage_b1(n):
        h, i = items[n]
        par = n % 2
        last = (i == NT - 1)
        qk4 = qk4_(par)

        def inmm(e):
            for dc in range(2):
                ins = e.matmul(pb[4][:, 0:128], lhsT=qk4[:, 1, dc, :], rhs=qk4[:, 0, dc, :], start=(dc == 0), stop=(dc == 1))
            return ins
        S.add("pe", inmm, reads=[TQK[par]], writes=[pbT[4]])
        S.add("dve", lambda e: e.tensor_tensor(out=innT[par], in0=pb[4][:, 0:128], in1=decT[:, h * 128:(h + 1) * 128], op=ALU.mult),
              reads=[pbT[4], Ttab], writes=[TinnT[par]])
        if not last:
            def trk(e):
                for dc in range(2):
                    ins = e.transpose(out=pbv3[:, dc * 128:(dc + 1) * 128], in_=qk4[:, 1, dc, :], identity=k.ident)
                return ins
            S.add("pe", trk, reads=[TQK[par], k.Tconst], writes=[pbT[3]])
            S.add("act", lambda e: e.activation(out=ktm[par], in_=pbv3[:, 0:256], func=AF.Copy, scale=zeta[:, h:h + 1]), reads=[pbT[3], Ttab], writes=[Tktm[par]])

    def stage_b2(n):
        h, i = items[n]
        par = n % 2
        ob = 5 + par
        first = (i == 0)
        last = (i == NT - 1)

        def omm(e):
            ins = e.matmul(pb[ob][:], lhsT=innT[par], rhs=Vt[par], start=True, stop=first)
            if not first:
                for dc in range(2):
                    ins = e.matmul(pb[ob][:], lhsT=qx[par][:, dc, :], rhs=Sb[:, dc, :], start=False, stop=(dc == 1))
            return ins
        S.add("pe", omm, reads=[TinnT[par], TVt[par]] + ([] if first else [Tqx[par], TSb[0], TSb[1]]), writes=[pbT[ob]])
        if not last:
            for dc in range(2):
                kb = 7 if dc == 0 else 0
                S.add("pe", (lambda e, dc=dc, kb=kb: e.matmul(pb[kb][:], lhsT=ktm[par][:, dc * 128:(dc + 1) * 128], rhs=Vt[par], start=True, stop=True)),
                      reads=[Tktm[par], TVt[par]], writes=[pbT[kb]])
                if first:
                    S.add("dve", (lambda e, dc=dc, kb=kb: e.tensor_copy(out=S32[:, dc, :], in_=pb[kb][:])), reads=[pbT[kb]], writes=[TS32[dc]])
                else:
                    S.add("dve", (lambda e, dc=dc, kb=kb: e.scalar_tensor_tensor(out=S32[:, dc, :], in0=S32[:, dc, :], scalar=GAM128[h], in1=pb[kb][:],
                                                                                op0=ALU.mult, op1=ALU.add)),
                          reads=[pbT[kb], TS32[dc]], writes=[TS32[dc]])
                S.add("act", (lambda e, dc=dc: e.activation(out=Sb[:, dc, :], in_=S32[:, dc, :], func=AF.Copy)), reads=[TS32[dc]], writes=[TSb[dc]])

    def stage_c1(n):
        h, i = items[n]
        par = n % 2
        ob = 5 + par
        st_, mv_ = stats[par], mv[par]
        S.add("dve", lambda e: e.bn_stats(out=st_[:, 0:6], in_=pb[ob][:]), reads=[pbT[ob]], writes=[Tst[par]])
        S.add("dve", lambda e: e.bn_aggr(out=mv_[:, 0:2], in_=st_[:, 0:6]), reads=[Tst[par]], writes=[Tst[par]])
        S.add("dve", lambda e: e.scalar_tensor_tensor(out=Y[par], in0=pb[ob][:], scalar=mv_[:, 0:1], in1=SGt[par], op0=ALU.subtract, op1=ALU.mult),
              reads=[pbT[ob], Tst[par], TSGt[par]], writes=[TY[par]])
        S.add("dve", lambda e: e.tensor_scalar(out=mv_[:, 2:3], in0=mv_[:, 1:2], scalar1=LN_EPS, scalar2=None, op0=ALU.add), reads=[Tst[par]], writes=[Trs[par]])
        S.add("pool", lambda e: e.tensor_tensor(out=mv_[:, 3:4], in0=mv_[:, 2:3], in1=k.neghalf, op=ALU.pow), reads=[Trs[par], k.Tconst], writes=[Trs[par]])

    def stage_c2(n):
        h, i = items[n]
        par = n % 2
        sl = h % 2
        mv_ = mv[par]

        def try_(e):
            for c in range(4):
                ins = e.transpose(out=pbv3[:, c * 128:(c + 1) * 128], in_=Y[par][:, c * 128:(c + 1) * 128], identity=k.ident)
            return ins
        S.add("pe", try_, reads=[TY[par], k.Tconst], writes=[pbT[3]])
        S.add("dve", lambda e: e.tensor_copy(out=YT[par], in_=r3(pbv3[:, 0:512], 128)), reads=[pbT[3]], writes=[TYT[par]])

    def stage_c2b(n):
        h, i = items[n]
        par = n % 2
        sl = h % 2
        mv_ = mv[par]

        def womm(e):
            for n2 in range(2):
                for kc in range(4):
                    ins = e.matmul(pb[1 + n2][:], lhsT=YT[par][:, kc, :], rhs=WO[sl][:, kc, n2 * 512:(n2 + 1) * 512], start=(kc == 0), stop=(kc == 3))
            return ins
        S.add("pe", womm, reads=[TYT[par], TWO[sl]], writes=[pbT[1], pbT[2]])
        for n2 in range(2):
            S.add("dve", (lambda e, n2=n2: e.scalar_tensor_tensor(out=k.X[:, i, n2 * 512:(n2 + 1) * 512], in0=pb[1 + n2][:], scalar=mv_[:, 3:4],
                                                                   in1=k.X[:, i, n2 * 512:(n2 + 1) * 512], op0=ALU.mult, op1=ALU.add)),
                  reads=[pbT[1 + n2], k.Xt[i], Trs[par]], writes=[k.Xt[i]])
        if h == 3:
            stat_q.append(lambda i=i: stat_tile(k, i, ring))
        if len(stat_q) > 2 or (stat_q and n == len(items) - 1):
            while len(stat_q) > (0 if n == len(items) - 1 else 2):
                stat_q.pop(0)()

    n_it = len(items)
    for step in range(n_it + 3):
        if 0 <= step - 3 < n_it:
            stage_c2(step - 3)
        if step < n_it:
            stage_a(step)
        if 0 <= step - 3 < n_it:
            stage_c2b(step - 3)
        if 0 <= step - 1 < n_it:
            stage_b2(step - 1)
        if 0 <= step - 2 < n_it:
            stage_c1(step - 2)
        if step < n_it:
            stage_b1(step)
            stage_a2(step)
    k.stats_ready = True


_PROG = {}


def kernel(**inputs):
    inp = {kk: np.asarray(v) for kk, v in inputs.items()}
    shared = _prep_shared(inp)
    if "nc" not in _PROG:
        _PROG["nc"] = build_program()[0]
    nc = _PROG["nc"]
    in_maps = []
    for c in range(N_CORES):
        m = dict(shared)
        m["x"] = np.ascontiguousarray(inp["x"][c * SEQ_PER_CORE:(c + 1) * SEQ_PER_CORE])
        m["p"] = np.ascontiguousarray(inp["p"][:, c * SEQ_PER_CORE:(c + 1) * SEQ_PER_CORE])
        in_maps.append(m)
    res = run_bass_kernel_spmd(nc, in_maps, core_ids=list(range(N_CORES)))
    out = np.concatenate([r["out"] for r in res.results], axis=0)
    return out.astype(np.float32, copy=False)
```
